# Optimizing a Trainium2 kernel written in Bass

```python
import math
import jax
import jax.numpy as jnp
from jax import lax
import numpy as np

D_MODEL = 2048
BATCH = 1
SEQ = 16384
DEPTH = 2

GRID_W = 64
CTX_LEN = 256
HEAD_DIM = 64
ROPE_BASE = 10000.0
ROPE_PAIRS_PER_AXIS = HEAD_DIM // 4
NORM_EPS = 1e-6
LN_EPS = 1e-5
Q_BLOCK = 128
ATTN_SCALE = HEAD_DIM ** -0.5
NEG_INF = -1e30
N_MOD = 6
MIX_WIDTH = D_MODEL

A_WIDTH = MIX_WIDTH // 2
A_IN = 2 * A_WIDTH
CONV_A_WIDTH = 31
B_HEADS = (MIX_WIDTH // 2) // (2 * HEAD_DIM)
B_WIDTH = B_HEADS * 2 * HEAD_DIM

C_WIDTH = MIX_WIDTH // 2
C_IN = 3 * C_WIDTH
CONV_C_WIDTH = 3
D_HEADS = (MIX_WIDTH // 2) // HEAD_DIM
D_KV_HEADS = 4
D_GROUP = D_HEADS // D_KV_HEADS
D_Q_WIDTH = D_HEADS * HEAD_DIM
D_KV_WIDTH = D_KV_HEADS * HEAD_DIM
WINDOW = 128

D_FF = 4 * D_MODEL

kernel_name = 'hybrid_diffusion_convmod_diffattn_shortconv_swa'


def rmsnorm(x, g):
    xf = x.astype(jnp.float32)
    y = xf * lax.rsqrt(jnp.mean(xf * xf, axis=-1, keepdims=True) + NORM_EPS)
    return (y * g.astype(jnp.float32)).astype(x.dtype)


def layernorm(x, g, b):
    xf = x.astype(jnp.float32)
    mu = jnp.mean(xf, axis=-1, keepdims=True)
    var = jnp.mean(jnp.square(xf - mu), axis=-1, keepdims=True)
    y = (xf - mu) * lax.rsqrt(var + LN_EPS) * g.astype(jnp.float32) + b.astype(jnp.float32)
    return y.astype(x.dtype)


def adaln(cvec, w, b):
    m = jax.nn.silu(cvec) @ w + b
    return [t[:, None, :] for t in jnp.split(m, N_MOD, axis=-1)]


def sandwich_in(s, g_pre, shift, scale):
    return rmsnorm(s, g_pre) * (1.0 + scale) + shift


def sandwich_out(s, y, g_post, gate):
    return s + gate * rmsnorm(y, g_post)


def sq_relu_mlp(h, w1, w2):
    return jnp.square(jax.nn.relu(h @ w1)) @ w2


def axial_rope(n):
    rows = n // GRID_W
    row = jnp.repeat(jnp.arange(rows, dtype=jnp.float32), GRID_W)
    col = jnp.tile(jnp.arange(GRID_W, dtype=jnp.float32), rows)
    inv = jnp.power(ROPE_BASE, -jnp.arange(ROPE_PAIRS_PER_AXIS, dtype=jnp.float32) / ROPE_PAIRS_PER_AXIS)
    ang = jnp.concatenate([row[:, None] * inv, col[:, None] * inv], axis=-1)
    return jnp.cos(ang), jnp.sin(ang)


def apply_rope(x, cos, sin):
    half = HEAD_DIM // 2
    shape = (1, x.shape[1]) + (1,) * (x.ndim - 3) + (half,)
    cos = cos.reshape(shape).astype(x.dtype)
    sin = sin.reshape(shape).astype(x.dtype)
    x1 = x[..., :half]
    x2 = x[..., half:]
    return jnp.concatenate([x1 * cos - x2 * sin, x2 * cos + x1 * sin], axis=-1)


def depthwise_conv(u, w):
    k = w.shape[0]
    return lax.conv_general_dilated(
        u, w[:, None, :].astype(u.dtype), window_strides=(1,), padding=[(k // 2, k // 2)],
        dimension_numbers=('NWC', 'WIO', 'NWC'), feature_group_count=u.shape[-1])


def conformer_conv(u2, conv_w, conv_b, ln_g, ln_b):
    a, g = jnp.split(u2, 2, axis=-1)
    u = a * jax.nn.sigmoid(g)
    u = depthwise_conv(u, conv_w) + conv_b
    u = layernorm(u, ln_g, ln_b)
    return jax.nn.silu(u)


def short_gated_conv(part, w):
    b_g, c_g, x_in = jnp.split(part, 3, axis=-1)
    return b_g * depthwise_conv(c_g * x_in, w)


def lambda_init_at(depth):
    return 0.8 - 0.6 * math.exp(-0.3 * depth)


def diff_lambda(lq1, lk1, lq2, lk2, lam_init):
    f = jnp.float32
    return (jnp.exp(jnp.sum(lq1.astype(f) * lk1.astype(f)))
            - jnp.exp(jnp.sum(lq2.astype(f) * lk2.astype(f))) + lam_init)


def diff_attend(q, k, v, lam):
    s = jnp.einsum('bqhmd,bkhmd->bhmqk', q, k).astype(jnp.float32) * ATTN_SCALE
    p = jax.nn.softmax(s, axis=-1)
    w = p[:, :, 0] - lam * p[:, :, 1]
    return jnp.einsum('bhqk,bkhe->bqhe', w.astype(v.dtype), v)


def diff_head_norm(o, g, lam_init):
    bsz, n = o.shape[:2]
    return (rmsnorm(o, g) * (1.0 - lam_init)).reshape(bsz, n, B_WIDTH)


def sink_attend(q, k_c, v_c, sink, kw=None, vw=None, valid=None):
    s_c = jnp.einsum('bqkgd,bckd->bkgqc', q, k_c).astype(jnp.float32) * ATTN_SCALE
    parts = [s_c]
    if kw is not None:
        s_w = jnp.einsum('bqkgd,bjkd->bkgqj', q, kw).astype(jnp.float32) * ATTN_SCALE
        parts.append(jnp.where(valid, s_w, NEG_INF))
    sk = jnp.broadcast_to(sink.astype(jnp.float32).reshape(1, D_KV_HEADS, D_GROUP, 1, 1), s_c.shape[:-1] + (1,))
    p = jax.nn.softmax(jnp.concatenate(parts + [sk], axis=-1), axis=-1)
    n_c = k_c.shape[1]
    o = jnp.einsum('bkgqc,bckd->bqkgd', p[..., :n_c].astype(v_c.dtype), v_c)
    if kw is not None:
        o = o + jnp.einsum('bkgqj,bjkd->bqkgd', p[..., n_c:-1].astype(vw.dtype), vw)
    return o


def window_gqa(q, k, v, k_c, v_c, sink):
    bsz, n = q.shape[:2]
    nb = n // Q_BLOCK
    pad = ((0, 0), (Q_BLOCK, Q_BLOCK), (0, 0), (0, 0))
    kp = jnp.pad(k, pad)
    vp = jnp.pad(v, pad)
    qb = q.reshape(bsz, nb, Q_BLOCK, D_KV_HEADS, D_GROUP, HEAD_DIM).swapaxes(0, 1)
    qi = jnp.arange(Q_BLOCK)
    kj = jnp.arange(3 * Q_BLOCK)
    band = jnp.abs(kj[None, :] - Q_BLOCK - qi[:, None]) <= WINDOW

    def block(args):
        blk, qblk = args
        start = blk * Q_BLOCK
        kw = lax.dynamic_slice_in_dim(kp, start, 3 * Q_BLOCK, axis=1)
        vw = lax.dynamic_slice_in_dim(vp, start, 3 * Q_BLOCK, axis=1)
        pos = start - Q_BLOCK + kj
        valid = band & ((pos >= 0) & (pos < n))[None, :]
        return sink_attend(qblk, k_c, v_c, sink, kw, vw, valid)

    o = lax.map(block, (jnp.arange(nb), qb))
    return o.swapaxes(0, 1).reshape(bsz, n, D_Q_WIDTH)


def mixer_ab(h, hc, p, depth, cos, sin, need_ctx):
    bsz, n, _ = h.shape
    nc = hc.shape[1]
    w_in = p['w_in']
    lam_init = lambda_init_at(depth)
    lam = diff_lambda(p['lambda_q1'], p['lambda_k1'], p['lambda_q2'], p['lambda_k2'], lam_init)
    proj = h @ w_in
    a_lat = conformer_conv(proj[..., :A_IN], p['conv_w'], p['conv_b'], p['ln_g'], p['ln_b'])
    q, k, v = jnp.split(proj[..., A_IN:], 3, axis=-1)
    q = apply_rope(q.reshape(bsz, n, B_HEADS, 2, HEAD_DIM), cos, sin)
    k = apply_rope(k.reshape(bsz, n, B_HEADS, 2, HEAD_DIM), cos, sin)
    v = v.reshape(bsz, n, B_HEADS, 2 * HEAD_DIM)
    k_c, v_c = jnp.split(hc @ w_in[:, A_IN + B_WIDTH:], 2, axis=-1)
    k_c = k_c.reshape(bsz, nc, B_HEADS, 2, HEAD_DIM)
    v_c = v_c.reshape(bsz, nc, B_HEADS, 2 * HEAD_DIM)
    k_all = jnp.concatenate([k_c, k], axis=1)
    v_all = jnp.concatenate([v_c, v], axis=1)
    nb = n // Q_BLOCK
    qb = q.reshape(bsz, nb, Q_BLOCK, B_HEADS, 2, HEAD_DIM).swapaxes(0, 1)
    o = lax.map(lambda qblk: diff_attend(qblk, k_all, v_all, lam), qb)
    o = o.swapaxes(0, 1).reshape(bsz, n, B_HEADS, 2 * HEAD_DIM)
    b_lat = diff_head_norm(o, p['subln_g'], lam_init)
    y = jnp.concatenate([a_lat, b_lat], axis=-1) @ p['w_out']
    yc = None
    if need_ctx:
        pc = hc @ w_in[:, :A_IN + B_WIDTH]
        a_ctx = conformer_conv(pc[..., :A_IN], p['conv_w'], p['conv_b'], p['ln_g'], p['ln_b'])
        q_c = pc[..., A_IN:].reshape(bsz, nc, B_HEADS, 2, HEAD_DIM)
        b_ctx = diff_head_norm(diff_attend(q_c, k_c, v_c, lam), p['subln_g'], lam_init)
        yc = jnp.concatenate([a_ctx, b_ctx], axis=-1) @ p['w_out']
    return y, yc


def mixer_cd(h, hc, p, cos, sin, need_ctx):
    bsz, n, _ = h.shape
    nc = hc.shape[1]
    w_in = p['w_in']
    proj = h @ w_in
    c_lat = short_gated_conv(proj[..., :C_IN], p['sconv_w'])
    q = proj[..., C_IN:C_IN + D_Q_WIDTH].reshape(bsz, n, D_KV_HEADS, D_GROUP, HEAD_DIM)
    k = proj[..., C_IN + D_Q_WIDTH:C_IN + D_Q_WIDTH + D_KV_WIDTH].reshape(bsz, n, D_KV_HEADS, HEAD_DIM)
    v = proj[..., C_IN + D_Q_WIDTH + D_KV_WIDTH:].reshape(bsz, n, D_KV_HEADS, HEAD_DIM)
    q = apply_rope(q, cos, sin)
    k = apply_rope(k, cos, sin)
    k_c, v_c = jnp.split(hc @ w_in[:, C_IN + D_Q_WIDTH:], 2, axis=-1)
    k_c = k_c.reshape(bsz, nc, D_KV_HEADS, HEAD_DIM)
    v_c = v_c.reshape(bsz, nc, D_KV_HEADS, HEAD_DIM)
    d_lat = window_gqa(q, k, v, k_c, v_c, p['sink'])
    y = jnp.concatenate([c_lat, d_lat], axis=-1) @ p['w_out']
    yc = None
    if need_ctx:
        pc = hc @ w_in[:, :C_IN + D_Q_WIDTH]
        c_ctx_out = short_gated_conv(pc[..., :C_IN], p['sconv_w'])
        q_c = pc[..., C_IN:].reshape(bsz, nc, D_KV_HEADS, D_GROUP, HEAD_DIM)
        d_ctx = sink_attend(q_c, k_c, v_c, p['sink']).reshape(bsz, nc, D_Q_WIDTH)
        yc = jnp.concatenate([c_ctx_out, d_ctx], axis=-1) @ p['w_out']
    return y, yc


def setup_inputs(seed: int = 0) -> dict:
    key = jax.random.key(seed)
    ks = iter(jax.random.split(key, 64))
    d = D_MODEL

    def nrm(shape, scale):
        return jax.random.normal(next(ks), shape, jnp.float32) * scale

    def gain(width):
        return 1.0 + nrm((width,), 0.05)

    inp = {}
    inp['x'] = nrm((BATCH, SEQ, d), 1.0)
    inp['c'] = nrm((BATCH, d), 1.0)
    inp['ctx'] = nrm((BATCH, CTX_LEN, d), 1.0)
    inp['c_ctx'] = nrm((d,), 1.0)
    inp['l0_mod_w'] = nrm((d, N_MOD * d), 0.5 * d ** -0.5)
    inp['l0_mod_b'] = nrm((N_MOD * d,), 0.01)
    inp['l0_norm_mix_pre'] = gain(d)
    inp['l0_norm_mix_post'] = gain(d)
    inp['l0_norm_mlp_pre'] = gain(d)
    inp['l0_norm_mlp_post'] = gain(d)
    inp['l0_w_in'] = nrm((d, A_IN + 3 * B_WIDTH), d ** -0.5)
    inp['l0_conv_w'] = nrm((CONV_A_WIDTH, A_WIDTH), CONV_A_WIDTH ** -0.5)
    inp['l0_conv_b'] = nrm((A_WIDTH,), 0.01)
    inp['l0_ln_g'] = gain(A_WIDTH)
    inp['l0_ln_b'] = nrm((A_WIDTH,), 0.01)
    inp['l0_lambda_q1'] = nrm((HEAD_DIM,), 0.1)
    inp['l0_lambda_k1'] = nrm((HEAD_DIM,), 0.1)
    inp['l0_lambda_q2'] = nrm((HEAD_DIM,), 0.1)
    inp['l0_lambda_k2'] = nrm((HEAD_DIM,), 0.1)
    inp['l0_subln_g'] = gain(2 * HEAD_DIM)
    inp['l0_w_out'] = nrm((MIX_WIDTH, d), MIX_WIDTH ** -0.5)
    inp['l0_mlp_w1'] = nrm((d, D_FF), d ** -0.5)
    inp['l0_mlp_w2'] = nrm((D_FF, d), D_FF ** -0.5)
    inp['l1_mod_w'] = nrm((d, N_MOD * d), 0.5 * d ** -0.5)
    inp['l1_mod_b'] = nrm((N_MOD * d,), 0.01)
    inp['l1_norm_mix_pre'] = gain(d)
    inp['l1_norm_mix_post'] = gain(d)
    inp['l1_norm_mlp_pre'] = gain(d)
    inp['l1_norm_mlp_post'] = gain(d)
    inp['l1_w_in'] = nrm((d, C_IN + D_Q_WIDTH + 2 * D_KV_WIDTH), d ** -0.5)
    inp['l1_sconv_w'] = nrm((CONV_C_WIDTH, C_WIDTH), CONV_C_WIDTH ** -0.5)
    inp['l1_sink'] = nrm((D_HEADS,), 0.5)
    inp['l1_w_out'] = nrm((MIX_WIDTH, d), MIX_WIDTH ** -0.5)
    inp['l1_mlp_w1'] = nrm((d, D_FF), d ** -0.5)
    inp['l1_mlp_w2'] = nrm((D_FF, d), D_FF ** -0.5)
    return inp


def reference(x, c, ctx, c_ctx,
              l0_mod_w, l0_mod_b, l0_norm_mix_pre, l0_norm_mix_post, l0_norm_mlp_pre, l0_norm_mlp_post,
              l0_w_in, l0_conv_w, l0_conv_b, l0_ln_g, l0_ln_b,
              l0_lambda_q1, l0_lambda_k1, l0_lambda_q2, l0_lambda_k2, l0_subln_g,
              l0_w_out, l0_mlp_w1, l0_mlp_w2,
              l1_mod_w, l1_mod_b, l1_norm_mix_pre, l1_norm_mix_post, l1_norm_mlp_pre, l1_norm_mlp_post,
              l1_w_in, l1_sconv_w, l1_sink, l1_w_out, l1_mlp_w1, l1_mlp_w2):
    n = x.shape[1]
    cos, sin = axial_rope(n)
    layers = [
        dict(mod_w=l0_mod_w, mod_b=l0_mod_b, norm_mix_pre=l0_norm_mix_pre, norm_mix_post=l0_norm_mix_post,
             norm_mlp_pre=l0_norm_mlp_pre, norm_mlp_post=l0_norm_mlp_post, w_in=l0_w_in,
             conv_w=l0_conv_w, conv_b=l0_conv_b, ln_g=l0_ln_g, ln_b=l0_ln_b,
             lambda_q1=l0_lambda_q1, lambda_k1=l0_lambda_k1, lambda_q2=l0_lambda_q2, lambda_k2=l0_lambda_k2,
             subln_g=l0_subln_g, w_out=l0_w_out, mlp_w1=l0_mlp_w1, mlp_w2=l0_mlp_w2),
        dict(mod_w=l1_mod_w, mod_b=l1_mod_b, norm_mix_pre=l1_norm_mix_pre, norm_mix_post=l1_norm_mix_post,
             norm_mlp_pre=l1_norm_mlp_pre, norm_mlp_post=l1_norm_mlp_post, w_in=l1_w_in,
             sconv_w=l1_sconv_w, sink=l1_sink, w_out=l1_w_out, mlp_w1=l1_mlp_w1, mlp_w2=l1_mlp_w2),
    ]
    for i in range(DEPTH):
        p = layers[i]
        need_ctx = i < DEPTH - 1
        sh_m, sc_m, gt_m, sh_f, sc_f, gt_f = adaln(c, p['mod_w'], p['mod_b'])
        csh_m, csc_m, cgt_m, csh_f, csc_f, cgt_f = adaln(c_ctx[None, :], p['mod_w'], p['mod_b'])
        h = sandwich_in(x, p['norm_mix_pre'], sh_m, sc_m)
        hc = sandwich_in(ctx, p['norm_mix_pre'], csh_m, csc_m)
        if i % 2 == 0:
            y, yc = mixer_ab(h, hc, p, i, cos, sin, need_ctx)
        else:
            y, yc = mixer_cd(h, hc, p, cos, sin, need_ctx)
        x = sandwich_out(x, y, p['norm_mix_post'], gt_m)
        hf = sandwich_in(x, p['norm_mlp_pre'], sh_f, sc_f)
        x = sandwich_out(x, sq_relu_mlp(hf, p['mlp_w1'], p['mlp_w2']), p['norm_mlp_post'], gt_f)
        if need_ctx:
            ctx = sandwich_out(ctx, yc, p['norm_mix_post'], cgt_m)
            hcf = sandwich_in(ctx, p['norm_mlp_pre'], csh_f, csc_f)
            ctx = sandwich_out(ctx, sq_relu_mlp(hcf, p['mlp_w1'], p['mlp_w2']), p['norm_mlp_post'], cgt_f)
    return x
```

```python
import contextlib
import math
import numpy as np
import ml_dtypes
import concourse.bass as bass
import concourse.mybir as mybir
from concourse.bass_utils import run_bass_kernel_spmd

F32 = mybir.dt.float32
BF16 = mybir.dt.bfloat16
AF = mybir.ActivationFunctionType
ALU = mybir.AluOpType
AX = mybir.AxisListType

NCORES = 8
D = 2048
KC = 16
SEQ = 16384
TO = SEQ // NCORES
NT = TO // 128
NCTX = 256
TT = TO + NCTX
HD = 64
SCALE = HD ** -0.5
NORM_EPS = 1e-6
LN_EPS = 1e-5
DFF = 8192
W0 = 5120
W1 = 4608
LAM_INIT0 = 0.8 - 0.6 * math.exp(0.0)
KB_MARGIN = 1.25

ENGS = ["tensor", "vector", "scalar", "gpsimd", "sync"]
NDMASEM = 8


class Buf:
    __slots__ = ("writer", "pw", "readers")

    def __init__(self):
        self.writer = None
        self.pw = []
        self.readers = []


class Prog:
    def __init__(self, nc):
        self.nc = nc
        self.ops = {e: [] for e in ENGS}
        self.dma_cnt = {}
        self.dma_rr = {e: 0 for e in ENGS}
        self.known = {e: {} for e in ENGS}
        self.need_inc = {e: set() for e in ENGS}

    def _add_wait(self, eng, waits, tok, is_raw):
        if tok is None:
            return
        if tok[0] == "E":
            _, e2, idx = tok
            if e2 == eng and (eng == "tensor" or not is_raw):
                return
            key = ("E", e2)
            val = idx
        else:
            _, q, slot, cnt = tok
            key = ("D", q, slot)
            val = cnt
        if self.known[eng].get(key, -1) >= val:
            return
        self.known[eng][key] = val
        waits.append(tok)
        if tok[0] == "E":
            self.need_inc[tok[1]].add(tok[2])

    def _deps(self, eng, reads, writes, pwrites, dma):
        waits = []
        for b in reads:
            self._add_wait(eng, waits, b.writer, True)
            for w in b.pw:
                self._add_wait(eng, waits, w, True)
        for b in writes:
            self._add_wait(eng, waits, b.writer, dma)
            for w in b.pw:
                self._add_wait(eng, waits, w, dma)
            for r in b.readers:
                self._add_wait(eng, waits, r, dma)
        for b in pwrites:
            self._add_wait(eng, waits, b.writer, dma)
            for r in b.readers:
                self._add_wait(eng, waits, r, dma)
        return waits

    def _commit(self, tok, reads, writes, pwrites):
        for b in reads:
            b.readers.append(tok)
            if len(b.readers) > 24:
                b.readers = b.readers[-24:] if False else b.readers
        for b in writes:
            b.writer = tok
            b.pw = []
            b.readers = []
        for b in pwrites:
            b.pw.append(tok)

    def op(self, eng, fn, reads=(), writes=(), pwrites=()):
        waits = self._deps(eng, reads, writes, pwrites, False)
        idx = len(self.ops[eng])
        tok = ("E", eng, idx)
        self.ops[eng].append(dict(fn=fn, waits=waits, dma=None))
        self._commit(tok, reads, writes, pwrites)
        return tok

    def dma(self, queue, fn, reads=(), writes=(), pwrites=()):
        waits = self._deps(queue, reads, writes, pwrites, True)
        slot = self.dma_rr[queue] % NDMASEM
        self.dma_rr[queue] += 1
        prev = self.dma_cnt.get((queue, slot), 0)
        if prev > 0:
            self._add_wait(queue, waits, ("D", queue, slot, prev), True)
        cnt = prev + 1
        self.dma_cnt[(queue, slot)] = cnt
        tok = ("D", queue, slot, cnt)
        self.ops[queue].append(dict(fn=fn, waits=waits, dma=(slot, cnt)))
        self._commit(tok, reads, writes, pwrites)
        return tok

    def barrier(self):
        toks = []
        for e in ENGS:
            for i in range(len(self.ops[e]) - 1, -1, -1):
                o = self.ops[e][i]
                if o["fn"] is not None and o["dma"] is None:
                    toks.append(("E", e, i))
                    break
        for (q, slot), cnt in self.dma_cnt.items():
            toks.append(("D", q, slot, cnt))
        for e in ENGS:
            waits = []
            for t in toks:
                self._add_wait(e, waits, t, True)
            if waits:
                self.ops[e].append(dict(fn=None, waits=waits, dma=None))

    def finish_wait(self, eng, toks):
        waits = []
        for t in toks:
            self._add_wait(eng, waits, t, True)
        self.ops[eng].append(dict(fn=None, waits=waits, dma=None))

    def emit(self, st):
        nc = self.nc
        esem = {e: st.enter_context(nc.semaphore("es_" + e)) for e in ENGS}
        dsem = {}
        for q in ENGS:
            for s in range(NDMASEM):
                if (q, s) in self.dma_cnt:
                    dsem[(q, s)] = st.enter_context(nc.semaphore("ds_%s_%d" % (q, s)))
        incval = {}
        for e in ENGS:
            c = 0
            m = {}
            for i in range(len(self.ops[e])):
                if i in self.need_inc[e]:
                    c += 1
                    m[i] = c
            incval[e] = m
        block = st.enter_context(nc.Block())

        def run(ename):
            def body(eng):
                for i, o in enumerate(self.ops[ename]):
                    for t in o["waits"]:
                        if t[0] == "E":
                            eng.wait_ge(esem[t[1]], incval[t[1]][t[2]])
                        else:
                            eng.wait_ge(dsem[(t[1], t[2])], 16 * t[3])
                    if o["fn"] is None:
                        continue
                    ins = o["fn"](eng)
                    if o["dma"] is not None:
                        ins.then_inc(dsem[(ename, o["dma"][0])], 16)
                    elif i in incval[ename]:
                        ins.then_inc(esem[ename], 1)
            return body

        for e in ENGS:
            if self.ops[e]:
                getattr(block, e)(run(e))


class Rot:
    def __init__(self, items):
        self.items = items
        self.i = 0
        self.reserved = set()

    def next(self):
        for _ in range(2 * len(self.items)):
            it = self.items[self.i % len(self.items)]
            self.i += 1
            if not isinstance(it, int) or it not in self.reserved:
                return it
        raise RuntimeError("no free item")

    def take(self):
        it = self.next()
        self.reserved.add(it)
        return it

    def release(self, it):
        self.reserved.discard(it)


NE = TO + 256
NET = NE // 128
TT = NE + NCTX
UW = 15 + NE + 15
EBLOCKS = [(0, 512), (512, 512), (1024, 512), (1536, 512), (2048, 256)]


def build(phases=None, dbg=()):
    ALLP = ['adaln', 'l0in', 'l0conv', 'l0attn', 'tail0', 'l1in', 'l1conv', 'l1attn', 'tail1']
    phases = ALLP if phases is None else phases
    declared = []
    nc = bass.Bass("TRN2", target_bir_lowering=False)
    P = Prog(nc)

    def din(name, shape, dt=F32):
        declared.append(name)
        return nc.dram_tensor(name, list(shape), dt, kind="ExternalInput").ap()

    x_ext = din("x_ext", [NE + 256, D])
    x_all = din("x_all", [SEQ, D])
    cs_all = din("cs_all", [SEQ, 64])
    ctx_in = din("ctx", [NCTX, D])
    cvT = din("cvT", [128, KC, 2])
    cs_ext = din("cs_ext", [NE, 64])
    ident_in = din("ident", [128, 128])
    flags = din("flags", [128, 2])
    masks_in = din("masks", [128, 2, 128])
    class LazyW(dict):
        def __init__(self, l, wcols):
            self.l = l
            self.shapes = dict(mod_w=("l%d_mod_w", [D, 6 * D]), mod_bT=("l%d_mod_bT", [128, 96]), gains=("l%d_gains", [128, 4, KC]),
                               w_in=("l%d_w_in", [D, wcols]), w_out=("l%d_w_out", [D, D]), w1=("l%d_mlp_w1", [D, DFF]), w2=("l%d_mlp_w2", [DFF, D]))

        def __missing__(self, k):
            nm, shp = self.shapes[k]
            v = din(nm % self.l, shp)
            self[k] = v
            return v
    W = {0: LazyW(0, W0), 1: LazyW(1, W1)}
    conv_wT = din("l0_conv_wT", [128, 8, 31])
    conv_misc = din("l0_conv_misc", [128, 3, 8])
    lam_in = din("l0_lam", [4, 64])
    subln = din("l0_subln", [128, 1])
    sconv_wT = din("l1_sconv_wT", [128, 8, 3])
    sink_in = din("l1_sink", [1, 16])
    out_ext = nc.dram_tensor("out", [TO, D], F32, kind="ExternalOutput").ap()

    def dscr(name, shape, dt):
        return nc.dram_tensor(name, list(shape), dt)

    xT0 = dscr("xT0", [KC, 128, TT], F32)
    xT1 = dscr("xT1", [KC, 128, TT], F32)
    uT_e = dscr("uT_e", [8, 128, UW], F32)
    uT_c = dscr("uT_c", [8, 128, NCTX], F32)
    mixT = [dscr("mixT%d" % l, [KC, 128, TT], BF16) for l in range(2)]
    qT0 = dscr("qT0", [16, 65, TT], BF16)
    kt_all = dscr("kt_all", [8 * 128, SEQ], BF16)
    v_all = dscr("v_all", [8 * 128, SEQ], BF16)
    kt_ctx = dscr("kt_ctx", [8, 128, NCTX], BF16)
    cxT = dscr("cxT", [8, 128, NE], F32)
    bT = dscr("bT", [8, 128, NE], F32)
    qT1 = dscr("qT1", [16, 66, NE], BF16)
    kt1 = dscr("kt1", [256, TT], BF16)
    v1 = dscr("v1", [TT, 256], BF16)
    B = {nm: Buf() for nm in ["xT0", "xT1", "uT_e", "uT_c", "mixT0", "mixT1", "qT0", "kt_all", "v_all",
                              "kt_ctx", "cxT", "bT", "qT1", "kt1", "v1"]}
    out_toks = []

    top = contextlib.ExitStack()
    with top:
        _uniq = [0]

        def sbt(st, name, shape, dt):
            _uniq[0] += 1
            return st.enter_context(nc.sbuf_tensor("%s_%d" % (name, _uniq[0]), list(shape), dt))

        identf = sbt(top, "identf", [128, 128], F32); b_identf = Buf()
        identb = sbt(top, "identb", [128, 128], BF16); b_identb = Buf()
        onesb = sbt(top, "onesb", [128, 128], BF16); b_onesb = Buf()
        onesf = sbt(top, "onesf", [128, 128], F32); b_onesf = Buf()
        tab = sbt(top, "tab", [128, 2, 2, 6, KC], F32); b_tab = Buf()
        flg = sbt(top, "flg", [128, 2], F32); b_flg = Buf()
        cs = sbt(top, "cs", [128, NET, 64], F32); b_cs = Buf()
        negkb = [sbt(top, "negkb%d" % l, [128, 1], F32) for l in range(2)]
        b_negkb = [Buf(), Buf()]
        kmax = sbt(top, "kmax", [128, 8], F32); b_kmax = Buf()
        WP = {}

        def alloc_wp(st, tag, nslots=3):
            WP["t"] = sbt(st, "wp_" + tag, [128, nslots, KC, 512], BF16)
            WP["rot"] = Rot([(i, Buf()) for i in range(nslots)])
        ps = top.enter_context(nc.psum_tensor("ps", [128, 8, 512], F32))
        b_ps = [Buf() for _ in range(8)]
        prot = Rot(list(range(8)))

        def psb(bk):
            return ps[:, bk, :].bitcast(BF16)

        P.dma("sync", lambda e: e.dma_start(out=identf[:], in_=ident_in), writes=[b_identf])
        P.op("vector", lambda e: e.tensor_copy(identb[:], identf[:]), reads=[b_identf], writes=[b_identb])
        P.op("vector", lambda e: e.memset(onesb[:], 1.0), writes=[b_onesb])
        P.op("vector", lambda e: e.memset(onesf[:], 1.0), writes=[b_onesf])
        P.dma("sync", lambda e: e.dma_start(out=flg[:], in_=flags), writes=[b_flg])
        P.dma("sync", lambda e: e.dma_start(out=cs[:], in_=cs_ext.rearrange("(t p) c -> p t c", p=128)), writes=[b_cs])

        def wr(b, first):
            return dict(writes=[b] if first else [], pwrites=[] if first else [b])

        WB = {}

        def wsrc(l, name):
            if (l, name) in WB:
                t, b = WB[(l, name)]
                return t.ap(), [b]
            return W[l][name], []

        def precast(l, name, rows_per):
            src = W[l][name]
            rows, cols = src.shape
            t = dscr("wbf%d_%s" % (l, name), [rows, cols], BF16)
            b = Buf()
            for r0 in range(0, rows, rows_per):
                P.dma("gpsimd", lambda e, r0=r0: e.dma_start(out=t.ap()[r0:r0 + rows_per, :], in_=src[r0:r0 + rows_per, :]), pwrites=[b])
            WB[(l, name)] = (t, b)

        def load_wpiece(wdb, r0, c0, nk=KC, ncol=512):
            wd, rb = wdb if isinstance(wdb, tuple) else (wdb, [])
            slot, bw = WP["rot"].next()
            wp = WP["t"]
            src = wd[r0:r0 + nk * 128, c0:c0 + ncol].rearrange("(kc p) n -> p kc n", p=128)
            P.dma("gpsimd", lambda e: e.dma_start(out=wp[:, slot, 0:nk, 0:ncol], in_=src), reads=rb, writes=[bw])
            return slot, bw

        def mm_acc(bk, ncols_ap, lhsT_fn, rhs_fn, n, reads):
            for k in range(n):
                P.op("tensor", lambda e, k=k: e.matmul(ncols_ap, lhsT_fn(k), rhs_fn(k), start=(k == 0), stop=(k == n - 1)),
                     reads=reads, **wr(b_ps[bk], k == 0))

        def linear_fm(wd, col0, ncolg, rhs_fn, rhs_bufs, nkc, N, evac, kp=1):
            wpt = WP["t"]
            kper = nkc // kp
            for g in range(ncolg):
                banks = [prot.next() for _ in range(4)]
                for kpi in range(kp):
                    slot, bw = load_wpiece(wd, kpi * kper * 128, col0 + g * 512, nk=kper)
                    for oc in range(4):
                        bk = banks[oc]
                        for k in range(kper):
                            kc = kpi * kper + k
                            P.op("tensor", lambda e, bk=bk, slot=slot, k=k, oc=oc, kc=kc: e.matmul(
                                ps[:, bk, 0:N], wpt[:, slot, k, oc * 128:(oc + 1) * 128], rhs_fn(kc),
                                start=(kc == 0), stop=(kc == nkc - 1)),
                                reads=[bw] + rhs_bufs, **wr(b_ps[bk], kc == 0))
                for oc in range(4):
                    evac(g * 4 + oc, ps[:, banks[oc], 0:N], b_ps[banks[oc]])

        def linear_tm(wd, col0, hT, b_hT, ntile, evac):
            wpt = WP["t"]
            slot, bw = load_wpiece(wd, 0, col0)
            for t in range(ntile):
                bk = prot.next()
                for kc in range(KC):
                    P.op("tensor", lambda e, bk=bk, kc=kc, t=t: e.matmul(ps[:, bk, :], hT[:, kc, t * 128:(t + 1) * 128], wpt[:, slot, kc, :],
                                                                         start=(kc == 0), stop=(kc == KC - 1)),
                         reads=[bw, b_hT], **wr(b_ps[bk], kc == 0))
                evac(t, bk)

        def stats_finish(bk, N, rstd, b_rstd, scale, eps):
            P.op("scalar", lambda e: e.activation(out=rstd[:, 0:N], in_=ps[:, bk, 0:N], func=AF.Sqrt, scale=scale, bias=eps),
                 reads=[b_ps[bk]], writes=[b_rstd])
            P.op("vector", lambda e: e.reciprocal(rstd[:, 0:N], rstd[:, 0:N]), reads=[b_rstd], writes=[b_rstd])

        def sandwich_in(xT, b_xT, N, l, r, ia, hT, b_hT, rstd, b_rstd, tmpr, sqr):
            bk = prot.next()
            for c in range(KC):
                si, bs_ = sqr.next()
                P.op("scalar", lambda e, c=c, si=si: e.activation(out=si[:, 0:N], in_=xT[:, c, 0:N], func=AF.Square),
                     reads=[b_xT], writes=[bs_])
                P.op("tensor", lambda e, c=c, si=si: e.matmul(ps[:, bk, 0:N], onesb[:], si[:, 0:N], start=(c == 0), stop=(c == KC - 1)),
                     reads=[bs_, b_onesb], **wr(b_ps[bk], c == 0))
            stats_finish(bk, N, rstd, b_rstd, 1.0 / D, NORM_EPS)
            for fc in range(KC):
                ti, bt = tmpr.next()
                P.op("vector", lambda e, fc=fc, ti=ti: e.tensor_tensor(ti[:, 0:N], xT[:, fc, 0:N], rstd[:, 0:N], ALU.mult),
                     reads=[b_xT, b_rstd], writes=[bt])
                P.op("scalar", lambda e, fc=fc, ti=ti: e.activation(out=hT[:, fc, 0:N], in_=ti[:, 0:N], func=AF.Identity,
                                                                     scale=tab[:, l, r, ia, fc:fc + 1], bias=tab[:, l, r, ia + 1, fc:fc + 1]),
                     reads=[bt, b_tab], **wr(b_hT, fc == 0))

        def adaln_phase():
            with contextlib.ExitStack() as st:
                alloc_wp(st, "m")
                wpt = WP["t"]
                scT = sbt(st, "scT", [128, KC, 2], BF16); b_scT = Buf()
                cvs = sbt(st, "cvs", [128, KC, 2], F32); b_cvs = Buf()
                msb = sbt(st, "msb", [2, 6 * D], F32); b_msb = Buf()
                modT = sbt(st, "modT", [128, 96, 2], F32); b_modT = Buf()
                modb = sbt(st, "modb", [128, 96], F32); b_modb = Buf()
                gn = sbt(st, "gn", [128, 4, KC], F32); b_gn = Buf()
                P.dma("sync", lambda e: e.dma_start(out=cvs[:], in_=cvT), writes=[b_cvs])
                P.op("scalar", lambda e: e.activation(out=scT[:], in_=cvs[:], func=AF.Silu), reads=[b_cvs], writes=[b_scT])

                def do_layer(l):
                    P.dma("sync", lambda e: e.dma_start(out=modb[:], in_=W[l]["mod_bT"]), writes=[b_modb])
                    P.dma("sync", lambda e: e.dma_start(out=gn[:], in_=W[l]["gains"]), writes=[b_gn])

                    def do_cg(cg):
                        slot, bw = load_wpiece(W[l]["mod_w"], 0, cg * 512)
                        bk = prot.next()
                        for kc in range(KC):
                            P.op("tensor", lambda e, kc=kc: e.matmul(ps[0:2, bk, :], scT[:, kc, :], wpt[:, slot, kc, :], start=(kc == 0), stop=(kc == KC - 1)),
                                 reads=[bw, b_scT], **wr(b_ps[bk], kc == 0))
                        P.op("vector", lambda e: e.tensor_copy(msb[:, cg * 512:(cg + 1) * 512], ps[0:2, bk, :]), reads=[b_ps[bk]], **wr(b_msb, cg == 0))
                    for cg in range(24):
                        do_cg(cg)
                    bkT = prot.next()
                    for j in range(96):
                        P.op("tensor", lambda e, j=j: e.transpose(ps[:, bkT, 2 * j:2 * j + 2], msb[0:2, j * 128:(j + 1) * 128], identf[0:2, 0:2]),
                             reads=[b_msb, b_identf], **wr(b_ps[bkT], j == 0))
                    psv = ps[:, bkT, 0:192].rearrange("p (j r) -> p j r", r=2)
                    for r in range(2):
                        P.op("vector", lambda e, r=r: e.tensor_tensor(modT[:, :, r], psv[:, :, r], modb[:], ALU.add),
                             reads=[b_ps[bkT], b_modb], **wr(b_modT, r == 0))
                    for r in range(2):
                        for i in range(6):
                            P.op("vector", _bind_tab(tab, modT, gn, l, r, i), reads=[b_modT, b_gn], pwrites=[b_tab])
                for l in range(2):
                    do_layer(l)

        def rope(bk, t_e, dst, b_dst, tmps, do_rope, cst=None, b_cst=None):
            pv = ps[:, bk, :].rearrange("p (h two d) -> p h two d", two=2, d=32)
            if not do_rope:
                P.op("vector", lambda e: e.tensor_copy(dst[:].rearrange("p h d -> p (h d)"), ps[:, bk, :]), reads=[b_ps[bk]], writes=[b_dst])
                return
            if cst is None:
                cosb = cs[:, t_e, 0:32].unsqueeze(1).broadcast_to([128, 8, 32])
                sinb = cs[:, t_e, 32:64].unsqueeze(1).broadcast_to([128, 8, 32])
                b_csx = b_cs
            else:
                cosb = cst[:, 0:32].unsqueeze(1).broadcast_to([128, 8, 32])
                sinb = cst[:, 32:64].unsqueeze(1).broadcast_to([128, 8, 32])
                b_csx = b_cst
            (t1, bt1), (t2, bt2) = tmps
            t1v = t1[:, 0:256].rearrange("p (h d) -> p h d", d=32)
            t2v = t2[:, 0:256].rearrange("p (h d) -> p h d", d=32)
            x1 = pv[:, :, 0, :]
            x2 = pv[:, :, 1, :]
            P.op("vector", lambda e: e.tensor_tensor(t1v, x1, cosb, ALU.mult), reads=[b_ps[bk], b_csx], writes=[bt1])
            P.op("vector", lambda e: e.tensor_tensor(t2v, x2, sinb, ALU.mult), reads=[b_ps[bk], b_csx], writes=[bt2])
            P.op("vector", lambda e: e.tensor_tensor(dst[:, :, 0:32], t1v, t2v, ALU.subtract), reads=[bt1, bt2], writes=[b_dst])
            P.op("vector", lambda e: e.tensor_tensor(t1v, x2, cosb, ALU.mult), reads=[b_ps[bk], b_csx], writes=[bt1])
            P.op("vector", lambda e: e.tensor_tensor(t2v, x1, sinb, ALU.mult), reads=[b_ps[bk], b_csx], writes=[bt2])
            P.op("vector", lambda e: e.tensor_tensor(dst[:, :, 32:64], t1v, t2v, ALU.add), reads=[bt1, bt2], pwrites=[b_dst])

        def sqnorm(src, b_src, sq, b_sq, red, b_red):
            P.op("vector", lambda e: e.tensor_tensor(sq[:], src[:], src[:], ALU.mult), reads=[b_src], writes=[b_sq])
            P.op("vector", lambda e: e.tensor_reduce(red[:], sq[:], AX.X, ALU.add), reads=[b_sq], writes=[b_red])

        def q_finish(qr, b_qr, sq, b_sq, red, b_red, qa, b_qa, naug, qTs, b_qTs, dst_ap_fn, b_dstbuf):
            sqnorm(qr, b_qr, sq, b_sq, red, b_red)
            P.op("scalar", lambda e: e.activation(out=qa[:, :, 64:65], in_=red[:].unsqueeze(2), func=AF.Sqrt, scale=SCALE * SCALE),
                 reads=[b_red], writes=[b_qa])
            P.op("scalar", lambda e: e.activation(out=qa[:, :, 0:64], in_=qr[:], func=AF.Identity, scale=SCALE), reads=[b_qr], pwrites=[b_qa])
            bk = prot.next()
            for hm in range(8):
                P.op("tensor", lambda e, hm=hm: e.transpose(psb(bk)[0:naug, hm * 128:(hm + 1) * 128], qa[:, hm, :], identb[:]),
                     reads=[b_qa, b_identb], **wr(b_ps[bk], hm == 0))
            P.op("vector", lambda e: e.tensor_copy(qTs[:].rearrange("r h n -> r (h n)"), psb(bk)[0:naug, :]), reads=[b_ps[bk]], writes=[b_qTs])
            P.dma("sync", lambda e: e.dma_start(out=dst_ap_fn(), in_=qTs[:]), reads=[b_qTs], pwrites=[b_dstbuf])

        def kmax_update(red, b_red, first):
            if first[0]:
                P.op("vector", lambda e: e.tensor_copy(kmax[:], red[:]), reads=[b_red], writes=[b_kmax])
                first[0] = False
            else:
                P.op("vector", lambda e: e.tensor_tensor(kmax[:], kmax[:], red[:], ALU.max), reads=[b_red, b_kmax], writes=[b_kmax])

        def kb_finish(l):
            with contextlib.ExitStack() as st:
                m1 = sbt(st, "kbm1", [128, 1], F32); b1 = Buf()
                row = sbt(st, "kbrow", [1, 128], F32); b2 = Buf()
                one = sbt(st, "kbone", [1, 1], F32); b3 = Buf()
                P.op("vector", lambda e: e.tensor_reduce(m1[:], kmax[:], AX.X, ALU.max), reads=[b_kmax], writes=[b1])
                bk = prot.next()
                P.op("tensor", lambda e: e.transpose(ps[0:1, bk, 0:128], m1[:], identf[:]), reads=[b1, b_identf], writes=[b_ps[bk]])
                P.op("vector", lambda e: e.tensor_copy(row[:], ps[0:1, bk, 0:128]), reads=[b_ps[bk]], writes=[b2])
                P.op("vector", lambda e: e.tensor_reduce(one[:], row[:], AX.X, ALU.max), reads=[b2], writes=[b3])
                P.op("scalar", lambda e: e.activation(out=one[:], in_=one[:], func=AF.Sqrt), reads=[b3], writes=[b3])
                P.op("vector", lambda e: e.tensor_scalar(one[:], one[:], -KB_MARGIN, None, ALU.mult), reads=[b3], writes=[b3])
                bk2 = prot.next()
                P.op("tensor", lambda e: e.matmul(ps[:, bk2, 0:1], onesf[0:1, :], one[:], start=True, stop=True),
                     reads=[b3, b_onesf], writes=[b_ps[bk2]])
                P.op("vector", lambda e: e.tensor_copy(negkb[l][:], ps[:, bk2, 0:1]), reads=[b_ps[bk2]], writes=[b_negkb[l]])

        def l0_inproj():
            with contextlib.ExitStack() as st:
                alloc_wp(st, "a", 2)
                wpt = WP["t"]
                xrow = [sbt(st, "xrow%d" % i, [128, D], F32) for i in range(2)]
                b_xrow = [Buf(), Buf()]
                xTbs = Rot([(sbt(st, "a_xTb%d" % i, [128, KC, 512], F32), Buf()) for i in range(2)])
                hTs = Rot([(sbt(st, "a_hT%d" % i, [128, KC, 512], BF16), Buf()) for i in range(2)])
                rstd = sbt(st, "a_rstd", [128, 512], F32); b_rstd = Buf()
                tmpr = Rot([(sbt(st, "a_tmp%d" % i, [128, 512], F32), Buf()) for i in range(3)])
                sqr = Rot([(sbt(st, "a_sq%d" % i, [128, 512], BF16), Buf()) for i in range(3)])
                sig = Rot([(sbt(st, "a_sig%d" % i, [128, 512], F32), Buf()) for i in range(2)])
                ust = Rot([(sbt(st, "a_ust%d" % i, [128, 512], F32), Buf()) for i in range(2)])
                qr = sbt(st, "a_qr", [128, 8, 64], F32); b_qr = Buf()
                sq = sbt(st, "a_sqq", [128, 8, 64], F32); b_sq = Buf()
                red = sbt(st, "a_red", [128, 8], F32); b_red = Buf()
                qa = sbt(st, "a_qa", [128, 8, 65], BF16); b_qa = Buf()
                qTs = sbt(st, "a_qTs", [65, 8, 128], BF16); b_qTs = Buf()
                kb_ = sbt(st, "a_kb", [128, 512], BF16); b_kb = Buf()
                kTs = sbt(st, "a_kTs", [128, 4, 128], BF16); b_kTs = Buf()
                vb = Rot([(sbt(st, "a_vb%d" % i, [128, 512], BF16), Buf()) for i in range(2)])
                rt = [(sbt(st, "a_rt%d" % i, [128, 256], F32), Buf()) for i in range(2)]
                kfirst = [True]
                blocks = [("e", 128 + c0, N, c0) for (c0, N) in EBLOCKS] + [("c", 0, NCTX, NE), ("h", 0, 256, None)]
                blocks += [("k", 512 * b, 512, 512 * b) for b in range(SEQ // 512)]
                csk = Rot([(sbt(st, "a_csk%d" % i, [128, 64], F32), Buf()) for i in range(2)])
                def do_block(kind, row0, N, col0):
                    ntile = N // 128
                    r = 1 if kind == "c" else 0
                    xTb, b_xTb = xTbs.next()
                    hT, b_hT = hTs.next()
                    for t in range(ntile):
                        xi = t % 2
                        if kind == "c":
                            src = ctx_in[t * 128:(t + 1) * 128, :]
                        elif kind == "h":
                            src = x_ext[0:128, :] if t == 0 else x_ext[NE + 128:NE + 256, :]
                        elif kind == "k":
                            src = x_all[row0 + t * 128:row0 + (t + 1) * 128, :]
                        else:
                            src = x_ext[row0 + t * 128:row0 + (t + 1) * 128, :]
                        P.dma("sync", lambda e, xi=xi, src=src: e.dma_start(out=xrow[xi][:], in_=src), writes=[b_xrow[xi]])
                        for g4 in range(4):
                            bk = prot.next()
                            for j in range(4):
                                fc = g4 * 4 + j
                                P.op("tensor", lambda e, fc=fc, xi=xi, j=j, bk=bk: e.transpose(ps[:, bk, j * 128:(j + 1) * 128], xrow[xi][:, fc * 128:(fc + 1) * 128], identf[:]),
                                     reads=[b_xrow[xi], b_identf], **wr(b_ps[bk], j == 0))
                            P.op("vector", lambda e, g4=g4, t=t, bk=bk: e.tensor_copy(xTb[:, g4 * 4:(g4 + 1) * 4, t * 128:(t + 1) * 128],
                                                                                       ps[:, bk, :].rearrange("p (j n) -> p j n", n=128)),
                                 reads=[b_ps[bk]], **wr(b_xTb, t == 0 and g4 == 0))
                    if kind in ("e", "c"):
                        P.dma("sync", lambda e, N=N, col0=col0: e.dma_start(out=xT0.ap()[:, :, col0:col0 + N].rearrange("f p n -> p f n"), in_=xTb[:, :, 0:N]),
                              reads=[b_xTb], pwrites=[B["xT0"]])
                    yield "T"
                    sandwich_in(xTb, b_xTb, N, 0, r, 0, hT, b_hT, rstd, b_rstd, tmpr, sqr)
                    yield "S"
                    def do_half(half):
                        sa, bwa = load_wpiece(W[0]["w_in"], 0, half * 512)
                        sg, bwg = load_wpiece(W[0]["w_in"], 0, 1024 + half * 512)
                        for oc in range(4):
                            i = half * 4 + oc
                            bkg = prot.next()
                            bka = prot.next()
                            for kc in range(KC):
                                P.op("tensor", lambda e, kc=kc, oc=oc, bkg=bkg: e.matmul(ps[:, bkg, 0:N], wpt[:, sg, kc, oc * 128:(oc + 1) * 128], hT[:, kc, 0:N],
                                                                                           start=(kc == 0), stop=(kc == KC - 1)),
                                     reads=[bwg, b_hT], **wr(b_ps[bkg], kc == 0))
                            for kc in range(KC):
                                P.op("tensor", lambda e, kc=kc, oc=oc, bka=bka: e.matmul(ps[:, bka, 0:N], wpt[:, sa, kc, oc * 128:(oc + 1) * 128], hT[:, kc, 0:N],
                                                                                           start=(kc == 0), stop=(kc == KC - 1)),
                                     reads=[bwa, b_hT], **wr(b_ps[bka], kc == 0))
                            si, bs_ = sig.next()
                            P.op("scalar", lambda e, bkg=bkg, si=si: e.activation(out=si[:, 0:N], in_=ps[:, bkg, 0:N], func=AF.Sigmoid),
                                 reads=[b_ps[bkg]], writes=[bs_])
                            ui, bu = ust.next()
                            P.op("vector", lambda e, bka=bka, si=si, ui=ui: e.tensor_tensor(ui[:, 0:N], ps[:, bka, 0:N], si[:, 0:N], ALU.mult),
                                 reads=[b_ps[bka], bs_], writes=[bu])
                            if kind == "e":
                                P.dma("sync", lambda e, i=i, ui=ui: e.dma_start(out=uT_e.ap()[i, :, 15 + col0:15 + col0 + N], in_=ui[:, 0:N]),
                                      reads=[bu], pwrites=[B["uT_e"]])
                            elif kind == "c":
                                P.dma("sync", lambda e, i=i, ui=ui: e.dma_start(out=uT_c.ap()[i, :, :], in_=ui[:, 0:N]), reads=[bu], pwrites=[B["uT_c"]])
                            else:
                                P.dma("sync", lambda e, i=i, ui=ui: e.dma_start(out=uT_e.ap()[i, :, 0:15], in_=ui[:, 113:128]), reads=[bu], pwrites=[B["uT_e"]])
                                P.dma("sync", lambda e, i=i, ui=ui: e.dma_start(out=uT_e.ap()[i, :, UW - 15:UW], in_=ui[:, 128:143]), reads=[bu], pwrites=[B["uT_e"]])
                    if kind != "k":
                        for half in range(2):
                            do_half(half)
                    if kind == "h":
                        return
                    for p in range(2 if kind != "k" else 0):
                        def evac_q(t, bk, p=p):
                            t_e = (col0 // 128 + t) if kind == "e" else 0
                            rope(bk, t_e, qr, b_qr, rt, kind == "e")
                            cc = col0 + t * 128
                            q_finish(qr, b_qr, sq, b_sq, red, b_red, qa, b_qa, 65, qTs, b_qTs,
                                     lambda: qT0.ap()[p * 8:(p + 1) * 8, :, cc:cc + 128].rearrange("h r n -> r h n"), B["qT0"])
                        linear_tm(W[0]["w_in"], 2048 + p * 512, hT, b_hT, ntile, evac_q)
                    for p in range(2 if kind != "e" else 0):
                        def evac_k(t, bk, p=p):
                            if kind == "k":
                                ci_, bci = csk.next()
                                g0 = row0 + t * 128
                                P.dma("sync", lambda e: e.dma_start(out=ci_[:], in_=cs_all[g0:g0 + 128, :]), writes=[bci])
                                rope(bk, 0, qr, b_qr, rt, True, cst=ci_, b_cst=bci)
                            else:
                                rope(bk, 0, qr, b_qr, rt, False)
                            sqnorm(qr, b_qr, sq, b_sq, red, b_red)
                            kmax_update(red, b_red, kfirst)
                            P.op("scalar", lambda e: e.activation(out=kb_[:], in_=qr[:].rearrange("p h d -> p (h d)"), func=AF.Identity), reads=[b_qr], writes=[b_kb])
                            bk2 = prot.next()
                            for hh in range(4):
                                P.op("tensor", lambda e, hh=hh: e.transpose(psb(bk2)[:, hh * 128:(hh + 1) * 128], kb_[:, hh * 128:(hh + 1) * 128], identb[:]),
                                     reads=[b_kb, b_identb], **wr(b_ps[bk2], hh == 0))
                            P.op("vector", lambda e: e.tensor_copy(kTs[:].rearrange("q h n -> q (h n)"), psb(bk2)[:, 0:512]), reads=[b_ps[bk2]], writes=[b_kTs])
                            if kind == "c":
                                P.dma("sync", lambda e: e.dma_start(out=kt_ctx.ap()[p * 4:(p + 1) * 4, :, t * 128:(t + 1) * 128].rearrange("h q n -> q h n"), in_=kTs[:]),
                                      reads=[b_kTs], pwrites=[B["kt_ctx"]])
                            else:
                                oc0 = row0 + t * 128
                                P.dma("sync", lambda e: e.dma_start(
                                    out=kt_all.ap().rearrange("(h q) n -> q h n", q=128)[:, p * 4:(p + 1) * 4, oc0:oc0 + 128], in_=kTs[:]),
                                    reads=[b_kTs], pwrites=[B["kt_all"]])
                        linear_tm(W[0]["w_in"], 3072 + p * 512, hT, b_hT, ntile, evac_k)
                    yield "K"
                    for p in range(2 if kind != "e" else 0):
                        def evac_v(t, bk, p=p):
                            vi, bv = vb.next()
                            P.op("scalar", lambda e: e.activation(out=vi[:], in_=ps[:, bk, :], func=AF.Identity), reads=[b_ps[bk]], writes=[bv])
                            if kind == "c":
                                P.dma("sync", lambda e: e.dma_start(out=vctx[:, t, p * 512:(p + 1) * 512], in_=vi[:]), reads=[bv], pwrites=[b_vctx])
                            else:
                                c = row0 // 128 + t
                                P.dma("sync", lambda e: e.dma_start(
                                    out=v_all.ap().rearrange("(h q) (c e) -> q h c e", q=128, e=128)[:, p * 4:(p + 1) * 4, c, :],
                                    in_=vi[:].rearrange("q (h e) -> q h e", e=128)),
                                    reads=[bv], pwrites=[B["v_all"]])
                        linear_tm(W[0]["w_in"], 4096 + p * 512, hT, b_hT, ntile, evac_v)
                    yield "V"
                for blk in blocks:
                    if blk[0] != "k":
                        for _ in do_block(*blk):
                            pass
                gens = [do_block(*blk) for blk in blocks if blk[0] == "k"]
                next(gens[0])
                next(gens[0])
                for i in range(len(gens)):
                    gn = gens[i + 1] if i + 1 < len(gens) else None
                    if gn is not None:
                        next(gn)
                    next(gens[i])
                    if gn is not None:
                        next(gn)
                    next(gens[i])
                kb_finish(0)

        vctx = sbt(top, "vctx", [128, 2, 1024], BF16); b_vctx = Buf()

        def l0_conv():
            with contextlib.ExitStack() as st:
                cw = sbt(st, "c_cw", [128, 8, 31], F32); b_cw = Buf()
                cm = sbt(st, "c_cm", [128, 3, 8], F32); b_cm = Buf()
                vT = sbt(st, "c_vT", [128, 8, TT], F32); b_vT = [Buf() for _ in range(8)]
                uS = [sbt(st, "c_uS%d" % i, [128, UW], F32) for i in range(2)]; b_uS = [Buf(), Buf()]
                uC = [sbt(st, "c_uC%d" % i, [128, NCTX + 30], F32) for i in range(2)]; b_uC = [Buf(), Buf()]
                mean = sbt(st, "c_mean", [128, 512], F32); b_mean = Buf()
                msq = sbt(st, "c_msq", [128, 512], F32); b_msq = Buf()
                rstd = sbt(st, "c_rstd", [128, 512], F32); b_rstd = Buf()
                tmpr = Rot([(sbt(st, "c_tmp%d" % i, [128, 512], F32), Buf()) for i in range(3)])
                vbr = Rot([(sbt(st, "c_vb%d" % i, [128, 512], BF16), Buf()) for i in range(3)])
                ast = Rot([(sbt(st, "c_ast%d" % i, [128, 512], BF16), Buf()) for i in range(2)])
                P.dma("sync", lambda e: e.dma_start(out=cw[:], in_=conv_wT), writes=[b_cw])
                P.dma("sync", lambda e: e.dma_start(out=cm[:], in_=conv_misc), writes=[b_cm])
                for i in range(2):
                    P.op("vector", lambda e, i=i: e.memset(uC[i][:], 0.0), writes=[b_uC[i]])
                for i in range(8):
                    eng = "vector"
                    s = i % 2
                    P.dma("sync", lambda e, i=i, s=s: e.dma_start(out=uS[s][:], in_=uT_e.ap()[i]), reads=[B["uT_e"]], writes=[b_uS[s]])
                    P.dma("sync", lambda e, i=i, s=s: e.dma_start(out=uC[s][:, 15:15 + NCTX], in_=uT_c.ap()[i]), reads=[B["uT_c"]], pwrites=[b_uC[s]])
                    P.op(eng, lambda e, s=s: e.tensor_scalar(uS[s][:, 0:15 + 128], uS[s][:, 0:15 + 128], flg[:, 0:1], None, ALU.mult),
                         reads=[b_uS[s], b_flg], writes=[b_uS[s]])
                    P.op(eng, lambda e, s=s: e.tensor_scalar(uS[s][:, UW - 15 - 128:UW], uS[s][:, UW - 15 - 128:UW], flg[:, 1:2], None, ALU.mult),
                         reads=[b_uS[s], b_flg], writes=[b_uS[s]])
                    for (src, bsrc, c0, n) in ((uS[s], b_uS[s], 0, NE), (uC[s], b_uC[s], NE, NCTX)):
                        dst = vT[:, i, c0:c0 + n]
                        P.op(eng, lambda e, src=src, dst=dst, i=i, n=n: e.tensor_scalar(dst, src[:, 0:n], cw[:, i, 0:1], cm[:, 0, i:i + 1], ALU.mult, ALU.add),
                             reads=[bsrc, b_cw, b_cm], writes=[b_vT[i]] if c0 == 0 else [], pwrites=[b_vT[i]] if c0 else [])
                        for j in range(1, 31):
                            P.op(eng, lambda e, src=src, dst=dst, i=i, n=n, j=j: e.scalar_tensor_tensor(dst, src[:, j:j + n], cw[:, i, j:j + 1], dst, ALU.mult, ALU.add),
                                 reads=[bsrc, b_cw], pwrites=[b_vT[i]])
                def do_ln(c0, N):
                    b1 = prot.next()
                    b2 = prot.next()
                    for i in range(8):
                        v1i, bv1 = vbr.next()
                        P.op("scalar", lambda e, i=i, v1i=v1i: e.activation(out=v1i[:, 0:N], in_=vT[:, i, c0:c0 + N], func=AF.Identity), reads=[b_vT[i]], writes=[bv1])
                        P.op("tensor", lambda e, i=i, v1i=v1i: e.matmul(ps[:, b1, 0:N], onesb[:], v1i[:, 0:N], start=(i == 0), stop=(i == 7)),
                             reads=[bv1, b_onesb], **wr(b_ps[b1], i == 0))
                        v2i, bv2 = vbr.next()
                        P.op("scalar", lambda e, i=i, v2i=v2i: e.activation(out=v2i[:, 0:N], in_=vT[:, i, c0:c0 + N], func=AF.Square), reads=[b_vT[i]], writes=[bv2])
                        P.op("tensor", lambda e, i=i, v2i=v2i: e.matmul(ps[:, b2, 0:N], onesb[:], v2i[:, 0:N], start=(i == 0), stop=(i == 7)),
                             reads=[bv2, b_onesb], **wr(b_ps[b2], i == 0))
                    P.op("scalar", lambda e: e.activation(out=mean[:, 0:N], in_=ps[:, b1, 0:N], func=AF.Identity, scale=1.0 / 1024), reads=[b_ps[b1]], writes=[b_mean])
                    P.op("vector", lambda e: e.tensor_tensor(msq[:, 0:N], mean[:, 0:N], mean[:, 0:N], ALU.mult), reads=[b_mean], writes=[b_msq])
                    P.op("vector", lambda e: e.scalar_tensor_tensor(msq[:, 0:N], ps[:, b2, 0:N], 1.0 / 1024, msq[:, 0:N], ALU.mult, ALU.subtract),
                         reads=[b_ps[b2], b_msq], writes=[b_msq])
                    P.op("scalar", lambda e: e.activation(out=rstd[:, 0:N], in_=msq[:, 0:N], func=AF.Sqrt, bias=LN_EPS), reads=[b_msq], writes=[b_rstd])
                    P.op("vector", lambda e: e.reciprocal(rstd[:, 0:N], rstd[:, 0:N]), reads=[b_rstd], writes=[b_rstd])
                    for i in range(8):
                        ti, bt = tmpr.next()
                        P.op("vector", lambda e, i=i, ti=ti: e.tensor_tensor(ti[:, 0:N], vT[:, i, c0:c0 + N], mean[:, 0:N], ALU.subtract), reads=[b_vT[i], b_mean], writes=[bt])
                        P.op("vector", lambda e, ti=ti: e.tensor_tensor(ti[:, 0:N], ti[:, 0:N], rstd[:, 0:N], ALU.mult), reads=[bt, b_rstd], writes=[bt])
                        ai, ba = ast.next()
                        P.op("scalar", lambda e, i=i, ti=ti, ai=ai: e.activation(out=ai[:, 0:N], in_=ti[:, 0:N], func=AF.Silu, scale=cm[:, 1, i:i + 1], bias=cm[:, 2, i:i + 1]),
                             reads=[bt, b_cm], writes=[ba])
                        P.dma("sync", lambda e, i=i, ai=ai: e.dma_start(out=mixT[0].ap()[i, :, c0:c0 + N], in_=ai[:, 0:N]), reads=[ba], pwrites=[B["mixT0"]])
                for (c0, N) in EBLOCKS + [(NE, NCTX)]:
                    do_ln(c0, N)

        def l0_attn():
            precast(0, "w_out", 512)
            precast(0, "w1", 128)
            precast(0, "w2", 512)
            precast(1, "w_in", 256)
            precast(1, "w_out", 512)
            precast(1, "w1", 128)
            precast(1, "w2", 512)
            with contextlib.ExitStack() as st:
                NKC = NCORES * NT
                KT = [sbt(st, "e_KT%d" % i, [65, SEQ + NCTX], BF16) for i in range(2)]; b_KT = [Buf(), Buf()]
                Vh = [sbt(st, "e_V0", [128, NKC, 128], BF16)]; b_Vh = [Buf()]
                qS = [sbt(st, "e_q%d" % i, [65, TT], BF16) for i in range(2)]; b_qS = [Buf(), Buf()]
                pT = Rot([(sbt(st, "e_pT%d" % i, [128, 512], BF16), Buf()) for i in range(4)])
                accr = Rot([[(sbt(st, "e_acc%d_%d" % (i, j), [128, 512], F32), Buf()) for j in range(2)] for i in range(2)])
                o0 = sbt(st, "e_o0", [128, TT], F32); b_o0 = Buf()
                rs = sbt(st, "e_rs", [128, 512], F32); b_rs = Buf()
                od = sbt(st, "e_od", [128, 512], F32); b_od = Buf()
                tt_ = sbt(st, "e_tt", [128, 512], F32); b_tt = Buf()
                sqb = sbt(st, "e_sqb", [128, 512], BF16); b_sqb = Buf()
                rn = sbt(st, "e_rn", [128, 512], F32); b_rn = Buf()
                bl = Rot([(sbt(st, "e_bl%d" % i, [128, 512], BF16), Buf()) for i in range(2)])
                lamv = sbt(st, "e_lamv", [128, 4, 64], F32); b_lamv = Buf()
                lam = sbt(st, "e_lam", [128, 4], F32); b_lam = Buf()
                gsub = sbt(st, "e_gsub", [128, 1], F32); b_gsub = Buf()
                P.dma("sync", lambda e: e.dma_start(out=lamv[:].rearrange("p a d -> p (a d)"), in_=lam_in.rearrange("a d -> (a d)").partition_broadcast(128)), writes=[b_lamv])
                P.op("vector", lambda e: e.tensor_tensor(lamv[:, 0, :], lamv[:, 0, :], lamv[:, 1, :], ALU.mult), reads=[b_lamv], writes=[b_lamv])
                P.op("vector", lambda e: e.tensor_tensor(lamv[:, 2, :], lamv[:, 2, :], lamv[:, 3, :], ALU.mult), reads=[b_lamv], writes=[b_lamv])
                P.op("vector", lambda e: e.tensor_reduce(lam[:, 0:1], lamv[:, 0, :], AX.X, ALU.add), reads=[b_lamv], writes=[b_lam])
                P.op("vector", lambda e: e.tensor_reduce(lam[:, 1:2], lamv[:, 2, :], AX.X, ALU.add), reads=[b_lamv], writes=[b_lam])
                P.op("scalar", lambda e: e.activation(out=lam[:, 0:2], in_=lam[:, 0:2], func=AF.Exp), reads=[b_lam], writes=[b_lam])
                P.op("vector", lambda e: e.tensor_tensor(lam[:, 2:3], lam[:, 1:2], lam[:, 0:1], ALU.subtract), reads=[b_lam], writes=[b_lam])
                P.op("vector", lambda e: e.tensor_scalar(lam[:, 2:3], lam[:, 2:3], -LAM_INIT0, None, ALU.add), reads=[b_lam], writes=[b_lam])
                P.dma("sync", lambda e: e.dma_start(out=gsub[:], in_=subln), writes=[b_gsub])
                P.op("vector", lambda e: e.tensor_scalar(gsub[:], gsub[:], 1.0 - LAM_INIT0, None, ALU.mult), reads=[b_gsub], writes=[b_gsub])
                for i in range(2):
                    P.op("vector", lambda e, i=i: e.memset(KT[i][64:65, :], 1.0), writes=[b_KT[i]])
                    P.op("vector", lambda e, i=i: e.tensor_scalar(KT[i][64:65, :], KT[i][64:65, :], negkb[0][64:65, 0:1], None, ALU.mult),
                         reads=[b_KT[i], b_negkb[0]], writes=[b_KT[i]])
                qblocks = [(c0, N, True) for (c0, N) in EBLOCKS] + [(NE, NCTX, False)]
                for h in range(8):
                    vi = 0
                    P.dma("sync", lambda e, h=h, vi=vi: e.dma_start(
                        out=Vh[vi][:], in_=v_all.ap().rearrange("(h p) (c e) -> h p c e", p=128, e=128)[h]),
                        reads=[B["v_all"]], writes=[b_Vh[vi]])
                    for m in range(2):
                        hm = 2 * h + m
                        ki = hm % 2
                        P.dma("sync", lambda e, h=h, m=m, ki=ki: e.dma_start(
                            out=KT[ki][0:64, NCTX:], in_=kt_all.ap()[h * 128 + m * 64:h * 128 + (m + 1) * 64, :]),
                            reads=[B["kt_all"]], pwrites=[b_KT[ki]])
                        P.dma("sync", lambda e, h=h, m=m, ki=ki: e.dma_start(out=KT[ki][0:64, 0:NCTX], in_=kt_ctx.ap()[h, m * 64:(m + 1) * 64, :]),
                              reads=[B["kt_ctx"]], pwrites=[b_KT[ki]])
                        P.dma("sync", lambda e, hm=hm, ki=ki: e.dma_start(out=qS[ki][:], in_=qT0.ap()[hm]), reads=[B["qT0"]], writes=[b_qS[ki]])
                        def do_qblock(h, m, hm, ki, vi, c0, N, own):
                            bo = prot.take()
                            bs = prot.take()
                            nch = 2 + (NKC if own else 0)
                            LOOK = 3
                            accs = accr.next()

                            def mm1(ci):
                                bsT = prot.next()
                                P.op("tensor", lambda e: e.matmul(ps[:, bsT, 0:N], KT[ki][:, ci * 128:(ci + 1) * 128], qS[ki][:, c0:c0 + N], start=True, stop=True),
                                     reads=[b_KT[ki], b_qS[ki]], writes=[b_ps[bsT]])
                                return bsT

                            def rest(ci, bsT):
                                pi, bp = pT.next()
                                P.op("scalar", lambda e: e.activation(out=pi[:, 0:N], in_=ps[:, bsT, 0:N], func=AF.Exp), reads=[b_ps[bsT]], writes=[bp])
                                return pi, bp

                            def mm23(ci, pi, bp):
                                if ci < 2:
                                    vl = vctx[:, ci, h * 128:(h + 1) * 128]
                                    vrd = b_vctx
                                else:
                                    vl = Vh[vi][:, ci - 2, :]
                                    vrd = b_Vh[vi]
                                P.op("tensor", lambda e: e.matmul(ps[:, bo, 0:N], vl, pi[:, 0:N], start=(ci == 0), stop=(ci == nch - 1)),
                                     reads=[vrd, bp], **wr(b_ps[bo], ci == 0))
                                ac, bac = accs[ci % 2]
                                aeng = "vector" if ci % 2 == 0 else "gpsimd"
                                if ci < 2:
                                    P.op(aeng, lambda e: e.tensor_copy(ac[:, 0:N], pi[:, 0:N]), reads=[bp], writes=[bac])
                                else:
                                    P.op(aeng, lambda e: e.tensor_tensor(ac[:, 0:N], ac[:, 0:N], pi[:, 0:N], ALU.add), reads=[bp, bac], writes=[bac])
                            pend = {}
                            for ci in range(min(LOOK, nch)):
                                pend[ci] = mm1(ci)
                            for ci in range(nch):
                                pi, bp = rest(ci, pend.pop(ci))
                                if ci + LOOK < nch:
                                    pend[ci + LOOK] = mm1(ci + LOOK)
                                mm23(ci, pi, bp)
                            for j in range(2):
                                P.op("tensor", lambda e, j=j: e.matmul(ps[:, bs, 0:N], onesf[:], accs[j][0][:, 0:N], start=(j == 0), stop=(j == 1)),
                                     reads=[accs[j][1], b_onesf], **wr(b_ps[bs], j == 0))
                            P.op("vector", lambda e: e.reciprocal(rs[:, 0:N], ps[:, bs, 0:N]), reads=[b_ps[bs]], writes=[b_rs])
                            prot.release(bo)
                            prot.release(bs)
                            if m == 0:
                                P.op("vector", lambda e: e.tensor_tensor(o0[:, c0:c0 + N], ps[:, bo, 0:N], rs[:, 0:N], ALU.mult),
                                     reads=[b_ps[bo], b_rs], pwrites=[b_o0])
                            else:
                                P.op("vector", lambda e: e.tensor_tensor(tt_[:, 0:N], ps[:, bo, 0:N], rs[:, 0:N], ALU.mult), reads=[b_ps[bo], b_rs], writes=[b_tt])
                                P.op("vector", lambda e: e.scalar_tensor_tensor(od[:, 0:N], tt_[:, 0:N], lam[:, 2:3], o0[:, c0:c0 + N], ALU.mult, ALU.add),
                                     reads=[b_tt, b_lam, b_o0], writes=[b_od])
                                P.op("scalar", lambda e: e.activation(out=sqb[:, 0:N], in_=od[:, 0:N], func=AF.Square), reads=[b_od], writes=[b_sqb])
                                bn = prot.next()
                                P.op("tensor", lambda e: e.matmul(ps[:, bn, 0:N], onesb[:], sqb[:, 0:N], start=True, stop=True), reads=[b_sqb, b_onesb], writes=[b_ps[bn]])
                                stats_finish(bn, N, rn, b_rn, 1.0 / 128, NORM_EPS)
                                P.op("vector", lambda e: e.tensor_tensor(od[:, 0:N], od[:, 0:N], rn[:, 0:N], ALU.mult), reads=[b_od, b_rn], writes=[b_od])
                                bi, bb = bl.next()
                                P.op("scalar", lambda e, bi=bi: e.activation(out=bi[:, 0:N], in_=od[:, 0:N], func=AF.Identity, scale=gsub[:, 0:1]), reads=[b_od, b_gsub], writes=[bb])
                                P.dma("sync", lambda e, bi=bi, h=h: e.dma_start(out=mixT[0].ap()[8 + h, :, c0:c0 + N], in_=bi[:, 0:N]), reads=[bb], pwrites=[B["mixT0"]])
                        for (c0, N, own) in qblocks:
                            do_qblock(h, m, hm, ki, vi, c0, N, own)

        def tail_phase(l, blocks, src, b_src, dst, b_dst, final):
            with contextlib.ExitStack() as st:
                alloc_wp(st, "t%d" % l, 2)
                xTb = sbt(st, "t_xTb", [128, KC, 512], F32); b_xTb = Buf()
                hid = sbt(st, "t_hid", [128, 64, 512], BF16); b_hid = Buf()
                mixb = hid; b_mixb = b_hid
                yst = sbt(st, "t_yst", [128, KC, 512], F32); b_yst = Buf()
                hT = yst[:].bitcast(BF16); b_hT = b_yst
                rstd = sbt(st, "t_rstd", [128, 512], F32); b_rstd = Buf()
                tmpr = Rot([(sbt(st, "t_tmp%d" % i, [128, 512], F32), Buf()) for i in range(3)])
                sqr = Rot([(sbt(st, "t_sq%d" % i, [128, 512], BF16), Buf()) for i in range(3)])
                orow = sbt(st, "t_orow", [128, D], F32) if final else None
                b_orow = Buf()
                for (r, c0, N, frow) in blocks:
                    P.dma("sync", lambda e, c0=c0, N=N: e.dma_start(out=xTb[:, :, 0:N], in_=src.ap()[:, :, c0:c0 + N].rearrange("f p n -> p f n")),
                          reads=[b_src], writes=[b_xTb])
                    P.dma("sync", lambda e, c0=c0, N=N: e.dma_start(out=mixb[:, 0:KC, 0:N], in_=mixT[l].ap()[:, :, c0:c0 + N].rearrange("f p n -> p f n")),
                          reads=[B["mixT%d" % l]], writes=[b_mixb])

                    def lin_post(wd, nkc, rhs_fn, rhs_bufs, kp, ig, N=N, r=r):
                        ssb = prot.take()

                        def evac(fc, pap, pbuf):
                            P.op("scalar", lambda e: e.activation(out=yst[:, fc, 0:N], in_=pap, func=AF.Identity), reads=[pbuf], **wr(b_yst, fc == 0))
                            si, bs_ = sqr.next()
                            P.op("scalar", lambda e: e.activation(out=si[:, 0:N], in_=pap, func=AF.Square), reads=[pbuf], writes=[bs_])
                            P.op("tensor", lambda e: e.matmul(ps[:, ssb, 0:N], onesb[:], si[:, 0:N], start=(fc == 0), stop=(fc == KC - 1)),
                                 reads=[bs_, b_onesb], **wr(b_ps[ssb], fc == 0))
                        linear_fm(wd, 0, 4, rhs_fn, rhs_bufs, nkc, N, evac, kp=kp)
                        stats_finish(ssb, N, rstd, b_rstd, 1.0 / D, NORM_EPS)
                        prot.release(ssb)
                        for fc in range(KC):
                            ti, bt = tmpr.next()
                            P.op("vector", lambda e, fc=fc, ti=ti: e.tensor_tensor(ti[:, 0:N], yst[:, fc, 0:N], rstd[:, 0:N], ALU.mult),
                                 reads=[b_yst, b_rstd], writes=[bt])
                            P.op("vector", lambda e, fc=fc, ti=ti: e.scalar_tensor_tensor(xTb[:, fc, 0:N], ti[:, 0:N], tab[:, l, r, ig, fc:fc + 1],
                                                                                           xTb[:, fc, 0:N], ALU.mult, ALU.add),
                                 reads=[bt, b_tab, b_xTb], writes=[b_xTb])

                    lin_post(wsrc(l, "w_out"), KC, lambda kc, N=N: mixb[:, kc, 0:N], [b_mixb], 1, 2)
                    sandwich_in(xTb, b_xTb, N, l, r, 3, hT, b_hT, rstd, b_rstd, tmpr, sqr)

                    def evac_h(hc, pap, pbuf, N=N):
                        ti, bt = tmpr.next()
                        P.op("scalar", lambda e: e.activation(out=ti[:, 0:N], in_=pap, func=AF.Relu), reads=[pbuf], writes=[bt])
                        P.op("vector", lambda e: e.tensor_tensor(hid[:, hc, 0:N], ti[:, 0:N], ti[:, 0:N], ALU.mult), reads=[bt], **wr(b_hid, hc == 0))
                    linear_fm(wsrc(l, "w1"), 0, 16, lambda kc, N=N: hT[:, kc, 0:N], [b_hT], KC, N, evac_h)
                    lin_post(wsrc(l, "w2"), 64, lambda kc, N=N: hid[:, kc, 0:N], [b_hid], 4, 5)
                    if dst is not None:
                        P.dma("sync", lambda e, c0=c0, N=N: e.dma_start(out=dst.ap()[:, :, c0:c0 + N].rearrange("f p n -> p f n"), in_=xTb[:, :, 0:N]),
                              reads=[b_xTb], pwrites=[b_dst])
                    if final:
                        for t in range(N // 128):
                            for g4 in range(4):
                                bk = prot.next()
                                for j in range(4):
                                    fc = g4 * 4 + j
                                    P.op("tensor", lambda e, fc=fc, t=t, j=j, bk=bk: e.transpose(ps[:, bk, j * 128:(j + 1) * 128], xTb[:, fc, t * 128:(t + 1) * 128], identf[:]),
                                         reads=[b_xTb, b_identf], **wr(b_ps[bk], j == 0))
                                P.op("vector", lambda e, g4=g4, bk=bk: e.tensor_copy(orow[:, g4 * 512:(g4 + 1) * 512], ps[:, bk, :]),
                                     reads=[b_ps[bk]], **wr(b_orow, g4 == 0))
                            r0 = frow + t * 128
                            out_toks.append(P.dma("sync", lambda e, r0=r0: e.dma_start(out=out_ext[r0:r0 + 128, :], in_=orow[:]), reads=[b_orow]))

        def l1_inproj():
            with contextlib.ExitStack() as st:
                alloc_wp(st, "b")
                wpt = WP["t"]
                xTb = sbt(st, "b_xTb", [128, KC, 512], F32); b_xTb = Buf()
                hT = sbt(st, "b_hT", [128, KC, 512], BF16); b_hT = Buf()
                rstd = sbt(st, "b_rstd", [128, 512], F32); b_rstd = Buf()
                tmpr = Rot([(sbt(st, "b_tmp%d" % i, [128, 512], F32), Buf()) for i in range(3)])
                sqr = Rot([(sbt(st, "b_sq%d" % i, [128, 512], BF16), Buf()) for i in range(3)])
                csb = Rot([(sbt(st, "b_cs%d" % i, [128, 512], F32), Buf()) for i in range(2)])
                stg = Rot([(sbt(st, "b_stg%d" % i, [128, 512], F32), Buf()) for i in range(3)])
                qr = sbt(st, "b_qr", [128, 8, 64], F32); b_qr = Buf()
                sq = sbt(st, "b_sqq", [128, 8, 64], F32); b_sq = Buf()
                red = sbt(st, "b_red", [128, 8], F32); b_red = Buf()
                qa = sbt(st, "b_qa", [128, 8, 66], BF16); b_qa = Buf()
                qTs = sbt(st, "b_qTs", [66, 8, 128], BF16); b_qTs = Buf()
                kr = sbt(st, "b_kr", [128, 4, 64], F32); b_kr = Buf()
                ksq = sbt(st, "b_ksq", [128, 4, 64], F32); b_ksq = Buf()
                kred = sbt(st, "b_kred", [128, 8], F32); b_kred = Buf()
                kb_ = sbt(st, "b_kb", [128, 256], BF16); b_kb = Buf()
                kTs = sbt(st, "b_kTs", [128, 2, 128], BF16); b_kTs = Buf()
                vb = Rot([(sbt(st, "b_vb%d" % i, [128, 256], BF16), Buf()) for i in range(2)])
                rt = [(sbt(st, "b_rt%d" % i, [128, 256], F32), Buf()) for i in range(2)]
                P.op("vector", lambda e: e.memset(qa[:, :, 65:66], 1.0), writes=[b_qa])
                P.op("vector", lambda e: e.memset(kred[:], 0.0), writes=[b_kred])
                kfirst = [True]
                blocks = [("e", c0, N) for (c0, N) in EBLOCKS] + [("c", NE, NCTX)]
                def do_block(kind, col0, N):
                    ntile = N // 128
                    r = 1 if kind == "c" else 0
                    P.dma("sync", lambda e, col0=col0, N=N: e.dma_start(out=xTb[:, :, 0:N], in_=xT1.ap()[:, :, col0:col0 + N].rearrange("f p n -> p f n")),
                          reads=[B["xT1"]], writes=[b_xTb])
                    sandwich_in(xTb, b_xTb, N, 1, r, 0, hT, b_hT, rstd, b_rstd, tmpr, sqr)
                    if kind == "e":
                        def do_half(half):
                            sb_, bwb = load_wpiece(wsrc(1, "w_in"), 0, half * 512)
                            sc_, bwc = load_wpiece(wsrc(1, "w_in"), 0, 1024 + half * 512)
                            sx_, bwx = load_wpiece(wsrc(1, "w_in"), 0, 2048 + half * 512)
                            for oc in range(4):
                                i = half * 4 + oc
                                bks = []
                                for (sl, bw) in ((sb_, bwb), (sc_, bwc), (sx_, bwx)):
                                    bk = prot.next()
                                    bks.append(bk)
                                    for kc in range(KC):
                                        P.op("tensor", lambda e, kc=kc, oc=oc, bk=bk, sl=sl: e.matmul(ps[:, bk, 0:N], wpt[:, sl, kc, oc * 128:(oc + 1) * 128], hT[:, kc, 0:N],
                                                                                                   start=(kc == 0), stop=(kc == KC - 1)),
                                             reads=[bw, b_hT], **wr(b_ps[bk], kc == 0))
                                ci, bc = csb.next()
                                P.op("scalar", lambda e, ci=ci, bk=bks[1]: e.activation(out=ci[:, 0:N], in_=ps[:, bk, 0:N], func=AF.Identity), reads=[b_ps[bks[1]]], writes=[bc])
                                s1, bs1 = stg.next()
                                P.op("vector", lambda e, ci=ci, s1=s1, bk=bks[2]: e.tensor_tensor(s1[:, 0:N], ps[:, bk, 0:N], ci[:, 0:N], ALU.mult), reads=[b_ps[bks[2]], bc], writes=[bs1])
                                P.dma("sync", lambda e, i=i, s1=s1: e.dma_start(out=cxT.ap()[i, :, col0:col0 + N], in_=s1[:, 0:N]), reads=[bs1], pwrites=[B["cxT"]])
                                s2, bs2 = stg.next()
                                P.op("scalar", lambda e, s2=s2, bk=bks[0]: e.activation(out=s2[:, 0:N], in_=ps[:, bk, 0:N], func=AF.Identity), reads=[b_ps[bks[0]]], writes=[bs2])
                                P.dma("sync", lambda e, i=i, s2=s2: e.dma_start(out=bT.ap()[i, :, col0:col0 + N], in_=s2[:, 0:N]), reads=[bs2], pwrites=[B["bT"]])
                        for half in range(2):
                            do_half(half)
                        for p in range(2):
                            def evac_q(t, bk, p=p):
                                t_e = col0 // 128 + t
                                rope(bk, t_e, qr, b_qr, rt, True)
                                cc = col0 + t * 128
                                q_finish(qr, b_qr, sq, b_sq, red, b_red, qa, b_qa, 66, qTs, b_qTs,
                                         lambda: qT1.ap()[p * 8:(p + 1) * 8, :, cc:cc + 128].rearrange("h r n -> r h n"), B["qT1"])
                            linear_tm(wsrc(1, "w_in"), 3072 + p * 512, hT, b_hT, ntile, evac_q)
                    def evac_kv(t, bk):
                        cc = col0 + t * 128
                        if kind == "e":
                            t_e = col0 // 128 + t
                            pv = ps[:, bk, 0:256].rearrange("p (h two d) -> p h two d", two=2, d=32)
                            cosb = cs[:, t_e, 0:32].unsqueeze(1).broadcast_to([128, 4, 32])
                            sinb = cs[:, t_e, 32:64].unsqueeze(1).broadcast_to([128, 4, 32])
                            (t1, bt1), (t2, bt2) = rt
                            t1v = t1[:, 0:128].rearrange("p (h d) -> p h d", d=32)
                            t2v = t2[:, 0:128].rearrange("p (h d) -> p h d", d=32)
                            x1 = pv[:, :, 0, :]
                            x2 = pv[:, :, 1, :]
                            P.op("vector", lambda e: e.tensor_tensor(t1v, x1, cosb, ALU.mult), reads=[b_ps[bk], b_cs], writes=[bt1])
                            P.op("vector", lambda e: e.tensor_tensor(t2v, x2, sinb, ALU.mult), reads=[b_ps[bk], b_cs], writes=[bt2])
                            P.op("vector", lambda e: e.tensor_tensor(kr[:, :, 0:32], t1v, t2v, ALU.subtract), reads=[bt1, bt2], writes=[b_kr])
                            P.op("vector", lambda e: e.tensor_tensor(t1v, x2, cosb, ALU.mult), reads=[b_ps[bk], b_cs], writes=[bt1])
                            P.op("vector", lambda e: e.tensor_tensor(t2v, x1, sinb, ALU.mult), reads=[b_ps[bk], b_cs], writes=[bt2])
                            P.op("vector", lambda e: e.tensor_tensor(kr[:, :, 32:64], t1v, t2v, ALU.add), reads=[bt1, bt2], pwrites=[b_kr])
                        else:
                            P.op("vector", lambda e: e.tensor_copy(kr[:].rearrange("p h d -> p (h d)"), ps[:, bk, 0:256]), reads=[b_ps[bk]], writes=[b_kr])
                        P.op("vector", lambda e: e.tensor_tensor(ksq[:], kr[:], kr[:], ALU.mult), reads=[b_kr], writes=[b_ksq])
                        P.op("vector", lambda e: e.tensor_reduce(kred[:, 0:4], ksq[:], AX.X, ALU.add), reads=[b_ksq], writes=[b_kred])
                        kmax_update(kred, b_kred, kfirst)
                        P.op("scalar", lambda e: e.activation(out=kb_[:], in_=kr[:].rearrange("p h d -> p (h d)"), func=AF.Identity), reads=[b_kr], writes=[b_kb])
                        bk2 = prot.next()
                        for hh in range(2):
                            P.op("tensor", lambda e, hh=hh: e.transpose(psb(bk2)[:, hh * 128:(hh + 1) * 128], kb_[:, hh * 128:(hh + 1) * 128], identb[:]),
                                 reads=[b_kb, b_identb], **wr(b_ps[bk2], hh == 0))
                        P.op("vector", lambda e: e.tensor_copy(kTs[:].rearrange("q h n -> q (h n)"), psb(bk2)[:, 0:256]), reads=[b_ps[bk2]], writes=[b_kTs])
                        P.dma("sync", lambda e: e.dma_start(out=kt1.ap().rearrange("(h q) n -> q h n", q=128)[:, :, cc:cc + 128], in_=kTs[:]),
                              reads=[b_kTs], pwrites=[B["kt1"]])
                        vi, bv = vb.next()
                        P.op("scalar", lambda e: e.activation(out=vi[:], in_=ps[:, bk, 256:512], func=AF.Identity), reads=[b_ps[bk]], writes=[bv])
                        P.dma("sync", lambda e: e.dma_start(out=v1.ap()[cc:cc + 128, :], in_=vi[:]), reads=[bv], pwrites=[B["v1"]])
                    linear_tm(wsrc(1, "w_in"), 4096, hT, b_hT, ntile, evac_kv)
                for blk in blocks:
                    do_block(*blk)
                kb_finish(1)

        def l1_conv():
            with contextlib.ExitStack() as st:
                sw = sbt(st, "d_sw", [128, 8, 3], F32); b_sw = Buf()
                cxS = [sbt(st, "d_cx%d" % i, [128, NE], F32) for i in range(2)]; b_cx = [Buf(), Buf()]
                bS = [sbt(st, "d_b%d" % i, [128, NE], F32) for i in range(2)]; b_b = [Buf(), Buf()]
                acc = [sbt(st, "d_acc%d" % i, [128, TO], F32) for i in range(2)]; b_acc = [Buf(), Buf()]
                cl = [sbt(st, "d_cl%d" % i, [128, TO], BF16) for i in range(2)]; b_cl = [Buf(), Buf()]
                P.dma("sync", lambda e: e.dma_start(out=sw[:], in_=sconv_wT), writes=[b_sw])
                for i in range(8):
                    s = i % 2
                    eng = "vector"
                    P.dma("sync", lambda e, i=i, s=s: e.dma_start(out=cxS[s][:], in_=cxT.ap()[i]), reads=[B["cxT"]], writes=[b_cx[s]])
                    P.dma("sync", lambda e, i=i, s=s: e.dma_start(out=bS[s][:], in_=bT.ap()[i]), reads=[B["bT"]], writes=[b_b[s]])
                    P.op(eng, lambda e, s=s: e.tensor_scalar(cxS[s][:, 0:128], cxS[s][:, 0:128], flg[:, 0:1], None, ALU.mult), reads=[b_cx[s], b_flg], writes=[b_cx[s]])
                    P.op(eng, lambda e, s=s: e.tensor_scalar(cxS[s][:, NE - 128:NE], cxS[s][:, NE - 128:NE], flg[:, 1:2], None, ALU.mult), reads=[b_cx[s], b_flg], writes=[b_cx[s]])
                    P.op(eng, lambda e, s=s, i=i: e.tensor_scalar(acc[s][:], cxS[s][:, 127:127 + TO], sw[:, i, 0:1], None, ALU.mult), reads=[b_cx[s], b_sw], writes=[b_acc[s]])
                    for j in (1, 2):
                        P.op(eng, lambda e, s=s, i=i, j=j: e.scalar_tensor_tensor(acc[s][:], cxS[s][:, 127 + j:127 + j + TO], sw[:, i, j:j + 1], acc[s][:], ALU.mult, ALU.add),
                             reads=[b_cx[s], b_sw, b_acc[s]], writes=[b_acc[s]])
                    P.op(eng, lambda e, s=s: e.tensor_tensor(cl[s][:], acc[s][:], bS[s][:, 128:128 + TO], ALU.mult), reads=[b_acc[s], b_b[s]], writes=[b_cl[s]])
                    P.dma("sync", lambda e, i=i, s=s: e.dma_start(out=mixT[1].ap()[i, :, 128:128 + TO], in_=cl[s][:]), reads=[b_cl[s]], pwrites=[B["mixT1"]])

        def l1_attn():
            with contextlib.ExitStack() as st:
                KT = sbt(st, "f_KT", [66, 4, TT], BF16); b_KT = Buf()
                V = sbt(st, "f_V", [128, TT // 128, 256], BF16); b_V = Buf()
                qS = [sbt(st, "f_q%d" % i, [66, NE], BF16) for i in range(2)]; b_qS = [Buf(), Buf()]
                kfk = sbt(st, "f_kfk", [66, 16], BF16); b_kfk = Buf()
                msk = sbt(st, "f_msk", [128, 4, 128], BF16); b_msk = Buf()
                mskf = sbt(st, "f_mskf", [128, 2, 128], F32); b_mskf = Buf()
                pA = Rot([(sbt(st, "f_pA%d" % i, [128, 640], BF16), Buf()) for i in range(3)])
                pk = Rot([(sbt(st, "f_pk%d" % i, [1, 128], BF16), Buf()) for i in range(2)])
                rs = Rot([(sbt(st, "f_rs%d" % i, [64, 128], F32), Buf()) for i in range(2)])
                ost = [sbt(st, "f_ost%d" % i, [64, TO], BF16) for i in range(2)]; b_ost = [Buf(), Buf()]
                P.dma("sync", lambda e: e.dma_start(out=mskf[:], in_=masks_in), writes=[b_mskf])
                P.op("vector", lambda e: e.tensor_copy(msk[:, 0:2, :], mskf[:]), reads=[b_mskf], writes=[b_msk])
                P.op("vector", lambda e: e.tensor_scalar(msk[:, 2, :], mskf[:, 0, :], flg[:, 0:1], None, ALU.mult), reads=[b_mskf, b_flg], pwrites=[b_msk])
                P.op("vector", lambda e: e.tensor_scalar(msk[:, 3, :], mskf[:, 1, :], flg[:, 1:2], None, ALU.mult), reads=[b_mskf, b_flg], pwrites=[b_msk])
                P.op("vector", lambda e: e.memset(KT[64:66, :, :], 0.0), writes=[b_KT])
                P.op("vector", lambda e: e.memset(KT[64:65, :, :], 1.0), reads=[], writes=[b_KT])
                P.op("vector", lambda e: e.tensor_scalar(KT[64:65, :, :], KT[64:65, :, :], negkb[1][64:65, 0:1], None, ALU.mult), reads=[b_negkb[1], b_KT], writes=[b_KT])
                for kvh in range(4):
                    P.dma("sync", lambda e, kvh=kvh: e.dma_start(out=KT[0:64, kvh, :], in_=kt1.ap()[kvh * 64:(kvh + 1) * 64, :]), reads=[B["kt1"]], pwrites=[b_KT])
                P.dma("sync", lambda e: e.dma_start(out=V[:], in_=v1.ap().rearrange("(c p) e -> p c e", p=128)), reads=[B["v1"]], writes=[b_V])
                P.op("vector", lambda e: e.memset(kfk[:], 0.0), writes=[b_kfk])
                P.op("vector", lambda e: e.memset(kfk[64:65, :], 1.0), writes=[b_kfk])
                P.op("vector", lambda e: e.tensor_scalar(kfk[64:65, :], kfk[64:65, :], negkb[1][64:65, 0:1], None, ALU.mult), reads=[b_negkb[1], b_kfk], writes=[b_kfk])
                P.dma("gpsimd", lambda e: e.dma_start(out=kfk[65:66, :], in_=sink_in), reads=[], writes=[], pwrites=[b_kfk])
                NCH_CTX = NE // 128
                for head in range(16):
                    kvh = head // 4
                    qi = head % 2
                    P.dma("sync", lambda e, head=head, qi=qi: e.dma_start(out=qS[qi][:], in_=qT1.ap()[head]), reads=[B["qT1"]], writes=[b_qS[qi]])
                    def do_tile(head, kvh, qi, n):
                        qcol = qS[qi][:, n * 128:(n + 1) * 128]
                        bA = prot.next()
                        bB = prot.next()
                        chunks = [NCH_CTX, NCH_CTX + 1, n - 1, n, n + 1]
                        for ci, ch in enumerate(chunks):
                            dstp = ps[:, bA, ci * 128:(ci + 1) * 128] if ci < 4 else ps[:, bB, 0:128]
                            bkk = bA if ci < 4 else bB
                            P.op("tensor", lambda e, ch=ch, dstp=dstp, kvh=kvh: e.matmul(dstp, KT[:, kvh, ch * 128:(ch + 1) * 128], qcol, start=True, stop=True),
                                 reads=[b_KT, b_qS[qi]], **wr(b_ps[bkk], ci == 0 or ci == 4))
                        P.op("tensor", lambda e, head=head: e.matmul(ps[0:1, bB, 128:256], kfk[:, head:head + 1], qcol, start=True, stop=True),
                             reads=[b_kfk, b_qS[qi]], pwrites=[b_ps[bB]])
                        pa, bpa = pA.next()
                        P.op("scalar", lambda e, pa=pa: e.activation(out=pa[:, 0:512], in_=ps[:, bA, :], func=AF.Exp), reads=[b_ps[bA]], writes=[bpa])
                        P.op("scalar", lambda e, pa=pa: e.activation(out=pa[:, 512:640], in_=ps[:, bB, 0:128], func=AF.Exp), reads=[b_ps[bB]], pwrites=[bpa])
                        pki, bpk = pk.next()
                        P.op("scalar", lambda e, pki=pki: e.activation(out=pki[:], in_=ps[0:1, bB, 128:256], func=AF.Exp), reads=[b_ps[bB]], writes=[bpk])
                        mp = 2 if n == 1 else 0
                        mn = 3 if n == NT else 1
                        P.op("vector", lambda e, pa=pa, mp=mp: e.tensor_tensor(pa[:, 256:384], pa[:, 256:384], msk[:, mp, :], ALU.mult), reads=[bpa, b_msk], writes=[bpa])
                        P.op("vector", lambda e, pa=pa, mn=mn: e.tensor_tensor(pa[:, 512:640], pa[:, 512:640], msk[:, mn, :], ALU.mult), reads=[bpa, b_msk], writes=[bpa])
                        bo = prot.next()
                        bs = prot.next()
                        for ci, ch in enumerate(chunks):
                            P.op("tensor", lambda e, ci=ci, ch=ch, pa=pa, kvh=kvh: e.matmul(ps[0:64, bo, 0:128], V[:, ch, kvh * 64:(kvh + 1) * 64], pa[:, ci * 128:(ci + 1) * 128],
                                                                                              start=(ci == 0), stop=(ci == 4)),
                                 reads=[b_V, bpa], **wr(b_ps[bo], ci == 0))
                            P.op("tensor", lambda e, ci=ci, pa=pa: e.matmul(ps[0:64, bs, 0:128], onesb[:, 0:64], pa[:, ci * 128:(ci + 1) * 128], start=(ci == 0), stop=False),
                                 reads=[b_onesb, bpa], **wr(b_ps[bs], ci == 0))
                        P.op("tensor", lambda e, pki=pki: e.matmul(ps[0:64, bs, 0:128], onesb[0:1, 0:64], pki[:], start=False, stop=True),
                             reads=[b_onesb, bpk], pwrites=[b_ps[bs]])
                        ri, br = rs.next()
                        P.op("vector", lambda e, ri=ri: e.reciprocal(ri[:], ps[0:64, bs, 0:128]), reads=[b_ps[bs]], writes=[br])
                        P.op("vector", lambda e, ri=ri, n=n: e.tensor_tensor(ost[qi][:, (n - 1) * 128:n * 128], ps[0:64, bo, 0:128], ri[:], ALU.mult),
                             reads=[b_ps[bo], br], **wr(b_ost[qi], n == 1))
                    for n in range(1, NT + 1):
                        do_tile(head, kvh, qi, n)
                    P.dma("sync", lambda e, head=head, qi=qi: e.dma_start(out=mixT[1].ap()[8 + head // 2, (head % 2) * 64:(head % 2) * 64 + 64, 128:128 + TO], in_=ost[qi][:]),
                          reads=[b_ost[qi]], pwrites=[B["mixT1"]])

        if 'adaln' in phases:
            adaln_phase()
            P.barrier()
        if 'l0in' in phases:
            l0_inproj()
            P.barrier()
        if 'l0conv' in phases:
            l0_conv()
            P.barrier()
        if 'l0attn' in phases:
            l0_attn()
            P.barrier()
        if 'tail0' in phases:
            tail_phase(0, [(0, c0, N, None) for (c0, N) in EBLOCKS] + [(1, NE, NCTX, None)], xT0, B["xT0"], xT1, B["xT1"], False)
            P.barrier()
        if 'l1in' in phases:
            l1_inproj()
            P.barrier()
        if 'l1conv' in phases:
            l1_conv()
            P.barrier()
        if 'l1attn' in phases:
            l1_attn()
            P.barrier()
        if 'tail1' in phases:
            tail_phase(1, [(0, 128 + 512 * b, 512, 512 * b) for b in range(4)], xT1, B["xT1"], None, None, True)
        scr = dict(xT0=xT0, xT1=xT1, uT_e=uT_e, uT_c=uT_c, mixT0=mixT[0], mixT1=mixT[1], qT0=qT0, kt_all=kt_all,
                   v_all=v_all, kt_ctx=kt_ctx, cxT=cxT, bT=bT, qT1=qT1, kt1=kt1, v1=v1)
        for nm in dbg:
            if nm == 'tab':
                o = nc.dram_tensor("dbg_tab", [128, 2 * 2 * 6 * KC], F32, kind="ExternalOutput").ap()
                out_toks.append(P.dma("sync", lambda e, o=o: e.dma_start(out=o, in_=tab[:].rearrange("p a b c d -> p (a b c d)")), reads=[b_tab]))
            elif nm == 'negkb':
                for l in range(2):
                    o = nc.dram_tensor("dbg_negkb%d" % l, [128, 1], F32, kind="ExternalOutput").ap()
                    out_toks.append(P.dma("sync", lambda e, o=o, l=l: e.dma_start(out=o, in_=negkb[l][:]), reads=[b_negkb[l]]))
            else:
                t = scr[nm]
                o = nc.dram_tensor("dbg_" + nm, list(t.shape), t.dtype, kind="ExternalOutput").ap()
                out_toks.append(P.dma("sync", lambda e, o=o, t=t: e.dma_start(out=o, in_=t.ap()), reads=[B[nm]]))
        P.finish_wait("sync", out_toks)
        P.emit(top)
    return nc, declared


def _bind_tab(tab, modT, gn, l, r, i):
    def T():
        return tab[:, l, r, i, :]

    def M(v):
        return modT[:, v * KC:(v + 1) * KC, r]
    if i == 0:
        return lambda e: e.scalar_tensor_tensor(T(), M(1), 1.0, gn[:, 0, :], ALU.add, ALU.mult)
    if i == 1:
        return lambda e: e.tensor_copy(T(), M(0))
    if i == 2:
        return lambda e: e.tensor_tensor(T(), M(2), gn[:, 1, :], ALU.mult)
    if i == 3:
        return lambda e: e.scalar_tensor_tensor(T(), M(4), 1.0, gn[:, 2, :], ALU.add, ALU.mult)
    if i == 4:
        return lambda e: e.tensor_copy(T(), M(3))
    return lambda e: e.tensor_tensor(T(), M(5), gn[:, 3, :], ALU.mult)


_NC_CACHE = {}
_ALL_INPUTS = ("x", "c", "ctx", "c_ctx",
               "l0_mod_w", "l0_mod_b", "l0_norm_mix_pre", "l0_norm_mix_post", "l0_norm_mlp_pre", "l0_norm_mlp_post",
               "l0_w_in", "l0_conv_w", "l0_conv_b", "l0_ln_g", "l0_ln_b", "l0_lambda_q1", "l0_lambda_k1", "l0_lambda_q2",
               "l0_lambda_k2", "l0_subln_g", "l0_w_out", "l0_mlp_w1", "l0_mlp_w2",
               "l1_mod_w", "l1_mod_b", "l1_norm_mix_pre", "l1_norm_mix_post", "l1_norm_mlp_pre", "l1_norm_mlp_post",
               "l1_w_in", "l1_sconv_w", "l1_sink", "l1_w_out", "l1_mlp_w1", "l1_mlp_w2")


def _fm(v, nchunk):
    return np.ascontiguousarray(np.asarray(v, np.float32).reshape(nchunk, 128).T)


def prep_inputs(inp):
    f = lambda k: np.asarray(inp[k], np.float32)
    x = f("x")[0]
    ctx = f("ctx")[0]
    inv = np.power(10000.0, -np.arange(16, dtype=np.float32) / 16).astype(np.float32)
    ident = np.eye(128, dtype=np.float32)
    jj = np.arange(128)[:, None]
    ii = np.arange(128)[None, :]
    masks = np.stack([(jj >= ii), (jj <= ii)], axis=1).astype(np.float32)
    shared = dict(ctx=np.ascontiguousarray(ctx), ident=ident, masks=np.ascontiguousarray(masks), x_all=np.ascontiguousarray(x))
    pa = np.arange(SEQ)
    anga = np.concatenate([(pa // 64).astype(np.float32)[:, None] * inv, (pa % 64).astype(np.float32)[:, None] * inv], axis=-1).astype(np.float32)
    shared["cs_all"] = np.concatenate([np.cos(anga), np.sin(anga)], axis=-1).astype(np.float32)
    cv = np.stack([f("c")[0], f("c_ctx")], axis=-1)
    shared["cvT"] = np.ascontiguousarray(cv.reshape(KC, 128, 2).transpose(1, 0, 2))
    for l in range(2):
        pre = "l%d_" % l
        shared[pre + "mod_w"] = f(pre + "mod_w")
        shared[pre + "mod_bT"] = _fm(f(pre + "mod_b"), 96)
        shared[pre + "gains"] = np.ascontiguousarray(np.stack(
            [_fm(f(pre + k), KC) for k in ("norm_mix_pre", "norm_mix_post", "norm_mlp_pre", "norm_mlp_post")], axis=1))
        for k in ("w_in", "w_out", "mlp_w1", "mlp_w2"):
            shared[pre + k] = f(pre + k)
    shared["l0_conv_wT"] = np.ascontiguousarray(f("l0_conv_w").T.reshape(8, 128, 31).transpose(1, 0, 2))
    shared["l0_conv_misc"] = np.ascontiguousarray(np.stack([_fm(f("l0_" + k), 8) for k in ("conv_b", "ln_g", "ln_b")], axis=1))
    shared["l0_lam"] = np.stack([f("l0_lambda_q1"), f("l0_lambda_k1"), f("l0_lambda_q2"), f("l0_lambda_k2")], axis=0)
    shared["l0_subln"] = f("l0_subln_g").reshape(128, 1)
    shared["l1_sconv_wT"] = np.ascontiguousarray(f("l1_sconv_w").T.reshape(8, 128, 3).transpose(1, 0, 2))
    shared["l1_sink"] = f("l1_sink").reshape(1, 16)
    in_maps = []
    for r in range(NCORES):
        s = r * TO
        xe = np.zeros((NE + 256, D), np.float32)
        lo = s - 256
        hi = s + TO + 256
        a = max(lo, 0)
        b = min(hi, SEQ)
        xe[a - lo:b - lo] = x[a:b]
        pos = np.arange(s - 128, s + TO + 128)
        pos = np.clip(pos, 0, SEQ - 1)
        row = (pos // 64).astype(np.float32)
        col = (pos % 64).astype(np.float32)
        ang = np.concatenate([row[:, None] * inv, col[:, None] * inv], axis=-1).astype(np.float32)
        cs = np.concatenate([np.cos(ang), np.sin(ang)], axis=-1).astype(np.float32)
        fl = np.zeros((128, 2), np.float32)
        fl[:, 0] = 1.0 if r > 0 else 0.0
        fl[:, 1] = 1.0 if r < NCORES - 1 else 0.0
        m = dict(shared)
        m["x_ext"] = xe
        m["cs_ext"] = cs
        m["flags"] = fl
        in_maps.append(m)
    return in_maps


def kernel(**inp):
    if "nc" not in _NC_CACHE:
        _NC_CACHE["nc"] = build()
    nc, declared = _NC_CACHE["nc"]
    in_maps = prep_inputs(inp)
    in_maps = [{k: v for k, v in m.items() if k in declared} for m in in_maps]
    res = run_bass_kernel_spmd(nc, in_maps, core_ids=list(range(NCORES)))
    out = np.concatenate([np.asarray(res.results[r]["out"], np.float32) for r in range(NCORES)], axis=0)
    return out[None]
```

```python
import contextlib
import math
import numpy as np
import ml_dtypes
import concourse.bass as bass
import concourse.mybir as mybir
from concourse.bass_utils import run_bass_kernel_spmd

F32 = mybir.dt.float32
BF16 = mybir.dt.bfloat16
AF = mybir.ActivationFunctionType
ALU = mybir.AluOpType
AX = mybir.AxisListType

NCORES = 8
D = 2048
KC = 16
SEQ = 16384
TO = SEQ // NCORES
NT = TO // 128
NCTX = 256
TT = TO + NCTX
HD = 64
SCALE = HD ** -0.5
NORM_EPS = 1e-6
LN_EPS = 1e-5
DFF = 8192
W0 = 5120
W1 = 4608
LAM_INIT0 = 0.8 - 0.6 * math.exp(0.0)
KB_MARGIN = 1.25

ENGS = ["tensor", "vector", "scalar", "gpsimd", "sync"]
NDMASEM = 8


class Buf:
    __slots__ = ("writer", "pw", "readers")

    def __init__(self):
        self.writer = None
        self.pw = []
        self.readers = []


class Prog:
    def __init__(self, nc):
        self.nc = nc
        self.ops = {e: [] for e in ENGS}
        self.dma_cnt = {}
        self.dma_rr = {e: 0 for e in ENGS}
        self.known = {e: {} for e in ENGS}
        self.need_inc = {e: set() for e in ENGS}

    def _add_wait(self, eng, waits, tok, is_raw):
        if tok is None:
            return
        if tok[0] == "E":
            _, e2, idx = tok
            if e2 == eng and (eng == "tensor" or not is_raw):
                return
            key = ("E", e2)
            val = idx
        else:
            _, q, slot, cnt = tok
            key = ("D", q, slot)
            val = cnt
        if self.known[eng].get(key, -1) >= val:
            return
        self.known[eng][key] = val
        waits.append(tok)
        if tok[0] == "E":
            self.need_inc[tok[1]].add(tok[2])

    def _deps(self, eng, reads, writes, pwrites, dma):
        waits = []
        for b in reads:
            self._add_wait(eng, waits, b.writer, True)
            for w in b.pw:
                self._add_wait(eng, waits, w, True)
        for b in writes:
            self._add_wait(eng, waits, b.writer, dma)
            for w in b.pw:
                self._add_wait(eng, waits, w, dma)
            for r in b.readers:
                self._add_wait(eng, waits, r, dma)
        for b in pwrites:
            self._add_wait(eng, waits, b.writer, dma)
            for r in b.readers:
                self._add_wait(eng, waits, r, dma)
        return waits

    def _commit(self, tok, reads, writes, pwrites):
        for b in reads:
            b.readers.append(tok)
            if len(b.readers) > 24:
                b.readers = b.readers[-24:] if False else b.readers
        for b in writes:
            b.writer = tok
            b.pw = []
            b.readers = []
        for b in pwrites:
            b.pw.append(tok)

    def op(self, eng, fn, reads=(), writes=(), pwrites=()):
        waits = self._deps(eng, reads, writes, pwrites, False)
        idx = len(self.ops[eng])
        tok = ("E", eng, idx)
        self.ops[eng].append(dict(fn=fn, waits=waits, dma=None))
        self._commit(tok, reads, writes, pwrites)
        return tok

    def dma(self, queue, fn, reads=(), writes=(), pwrites=()):
        waits = self._deps(queue, reads, writes, pwrites, True)
        slot = self.dma_rr[queue] % NDMASEM
        self.dma_rr[queue] += 1
        prev = self.dma_cnt.get((queue, slot), 0)
        if prev > 0:
            self._add_wait(queue, waits, ("D", queue, slot, prev), True)
        cnt = prev + 1
        self.dma_cnt[(queue, slot)] = cnt
        tok = ("D", queue, slot, cnt)
        self.ops[queue].append(dict(fn=fn, waits=waits, dma=(slot, cnt)))
        self._commit(tok, reads, writes, pwrites)
        return tok

    def barrier(self):
        toks = []
        for e in ENGS:
            for i in range(len(self.ops[e]) - 1, -1, -1):
                o = self.ops[e][i]
                if o["fn"] is not None and o["dma"] is None:
                    toks.append(("E", e, i))
                    break
        for (q, slot), cnt in self.dma_cnt.items():
            toks.append(("D", q, slot, cnt))
        for e in ENGS:
            waits = []
            for t in toks:
                self._add_wait(e, waits, t, True)
            if waits:
                self.ops[e].append(dict(fn=None, waits=waits, dma=None))

    def finish_wait(self, eng, toks):
        waits = []
        for t in toks:
            self._add_wait(eng, waits, t, True)
        self.ops[eng].append(dict(fn=None, waits=waits, dma=None))

    def emit(self, st):
        nc = self.nc
        esem = {e: st.enter_context(nc.semaphore("es_" + e)) for e in ENGS}
        dsem = {}
        for q in ENGS:
            for s in range(NDMASEM):
                if (q, s) in self.dma_cnt:
                    dsem[(q, s)] = st.enter_context(nc.semaphore("ds_%s_%d" % (q, s)))
        incval = {}
        for e in ENGS:
            c = 0
            m = {}
            for i in range(len(self.ops[e])):
                if i in self.need_inc[e]:
                    c += 1
                    m[i] = c
            incval[e] = m
        block = st.enter_context(nc.Block())

        def run(ename):
            def body(eng):
                for i, o in enumerate(self.ops[ename]):
                    for t in o["waits"]:
                        if t[0] == "E":
                            eng.wait_ge(esem[t[1]], incval[t[1]][t[2]])
                        else:
                            eng.wait_ge(dsem[(t[1], t[2])], 16 * t[3])
                    if o["fn"] is None:
                        continue
                    ins = o["fn"](eng)
                    if o["dma"] is not None:
                        ins.then_inc(dsem[(ename, o["dma"][0])], 16)
                    elif i in incval[ename]:
                        ins.then_inc(esem[ename], 1)
            return body

        for e in ENGS:
            if self.ops[e]:
                getattr(block, e)(run(e))


class Rot:
    def __init__(self, items):
        self.items = items
        self.i = 0
        self.reserved = set()

    def next(self):
        for _ in range(2 * len(self.items)):
            it = self.items[self.i % len(self.items)]
            self.i += 1
            if not isinstance(it, int) or it not in self.reserved:
                return it
        raise RuntimeError("no free item")

    def take(self):
        it = self.next()
        self.reserved.add(it)
        return it

    def release(self, it):
        self.reserved.discard(it)


NE = TO + 256
NET = NE // 128
TT = NE + NCTX
UW = 15 + NE + 15
EBLOCKS = [(0, 512), (512, 512), (1024, 512), (1536, 512), (2048, 256)]


def build(phases=None, dbg=()):
    ALLP = ['adaln', 'l0in', 'l0conv', 'l0attn', 'tail0', 'l1in', 'l1conv', 'l1attn', 'tail1']
    phases = ALLP if phases is None else phases
    declared = []
    nc = bass.Bass("TRN2", target_bir_lowering=False)
    P = Prog(nc)

    def din(name, shape, dt=F32):
        declared.append(name)
        return nc.dram_tensor(name, list(shape), dt, kind="ExternalInput").ap()

    x_ext = din("x_ext", [NE + 256, D])
    x_all = din("x_all", [SEQ, D])
    cs_all = din("cs_all", [SEQ, 64])
    ctx_in = din("ctx", [NCTX, D])
    cvT = din("cvT", [128, KC, 2])
    cs_ext = din("cs_ext", [NE, 64])
    ident_in = din("ident", [128, 128])
    flags = din("flags", [128, 2])
    masks_in = din("masks", [128, 2, 128])
    class LazyW(dict):
        def __init__(self, l, wcols):
            self.l = l
            self.shapes = dict(mod_w=("l%d_mod_w", [D, 6 * D]), mod_bT=("l%d_mod_bT", [128, 96]), gains=("l%d_gains", [128, 4, KC]),
                               w_in=("l%d_w_in", [D, wcols]), w_out=("l%d_w_out", [D, D]), w1=("l%d_mlp_w1", [D, DFF]), w2=("l%d_mlp_w2", [DFF, D]))

        def __missing__(self, k):
            nm, shp = self.shapes[k]
            v = din(nm % self.l, shp)
            self[k] = v
            return v
    W = {0: LazyW(0, W0), 1: LazyW(1, W1)}
    conv_wT = din("l0_conv_wT", [128, 8, 31])
    conv_misc = din("l0_conv_misc", [128, 3, 8])
    lam_in = din("l0_lam", [4, 64])
    subln = din("l0_subln", [128, 1])
    sconv_wT = din("l1_sconv_wT", [128, 8, 3])
    sink_in = din("l1_sink", [1, 16])
    out_ext = nc.dram_tensor("out", [TO, D], F32, kind="ExternalOutput").ap()

    def dscr(name, shape, dt):
        return nc.dram_tensor(name, list(shape), dt)

    xT0 = dscr("xT0", [KC, 128, TT], F32)
    xT1 = dscr("xT1", [KC, 128, TT], F32)
    uT_e = dscr("uT_e", [8, 128, UW], F32)
    uT_c = dscr("uT_c", [8, 128, NCTX], F32)
    mixT = [dscr("mixT%d" % l, [KC, 128, TT], BF16) for l in range(2)]
    qT0 = dscr("qT0", [16, 65, TT], BF16)
    kt_all = dscr("kt_all", [8 * 128, SEQ], BF16)
    v_all = dscr("v_all", [8 * 128, SEQ], BF16)
    kt_ctx = dscr("kt_ctx", [8, 128, NCTX], BF16)
    cxT = dscr("cxT", [8, 128, NE], F32)
    bT = dscr("bT", [8, 128, NE], F32)
    qT1 = dscr("qT1", [16, 66, NE], BF16)
    kt1 = dscr("kt1", [256, TT], BF16)
    v1 = dscr("v1", [TT, 256], BF16)
    B = {nm: Buf() for nm in ["xT0", "xT1", "uT_e", "uT_c", "mixT0", "mixT1", "qT0", "kt_all", "v_all",
                              "kt_ctx", "cxT", "bT", "qT1", "kt1", "v1"]}
    out_toks = []

    top = contextlib.ExitStack()
    with top:
        _uniq = [0]

        def sbt(st, name, shape, dt):
            _uniq[0] += 1
            return st.enter_context(nc.sbuf_tensor("%s_%d" % (name, _uniq[0]), list(shape), dt))

        identf = sbt(top, "identf", [128, 128], F32); b_identf = Buf()
        identb = sbt(top, "identb", [128, 128], BF16); b_identb = Buf()
        onesb = sbt(top, "onesb", [128, 128], BF16); b_onesb = Buf()
        onesf = sbt(top, "onesf", [128, 128], F32); b_onesf = Buf()
        tab = sbt(top, "tab", [128, 2, 2, 6, KC], F32); b_tab = Buf()
        flg = sbt(top, "flg", [128, 2], F32); b_flg = Buf()
        cs = sbt(top, "cs", [128, NET, 64], F32); b_cs = Buf()
        negkb = [sbt(top, "negkb%d" % l, [128, 1], F32) for l in range(2)]
        b_negkb = [Buf(), Buf()]
        kmax = sbt(top, "kmax", [128, 8], F32); b_kmax = Buf()
        WP = {}

        def alloc_wp(st, tag, nslots=3):
            WP["t"] = sbt(st, "wp_" + tag, [128, nslots, KC, 512], BF16)
            WP["rot"] = Rot([(i, Buf()) for i in range(nslots)])
        ps = top.enter_context(nc.psum_tensor("ps", [128, 8, 512], F32))
        b_ps = [Buf() for _ in range(8)]
        prot = Rot(list(range(8)))

        def psb(bk):
            return ps[:, bk, :].bitcast(BF16)

        P.dma("sync", lambda e: e.dma_start(out=identf[:], in_=ident_in), writes=[b_identf])
        P.op("vector", lambda e: e.tensor_copy(identb[:], identf[:]), reads=[b_identf], writes=[b_identb])
        P.op("vector", lambda e: e.memset(onesb[:], 1.0), writes=[b_onesb])
        P.op("vector", lambda e: e.memset(onesf[:], 1.0), writes=[b_onesf])
        P.dma("sync", lambda e: e.dma_start(out=flg[:], in_=flags), writes=[b_flg])
        P.dma("sync", lambda e: e.dma_start(out=cs[:], in_=cs_ext.rearrange("(t p) c -> p t c", p=128)), writes=[b_cs])

        def wr(b, first):
            return dict(writes=[b] if first else [], pwrites=[] if first else [b])

        WB = {}

        def wsrc(l, name):
            if (l, name) in WB:
                t, b = WB[(l, name)]
                return t.ap(), [b]
            return W[l][name], []

        def precast(l, name, rows_per):
            src = W[l][name]
            rows, cols = src.shape
            t = dscr("wbf%d_%s" % (l, name), [rows, cols], BF16)
            b = Buf()
            for r0 in range(0, rows, rows_per):
                P.dma("gpsimd", lambda e, r0=r0: e.dma_start(out=t.ap()[r0:r0 + rows_per, :], in_=src[r0:r0 + rows_per, :]), pwrites=[b])
            WB[(l, name)] = (t, b)

        def load_wpiece(wdb, r0, c0, nk=KC, ncol=512):
            wd, rb = wdb if isinstance(wdb, tuple) else (wdb, [])
            slot, bw = WP["rot"].next()
            wp = WP["t"]
            src = wd[r0:r0 + nk * 128, c0:c0 + ncol].rearrange("(kc p) n -> p kc n", p=128)
            P.dma("gpsimd", lambda e: e.dma_start(out=wp[:, slot, 0:nk, 0:ncol], in_=src), reads=rb, writes=[bw])
            return slot, bw

        def mm_acc(bk, ncols_ap, lhsT_fn, rhs_fn, n, reads):
            for k in range(n):
                P.op("tensor", lambda e, k=k: e.matmul(ncols_ap, lhsT_fn(k), rhs_fn(k), start=(k == 0), stop=(k == n - 1)),
                     reads=reads, **wr(b_ps[bk], k == 0))

        def linear_fm(wd, col0, ncolg, rhs_fn, rhs_bufs, nkc, N, evac, kp=1):
            wpt = WP["t"]
            kper = nkc // kp
            for g in range(ncolg):
                banks = [prot.next() for _ in range(4)]
                for kpi in range(kp):
                    slot, bw = load_wpiece(wd, kpi * kper * 128, col0 + g * 512, nk=kper)
                    for oc in range(4):
                        bk = banks[oc]
                        for k in range(kper):
                            kc = kpi * kper + k
                            P.op("tensor", lambda e, bk=bk, slot=slot, k=k, oc=oc, kc=kc: e.matmul(
                                ps[:, bk, 0:N], wpt[:, slot, k, oc * 128:(oc + 1) * 128], rhs_fn(kc),
                                start=(kc == 0), stop=(kc == nkc - 1)),
                                reads=[bw] + rhs_bufs, **wr(b_ps[bk], kc == 0))
                for oc in range(4):
                    evac(g * 4 + oc, ps[:, banks[oc], 0:N], b_ps[banks[oc]])

        def linear_tm(wd, col0, hT, b_hT, ntile, evac):
            wpt = WP["t"]
            slot, bw = load_wpiece(wd, 0, col0)
            for t in range(ntile):
                bk = prot.next()
                for kc in range(KC):
                    P.op("tensor", lambda e, bk=bk, kc=kc, t=t: e.matmul(ps[:, bk, :], hT[:, kc, t * 128:(t + 1) * 128], wpt[:, slot, kc, :],
                                                                         start=(kc == 0), stop=(kc == KC - 1)),
                         reads=[bw, b_hT], **wr(b_ps[bk], kc == 0))
                evac(t, bk)

        def stats_finish(bk, N, rstd, b_rstd, scale, eps):
            P.op("scalar", lambda e: e.activation(out=rstd[:, 0:N], in_=ps[:, bk, 0:N], func=AF.Sqrt, scale=scale, bias=eps),
                 reads=[b_ps[bk]], writes=[b_rstd])
            P.op("vector", lambda e: e.reciprocal(rstd[:, 0:N], rstd[:, 0:N]), reads=[b_rstd], writes=[b_rstd])

        def sandwich_in(xT, b_xT, N, l, r, ia, hT, b_hT, rstd, b_rstd, tmpr, sqr):
            bk = prot.next()
            for c in range(KC):
                si, bs_ = sqr.next()
                P.op("scalar", lambda e, c=c, si=si: e.activation(out=si[:, 0:N], in_=xT[:, c, 0:N], func=AF.Square),
                     reads=[b_xT], writes=[bs_])
                P.op("tensor", lambda e, c=c, si=si: e.matmul(ps[:, bk, 0:N], onesb[:], si[:, 0:N], start=(c == 0), stop=(c == KC - 1)),
                     reads=[bs_, b_onesb], **wr(b_ps[bk], c == 0))
            stats_finish(bk, N, rstd, b_rstd, 1.0 / D, NORM_EPS)
            for fc in range(KC):
                ti, bt = tmpr.next()
                P.op("vector", lambda e, fc=fc, ti=ti: e.tensor_tensor(ti[:, 0:N], xT[:, fc, 0:N], rstd[:, 0:N], ALU.mult),
                     reads=[b_xT, b_rstd], writes=[bt])
                P.op("scalar", lambda e, fc=fc, ti=ti: e.activation(out=hT[:, fc, 0:N], in_=ti[:, 0:N], func=AF.Identity,
                                                                     scale=tab[:, l, r, ia, fc:fc + 1], bias=tab[:, l, r, ia + 1, fc:fc + 1]),
                     reads=[bt, b_tab], **wr(b_hT, fc == 0))

        def adaln_phase():
            with contextlib.ExitStack() as st:
                alloc_wp(st, "m")
                wpt = WP["t"]
                scT = sbt(st, "scT", [128, KC, 2], BF16); b_scT = Buf()
                cvs = sbt(st, "cvs", [128, KC, 2], F32); b_cvs = Buf()
                msb = sbt(st, "msb", [2, 6 * D], F32); b_msb = Buf()
                modT = sbt(st, "modT", [128, 96, 2], F32); b_modT = Buf()
                modb = sbt(st, "modb", [128, 96], F32); b_modb = Buf()
                gn = sbt(st, "gn", [128, 4, KC], F32); b_gn = Buf()
                P.dma("sync", lambda e: e.dma_start(out=cvs[:], in_=cvT), writes=[b_cvs])
                P.op("scalar", lambda e: e.activation(out=scT[:], in_=cvs[:], func=AF.Silu), reads=[b_cvs], writes=[b_scT])

                def do_layer(l):
                    P.dma("sync", lambda e: e.dma_start(out=modb[:], in_=W[l]["mod_bT"]), writes=[b_modb])
                    P.dma("sync", lambda e: e.dma_start(out=gn[:], in_=W[l]["gains"]), writes=[b_gn])

                    def do_cg(cg):
                        slot, bw = load_wpiece(W[l]["mod_w"], 0, cg * 512)
                        bk = prot.next()
                        for kc in range(KC):
                            P.op("tensor", lambda e, kc=kc: e.matmul(ps[0:2, bk, :], scT[:, kc, :], wpt[:, slot, kc, :], start=(kc == 0), stop=(kc == KC - 1)),
                                 reads=[bw, b_scT], **wr(b_ps[bk], kc == 0))
                        P.op("vector", lambda e: e.tensor_copy(msb[:, cg * 512:(cg + 1) * 512], ps[0:2, bk, :]), reads=[b_ps[bk]], **wr(b_msb, cg == 0))
                    for cg in range(24):
                        do_cg(cg)
                    bkT = prot.next()
                    for j in range(96):
                        P.op("tensor", lambda e, j=j: e.transpose(ps[:, bkT, 2 * j:2 * j + 2], msb[0:2, j * 128:(j + 1) * 128], identf[0:2, 0:2]),
                             reads=[b_msb, b_identf], **wr(b_ps[bkT], j == 0))
                    psv = ps[:, bkT, 0:192].rearrange("p (j r) -> p j r", r=2)
                    for r in range(2):
                        P.op("vector", lambda e, r=r: e.tensor_tensor(modT[:, :, r], psv[:, :, r], modb[:], ALU.add),
                             reads=[b_ps[bkT], b_modb], **wr(b_modT, r == 0))
                    for r in range(2):
                        for i in range(6):
                            P.op("vector", _bind_tab(tab, modT, gn, l, r, i), reads=[b_modT, b_gn], pwrites=[b_tab])
                for l in range(2):
                    do_layer(l)

        def rope(bk, t_e, dst, b_dst, tmps, do_rope, cst=None, b_cst=None):
            pv = ps[:, bk, :].rearrange("p (h two d) -> p h two d", two=2, d=32)
            if not do_rope:
                P.op("vector", lambda e: e.tensor_copy(dst[:].rearrange("p h d -> p (h d)"), ps[:, bk, :]), reads=[b_ps[bk]], writes=[b_dst])
                return
            if cst is None:
                cosb = cs[:, t_e, 0:32].unsqueeze(1).broadcast_to([128, 8, 32])
                sinb = cs[:, t_e, 32:64].unsqueeze(1).broadcast_to([128, 8, 32])
                b_csx = b_cs
            else:
                cosb = cst[:, 0:32].unsqueeze(1).broadcast_to([128, 8, 32])
                sinb = cst[:, 32:64].unsqueeze(1).broadcast_to([128, 8, 32])
                b_csx = b_cst
            (t1, bt1), (t2, bt2) = tmps
            t1v = t1[:, 0:256].rearrange("p (h d) -> p h d", d=32)
            t2v = t2[:, 0:256].rearrange("p (h d) -> p h d", d=32)
            x1 = pv[:, :, 0, :]
            x2 = pv[:, :, 1, :]
            P.op("vector", lambda e: e.tensor_tensor(t1v, x1, cosb, ALU.mult), reads=[b_ps[bk], b_csx], writes=[bt1])
            P.op("vector", lambda e: e.tensor_tensor(t2v, x2, sinb, ALU.mult), reads=[b_ps[bk], b_csx], writes=[bt2])
            P.op("vector", lambda e: e.tensor_tensor(dst[:, :, 0:32], t1v, t2v, ALU.subtract), reads=[bt1, bt2], writes=[b_dst])
            P.op("vector", lambda e: e.tensor_tensor(t1v, x2, cosb, ALU.mult), reads=[b_ps[bk], b_csx], writes=[bt1])
            P.op("vector", lambda e: e.tensor_tensor(t2v, x1, sinb, ALU.mult), reads=[b_ps[bk], b_csx], writes=[bt2])
            P.op("vector", lambda e: e.tensor_tensor(dst[:, :, 32:64], t1v, t2v, ALU.add), reads=[bt1, bt2], pwrites=[b_dst])

        def sqnorm(src, b_src, sq, b_sq, red, b_red):
            P.op("vector", lambda e: e.tensor_tensor(sq[:], src[:], src[:], ALU.mult), reads=[b_src], writes=[b_sq])
            P.op("vector", lambda e: e.tensor_reduce(red[:], sq[:], AX.X, ALU.add), reads=[b_sq], writes=[b_red])

        def q_finish(qr, b_qr, sq, b_sq, red, b_red, qa, b_qa, naug, qTs, b_qTs, dst_ap_fn, b_dstbuf):
            sqnorm(qr, b_qr, sq, b_sq, red, b_red)
            P.op("scalar", lambda e: e.activation(out=qa[:, :, 64:65], in_=red[:].unsqueeze(2), func=AF.Sqrt, scale=SCALE * SCALE),
                 reads=[b_red], writes=[b_qa])
            P.op("scalar", lambda e: e.activation(out=qa[:, :, 0:64], in_=qr[:], func=AF.Identity, scale=SCALE), reads=[b_qr], pwrites=[b_qa])
            bk = prot.next()
            for hm in range(8):
                P.op("tensor", lambda e, hm=hm: e.transpose(psb(bk)[0:naug, hm * 128:(hm + 1) * 128], qa[:, hm, :], identb[:]),
                     reads=[b_qa, b_identb], **wr(b_ps[bk], hm == 0))
            P.op("vector", lambda e: e.tensor_copy(qTs[:].rearrange("r h n -> r (h n)"), psb(bk)[0:naug, :]), reads=[b_ps[bk]], writes=[b_qTs])
            P.dma("sync", lambda e: e.dma_start(out=dst_ap_fn(), in_=qTs[:]), reads=[b_qTs], pwrites=[b_dstbuf])

        def kmax_update(red, b_red, first):
            if first[0]:
                P.op("vector", lambda e: e.tensor_copy(kmax[:], red[:]), reads=[b_red], writes=[b_kmax])
                first[0] = False
            else:
                P.op("vector", lambda e: e.tensor_tensor(kmax[:], kmax[:], red[:], ALU.max), reads=[b_red, b_kmax], writes=[b_kmax])

        def kb_finish(l):
            with contextlib.ExitStack() as st:
                m1 = sbt(st, "kbm1", [128, 1], F32); b1 = Buf()
                row = sbt(st, "kbrow", [1, 128], F32); b2 = Buf()
                one = sbt(st, "kbone", [1, 1], F32); b3 = Buf()
                P.op("vector", lambda e: e.tensor_reduce(m1[:], kmax[:], AX.X, ALU.max), reads=[b_kmax], writes=[b1])
                bk = prot.next()
                P.op("tensor", lambda e: e.transpose(ps[0:1, bk, 0:128], m1[:], identf[:]), reads=[b1, b_identf], writes=[b_ps[bk]])
                P.op("vector", lambda e: e.tensor_copy(row[:], ps[0:1, bk, 0:128]), reads=[b_ps[bk]], writes=[b2])
                P.op("vector", lambda e: e.tensor_reduce(one[:], row[:], AX.X, ALU.max), reads=[b2], writes=[b3])
                P.op("scalar", lambda e: e.activation(out=one[:], in_=one[:], func=AF.Sqrt), reads=[b3], writes=[b3])
                P.op("vector", lambda e: e.tensor_scalar(one[:], one[:], -KB_MARGIN, None, ALU.mult), reads=[b3], writes=[b3])
                bk2 = prot.next()
                P.op("tensor", lambda e: e.matmul(ps[:, bk2, 0:1], onesf[0:1, :], one[:], start=True, stop=True),
                     reads=[b3, b_onesf], writes=[b_ps[bk2]])
                P.op("vector", lambda e: e.tensor_copy(negkb[l][:], ps[:, bk2, 0:1]), reads=[b_ps[bk2]], writes=[b_negkb[l]])

        def l0_inproj():
            with contextlib.ExitStack() as st:
                alloc_wp(st, "a", 2)
                wpt = WP["t"]
                xrow = [sbt(st, "xrow%d" % i, [128, D], F32) for i in range(2)]
                b_xrow = [Buf(), Buf()]
                xTbs = Rot([(sbt(st, "a_xTb%d" % i, [128, KC, 512], F32), Buf()) for i in range(2)])
                hTs = Rot([(sbt(st, "a_hT%d" % i, [128, KC, 512], BF16), Buf()) for i in range(2)])
                rstd = sbt(st, "a_rstd", [128, 512], F32); b_rstd = Buf()
                tmpr = Rot([(sbt(st, "a_tmp%d" % i, [128, 512], F32), Buf()) for i in range(3)])
                sqr = Rot([(sbt(st, "a_sq%d" % i, [128, 512], BF16), Buf()) for i in range(3)])
                sig = Rot([(sbt(st, "a_sig%d" % i, [128, 512], F32), Buf()) for i in range(2)])
                ust = Rot([(sbt(st, "a_ust%d" % i, [128, 512], F32), Buf()) for i in range(2)])
                qr = sbt(st, "a_qr", [128, 8, 64], F32); b_qr = Buf()
                sq = sbt(st, "a_sqq", [128, 8, 64], F32); b_sq = Buf()
                red = sbt(st, "a_red", [128, 8], F32); b_red = Buf()
                qa = sbt(st, "a_qa", [128, 8, 65], BF16); b_qa = Buf()
                qTs = sbt(st, "a_qTs", [65, 8, 128], BF16); b_qTs = Buf()
                kbr = Rot([(sbt(st, "a_kb%d" % i, [128, 512], BF16), Buf()) for i in range(4)])
                kTr = Rot([(sbt(st, "a_kTs%d" % i, [128, 4, 128], BF16), Buf()) for i in range(2)])
                kpend = []
                vb = Rot([(sbt(st, "a_vb%d" % i, [128, 512], BF16), Buf()) for i in range(2)])
                rt = [(sbt(st, "a_rt%d" % i, [128, 256], F32), Buf()) for i in range(2)]
                kfirst = [True]
                blocks = [("e", 128 + c0, N, c0) for (c0, N) in EBLOCKS] + [("c", 0, NCTX, NE), ("h", 0, 256, None)]
                blocks += [("k", 512 * b, 512, 512 * b) for b in range(SEQ // 512)]
                csk = Rot([(sbt(st, "a_csk%d" % i, [128, 64], F32), Buf()) for i in range(2)])
                def do_block(kind, row0, N, col0):
                    ntile = N // 128
                    r = 1 if kind == "c" else 0
                    xTb, b_xTb = xTbs.next()
                    hT, b_hT = hTs.next()
                    for t in range(ntile):
                        xi = t % 2
                        if kind == "c":
                            src = ctx_in[t * 128:(t + 1) * 128, :]
                        elif kind == "h":
                            src = x_ext[0:128, :] if t == 0 else x_ext[NE + 128:NE + 256, :]
                        elif kind == "k":
                            src = x_all[row0 + t * 128:row0 + (t + 1) * 128, :]
                        else:
                            src = x_ext[row0 + t * 128:row0 + (t + 1) * 128, :]
                        P.dma("sync", lambda e, xi=xi, src=src: e.dma_start(out=xrow[xi][:], in_=src), writes=[b_xrow[xi]])
                        for g4 in range(4):
                            bk = prot.next()
                            for j in range(4):
                                fc = g4 * 4 + j
                                P.op("tensor", lambda e, fc=fc, xi=xi, j=j, bk=bk: e.transpose(ps[:, bk, j * 128:(j + 1) * 128], xrow[xi][:, fc * 128:(fc + 1) * 128], identf[:]),
                                     reads=[b_xrow[xi], b_identf], **wr(b_ps[bk], j == 0))
                            P.op("vector", lambda e, g4=g4, t=t, bk=bk: e.tensor_copy(xTb[:, g4 * 4:(g4 + 1) * 4, t * 128:(t + 1) * 128],
                                                                                       ps[:, bk, :].rearrange("p (j n) -> p j n", n=128)),
                                 reads=[b_ps[bk]], **wr(b_xTb, t == 0 and g4 == 0))
                    if kind in ("e", "c"):
                        P.dma("sync", lambda e, N=N, col0=col0: e.dma_start(out=xT0.ap()[:, :, col0:col0 + N].rearrange("f p n -> p f n"), in_=xTb[:, :, 0:N]),
                              reads=[b_xTb], pwrites=[B["xT0"]])
                    yield "T"
                    sandwich_in(xTb, b_xTb, N, 0, r, 0, hT, b_hT, rstd, b_rstd, tmpr, sqr)
                    yield "S"
                    def do_half(half):
                        sa, bwa = load_wpiece(W[0]["w_in"], 0, half * 512)
                        sg, bwg = load_wpiece(W[0]["w_in"], 0, 1024 + half * 512)
                        for oc in range(4):
                            i = half * 4 + oc
                            bkg = prot.next()
                            bka = prot.next()
                            for kc in range(KC):
                                P.op("tensor", lambda e, kc=kc, oc=oc, bkg=bkg: e.matmul(ps[:, bkg, 0:N], wpt[:, sg, kc, oc * 128:(oc + 1) * 128], hT[:, kc, 0:N],
                                                                                           start=(kc == 0), stop=(kc == KC - 1)),
                                     reads=[bwg, b_hT], **wr(b_ps[bkg], kc == 0))
                            for kc in range(KC):
                                P.op("tensor", lambda e, kc=kc, oc=oc, bka=bka: e.matmul(ps[:, bka, 0:N], wpt[:, sa, kc, oc * 128:(oc + 1) * 128], hT[:, kc, 0:N],
                                                                                           start=(kc == 0), stop=(kc == KC - 1)),
                                     reads=[bwa, b_hT], **wr(b_ps[bka], kc == 0))
                            si, bs_ = sig.next()
                            P.op("scalar", lambda e, bkg=bkg, si=si: e.activation(out=si[:, 0:N], in_=ps[:, bkg, 0:N], func=AF.Sigmoid),
                                 reads=[b_ps[bkg]], writes=[bs_])
                            ui, bu = ust.next()
                            P.op("vector", lambda e, bka=bka, si=si, ui=ui: e.tensor_tensor(ui[:, 0:N], ps[:, bka, 0:N], si[:, 0:N], ALU.mult),
                                 reads=[b_ps[bka], bs_], writes=[bu])
                            if kind == "e":
                                P.dma("sync", lambda e, i=i, ui=ui: e.dma_start(out=uT_e.ap()[i, :, 15 + col0:15 + col0 + N], in_=ui[:, 0:N]),
                                      reads=[bu], pwrites=[B["uT_e"]])
                            elif kind == "c":
                                P.dma("sync", lambda e, i=i, ui=ui: e.dma_start(out=uT_c.ap()[i, :, :], in_=ui[:, 0:N]), reads=[bu], pwrites=[B["uT_c"]])
                            else:
                                P.dma("sync", lambda e, i=i, ui=ui: e.dma_start(out=uT_e.ap()[i, :, 0:15], in_=ui[:, 113:128]), reads=[bu], pwrites=[B["uT_e"]])
                                P.dma("sync", lambda e, i=i, ui=ui: e.dma_start(out=uT_e.ap()[i, :, UW - 15:UW], in_=ui[:, 128:143]), reads=[bu], pwrites=[B["uT_e"]])
                    if kind != "k":
                        for half in range(2):
                            do_half(half)
                    if kind == "h":
                        return
                    for p in range(2 if kind != "k" else 0):
                        def evac_q(t, bk, p=p):
                            t_e = (col0 // 128 + t) if kind == "e" else 0
                            rope(bk, t_e, qr, b_qr, rt, kind == "e")
                            cc = col0 + t * 128
                            q_finish(qr, b_qr, sq, b_sq, red, b_red, qa, b_qa, 65, qTs, b_qTs,
                                     lambda: qT0.ap()[p * 8:(p + 1) * 8, :, cc:cc + 128].rearrange("h r n -> r h n"), B["qT0"])
                        linear_tm(W[0]["w_in"], 2048 + p * 512, hT, b_hT, ntile, evac_q)
                    for p in range(2 if kind != "e" else 0):
                        def evac_k(t, bk, p=p):
                            if kind == "k":
                                ci_, bci = csk.next()
                                g0 = row0 + t * 128
                                P.dma("sync", lambda e: e.dma_start(out=ci_[:], in_=cs_all[g0:g0 + 128, :]), writes=[bci])
                                rope(bk, 0, qr, b_qr, rt, True, cst=ci_, b_cst=bci)
                            else:
                                rope(bk, 0, qr, b_qr, rt, False)
                            sqnorm(qr, b_qr, sq, b_sq, red, b_red)
                            kmax_update(red, b_red, kfirst)
                            kb_, b_kb = kbr.next()
                            P.op("scalar", lambda e: e.activation(out=kb_[:], in_=qr[:].rearrange("p h d -> p (h d)"), func=AF.Identity), reads=[b_qr], writes=[b_kb])

                            def part2():
                                bk2 = prot.next()
                                kTs, b_kTs = kTr.next()
                                for hh in range(4):
                                    P.op("tensor", lambda e, hh=hh: e.transpose(psb(bk2)[:, hh * 128:(hh + 1) * 128], kb_[:, hh * 128:(hh + 1) * 128], identb[:]),
                                         reads=[b_kb, b_identb], **wr(b_ps[bk2], hh == 0))
                                P.op("vector", lambda e: e.tensor_copy(kTs[:].rearrange("q h n -> q (h n)"), psb(bk2)[:, 0:512]), reads=[b_ps[bk2]], writes=[b_kTs])
                                if kind == "c":
                                    P.dma("sync", lambda e: e.dma_start(out=kt_ctx.ap()[p * 4:(p + 1) * 4, :, t * 128:(t + 1) * 128].rearrange("h q n -> q h n"), in_=kTs[:]),
                                          reads=[b_kTs], pwrites=[B["kt_ctx"]])
                                else:
                                    oc0 = row0 + t * 128
                                    P.dma("sync", lambda e: e.dma_start(
                                        out=kt_all.ap().rearrange("(h q) n -> q h n", q=128)[:, p * 4:(p + 1) * 4, oc0:oc0 + 128], in_=kTs[:]),
                                        reads=[b_kTs], pwrites=[B["kt_all"]])
                            kpend.append(part2)
                            if len(kpend) > 2:
                                kpend.pop(0)()
                        linear_tm(W[0]["w_in"], 3072 + p * 512, hT, b_hT, ntile, evac_k)
                    while kpend:
                        kpend.pop(0)()
                    yield "K"
                    for p in range(2 if kind != "e" else 0):
                        def evac_v(t, bk, p=p):
                            vi, bv = vb.next()
                            P.op("scalar", lambda e: e.activation(out=vi[:], in_=ps[:, bk, :], func=AF.Identity), reads=[b_ps[bk]], writes=[bv])
                            if kind == "c":
                                P.dma("sync", lambda e: e.dma_start(out=vctx[:, t, p * 512:(p + 1) * 512], in_=vi[:]), reads=[bv], pwrites=[b_vctx])
                            else:
                                c = row0 // 128 + t
                                P.dma("sync", lambda e: e.dma_start(
                                    out=v_all.ap().rearrange("(h q) (c e) -> q h c e", q=128, e=128)[:, p * 4:(p + 1) * 4, c, :],
                                    in_=vi[:].rearrange("q (h e) -> q h e", e=128)),
                                    reads=[bv], pwrites=[B["v_all"]])
                        linear_tm(W[0]["w_in"], 4096 + p * 512, hT, b_hT, ntile, evac_v)
                    yield "V"
                for blk in blocks:
                    if blk[0] != "k":
                        for _ in do_block(*blk):
                            pass
                gens = [do_block(*blk) for blk in blocks if blk[0] == "k"]
                next(gens[0])
                next(gens[0])
                for i in range(len(gens)):
                    gn = gens[i + 1] if i + 1 < len(gens) else None
                    if gn is not None:
                        next(gn)
                    next(gens[i])
                    if gn is not None:
                        next(gn)
                    next(gens[i])
                kb_finish(0)

        vctx = sbt(top, "vctx", [128, 2, 1024], BF16); b_vctx = Buf()

        def l0_conv():
            with contextlib.ExitStack() as st:
                cw = sbt(st, "c_cw", [128, 8, 31], F32); b_cw = Buf()
                cm = sbt(st, "c_cm", [128, 3, 8], F32); b_cm = Buf()
                vT = sbt(st, "c_vT", [128, 8, TT], F32); b_vT = [Buf() for _ in range(8)]
                uS = [sbt(st, "c_uS%d" % i, [128, UW], F32) for i in range(2)]; b_uS = [Buf(), Buf()]
                uC = [sbt(st, "c_uC%d" % i, [128, NCTX + 30], F32) for i in range(2)]; b_uC = [Buf(), Buf()]
                mean = sbt(st, "c_mean", [128, 512], F32); b_mean = Buf()
                msq = sbt(st, "c_msq", [128, 512], F32); b_msq = Buf()
                rstd = sbt(st, "c_rstd", [128, 512], F32); b_rstd = Buf()
                tmpr = Rot([(sbt(st, "c_tmp%d" % i, [128, 512], F32), Buf()) for i in range(3)])
                vbr = Rot([(sbt(st, "c_vb%d" % i, [128, 512], BF16), Buf()) for i in range(3)])
                ast = Rot([(sbt(st, "c_ast%d" % i, [128, 512], BF16), Buf()) for i in range(2)])
                P.dma("sync", lambda e: e.dma_start(out=cw[:], in_=conv_wT), writes=[b_cw])
                P.dma("sync", lambda e: e.dma_start(out=cm[:], in_=conv_misc), writes=[b_cm])
                for i in range(2):
                    P.op("vector", lambda e, i=i: e.memset(uC[i][:], 0.0), writes=[b_uC[i]])
                for i in range(8):
                    eng = "vector"
                    s = i % 2
                    P.dma("sync", lambda e, i=i, s=s: e.dma_start(out=uS[s][:], in_=uT_e.ap()[i]), reads=[B["uT_e"]], writes=[b_uS[s]])
                    P.dma("sync", lambda e, i=i, s=s: e.dma_start(out=uC[s][:, 15:15 + NCTX], in_=uT_c.ap()[i]), reads=[B["uT_c"]], pwrites=[b_uC[s]])
                    P.op(eng, lambda e, s=s: e.tensor_scalar(uS[s][:, 0:15 + 128], uS[s][:, 0:15 + 128], flg[:, 0:1], None, ALU.mult),
                         reads=[b_uS[s], b_flg], writes=[b_uS[s]])
                    P.op(eng, lambda e, s=s: e.tensor_scalar(uS[s][:, UW - 15 - 128:UW], uS[s][:, UW - 15 - 128:UW], flg[:, 1:2], None, ALU.mult),
                         reads=[b_uS[s], b_flg], writes=[b_uS[s]])
                    for (src, bsrc, c0, n) in ((uS[s], b_uS[s], 0, NE), (uC[s], b_uC[s], NE, NCTX)):
                        dst = vT[:, i, c0:c0 + n]
                        P.op(eng, lambda e, src=src, dst=dst, i=i, n=n: e.tensor_scalar(dst, src[:, 0:n], cw[:, i, 0:1], cm[:, 0, i:i + 1], ALU.mult, ALU.add),
                             reads=[bsrc, b_cw, b_cm], writes=[b_vT[i]] if c0 == 0 else [], pwrites=[b_vT[i]] if c0 else [])
                        for j in range(1, 31):
                            P.op(eng, lambda e, src=src, dst=dst, i=i, n=n, j=j: e.scalar_tensor_tensor(dst, src[:, j:j + n], cw[:, i, j:j + 1], dst, ALU.mult, ALU.add),
                                 reads=[bsrc, b_cw], pwrites=[b_vT[i]])
                def do_ln(c0, N):
                    b1 = prot.next()
                    b2 = prot.next()
                    for i in range(8):
                        v1i, bv1 = vbr.next()
                        P.op("scalar", lambda e, i=i, v1i=v1i: e.activation(out=v1i[:, 0:N], in_=vT[:, i, c0:c0 + N], func=AF.Identity), reads=[b_vT[i]], writes=[bv1])
                        P.op("tensor", lambda e, i=i, v1i=v1i: e.matmul(ps[:, b1, 0:N], onesb[:], v1i[:, 0:N], start=(i == 0), stop=(i == 7)),
                             reads=[bv1, b_onesb], **wr(b_ps[b1], i == 0))
                        v2i, bv2 = vbr.next()
                        P.op("scalar", lambda e, i=i, v2i=v2i: e.activation(out=v2i[:, 0:N], in_=vT[:, i, c0:c0 + N], func=AF.Square), reads=[b_vT[i]], writes=[bv2])
                        P.op("tensor", lambda e, i=i, v2i=v2i: e.matmul(ps[:, b2, 0:N], onesb[:], v2i[:, 0:N], start=(i == 0), stop=(i == 7)),
                             reads=[bv2, b_onesb], **wr(b_ps[b2], i == 0))
                    P.op("scalar", lambda e: e.activation(out=mean[:, 0:N], in_=ps[:, b1, 0:N], func=AF.Identity, scale=1.0 / 1024), reads=[b_ps[b1]], writes=[b_mean])
                    P.op("vector", lambda e: e.tensor_tensor(msq[:, 0:N], mean[:, 0:N], mean[:, 0:N], ALU.mult), reads=[b_mean], writes=[b_msq])
                    P.op("vector", lambda e: e.scalar_tensor_tensor(msq[:, 0:N], ps[:, b2, 0:N], 1.0 / 1024, msq[:, 0:N], ALU.mult, ALU.subtract),
                         reads=[b_ps[b2], b_msq], writes=[b_msq])
                    P.op("scalar", lambda e: e.activation(out=rstd[:, 0:N], in_=msq[:, 0:N], func=AF.Sqrt, bias=LN_EPS), reads=[b_msq], writes=[b_rstd])
                    P.op("vector", lambda e: e.reciprocal(rstd[:, 0:N], rstd[:, 0:N]), reads=[b_rstd], writes=[b_rstd])
                    for i in range(8):
                        ti, bt = tmpr.next()
                        P.op("vector", lambda e, i=i, ti=ti: e.tensor_tensor(ti[:, 0:N], vT[:, i, c0:c0 + N], mean[:, 0:N], ALU.subtract), reads=[b_vT[i], b_mean], writes=[bt])
                        P.op("vector", lambda e, ti=ti: e.tensor_tensor(ti[:, 0:N], ti[:, 0:N], rstd[:, 0:N], ALU.mult), reads=[bt, b_rstd], writes=[bt])
                        ai, ba = ast.next()
                        P.op("scalar", lambda e, i=i, ti=ti, ai=ai: e.activation(out=ai[:, 0:N], in_=ti[:, 0:N], func=AF.Silu, scale=cm[:, 1, i:i + 1], bias=cm[:, 2, i:i + 1]),
                             reads=[bt, b_cm], writes=[ba])
                        P.dma("sync", lambda e, i=i, ai=ai: e.dma_start(out=mixT[0].ap()[i, :, c0:c0 + N], in_=ai[:, 0:N]), reads=[ba], pwrites=[B["mixT0"]])
                for (c0, N) in EBLOCKS + [(NE, NCTX)]:
                    do_ln(c0, N)

        def l0_attn():
            precast(0, "w_out", 512)
            precast(0, "w1", 128)
            precast(0, "w2", 512)
            precast(1, "w_in", 256)
            precast(1, "w_out", 512)
            precast(1, "w1", 128)
            precast(1, "w2", 512)
            with contextlib.ExitStack() as st:
                NKC = NCORES * NT
                KT = [sbt(st, "e_KT%d" % i, [65, SEQ + NCTX], BF16) for i in range(2)]; b_KT = [Buf(), Buf()]
                Vh = [sbt(st, "e_V0", [128, NKC, 128], BF16)]; b_Vh = [Buf()]
                qS = [sbt(st, "e_q%d" % i, [65, TT], BF16) for i in range(2)]; b_qS = [Buf(), Buf()]
                pT = Rot([(sbt(st, "e_pT%d" % i, [128, 2, 512], BF16), Buf()) for i in range(3)])
                o0 = sbt(st, "e_o0", [128, TT], F32); b_o0 = Buf()
                rs = sbt(st, "e_rs", [128, 512], F32); b_rs = Buf()
                od = sbt(st, "e_od", [128, 512], F32); b_od = Buf()
                tt_ = sbt(st, "e_tt", [128, 512], F32); b_tt = Buf()
                sqb = sbt(st, "e_sqb", [128, 512], BF16); b_sqb = Buf()
                rn = sbt(st, "e_rn", [128, 512], F32); b_rn = Buf()
                bl = Rot([(sbt(st, "e_bl%d" % i, [128, 512], BF16), Buf()) for i in range(2)])
                lamv = sbt(st, "e_lamv", [128, 4, 64], F32); b_lamv = Buf()
                lam = sbt(st, "e_lam", [128, 4], F32); b_lam = Buf()
                gsub = sbt(st, "e_gsub", [128, 1], F32); b_gsub = Buf()
                P.dma("sync", lambda e: e.dma_start(out=lamv[:].rearrange("p a d -> p (a d)"), in_=lam_in.rearrange("a d -> (a d)").partition_broadcast(128)), writes=[b_lamv])
                P.op("vector", lambda e: e.tensor_tensor(lamv[:, 0, :], lamv[:, 0, :], lamv[:, 1, :], ALU.mult), reads=[b_lamv], writes=[b_lamv])
                P.op("vector", lambda e: e.tensor_tensor(lamv[:, 2, :], lamv[:, 2, :], lamv[:, 3, :], ALU.mult), reads=[b_lamv], writes=[b_lamv])
                P.op("vector", lambda e: e.tensor_reduce(lam[:, 0:1], lamv[:, 0, :], AX.X, ALU.add), reads=[b_lamv], writes=[b_lam])
                P.op("vector", lambda e: e.tensor_reduce(lam[:, 1:2], lamv[:, 2, :], AX.X, ALU.add), reads=[b_lamv], writes=[b_lam])
                P.op("scalar", lambda e: e.activation(out=lam[:, 0:2], in_=lam[:, 0:2], func=AF.Exp), reads=[b_lam], writes=[b_lam])
                P.op("vector", lambda e: e.tensor_tensor(lam[:, 2:3], lam[:, 1:2], lam[:, 0:1], ALU.subtract), reads=[b_lam], writes=[b_lam])
                P.op("vector", lambda e: e.tensor_scalar(lam[:, 2:3], lam[:, 2:3], -LAM_INIT0, None, ALU.add), reads=[b_lam], writes=[b_lam])
                P.dma("sync", lambda e: e.dma_start(out=gsub[:], in_=subln), writes=[b_gsub])
                P.op("vector", lambda e: e.tensor_scalar(gsub[:], gsub[:], 1.0 - LAM_INIT0, None, ALU.mult), reads=[b_gsub], writes=[b_gsub])
                for i in range(2):
                    P.op("vector", lambda e, i=i: e.memset(KT[i][64:65, :], 1.0), writes=[b_KT[i]])
                    P.op("vector", lambda e, i=i: e.tensor_scalar(KT[i][64:65, :], KT[i][64:65, :], negkb[0][64:65, 0:1], None, ALU.mult),
                         reads=[b_KT[i], b_negkb[0]], writes=[b_KT[i]])
                qblocks = [(c0, N, True) for (c0, N) in EBLOCKS] + [(NE, NCTX, False)]
                for h in range(8):
                    vi = 0
                    P.dma("sync", lambda e, h=h, vi=vi: e.dma_start(
                        out=Vh[vi][:], in_=v_all.ap().rearrange("(h p) (c e) -> h p c e", p=128, e=128)[h]),
                        reads=[B["v_all"]], writes=[b_Vh[vi]])
                    for m in range(2):
                        hm = 2 * h + m
                        ki = hm % 2
                        P.dma("sync", lambda e, h=h, m=m, ki=ki: e.dma_start(
                            out=KT[ki][0:64, NCTX:], in_=kt_all.ap()[h * 128 + m * 64:h * 128 + (m + 1) * 64, :]),
                            reads=[B["kt_all"]], pwrites=[b_KT[ki]])
                        P.dma("sync", lambda e, h=h, m=m, ki=ki: e.dma_start(out=KT[ki][0:64, 0:NCTX], in_=kt_ctx.ap()[h, m * 64:(m + 1) * 64, :]),
                              reads=[B["kt_ctx"]], pwrites=[b_KT[ki]])
                        P.dma("sync", lambda e, hm=hm, ki=ki: e.dma_start(out=qS[ki][:], in_=qT0.ap()[hm]), reads=[B["qT0"]], writes=[b_qS[ki]])
                        def do_qblock(h, m, hm, ki, vi, c0, N, own):
                            bo, bs = 6, 7
                            nch = 2 + (NKC if own else 0)
                            npair = nch // 2
                            PAIRS = [(0, 1), (2, 3), (4, 5)]
                            LOOK = 2

                            def mm1(cp):
                                b0, b1 = PAIRS[cp % 3]
                                for j, bb in ((0, b0), (1, b1)):
                                    ci = 2 * cp + j
                                    P.op("tensor", lambda e, ci=ci, bb=bb: e.matmul(ps[:, bb, 0:N], KT[ki][:, ci * 128:(ci + 1) * 128], qS[ki][:, c0:c0 + N], start=True, stop=True),
                                         reads=[b_KT[ki], b_qS[ki]], writes=[b_ps[bb]])
                                return b0

                            def rest(cp, b0):
                                pi, bp = pT.next()
                                P.op("scalar", lambda e: e.activation(out=pi[:, :, 0:N], in_=ps[:, b0:b0 + 2, 0:N], func=AF.Exp),
                                     reads=[b_ps[b0], b_ps[b0 + 1]], writes=[bp])
                                return pi, bp

                            def mm23(cp, pi, bp):
                                for j in range(2):
                                    ci = 2 * cp + j
                                    if ci < 2:
                                        vl = vctx[:, ci, h * 128:(h + 1) * 128]
                                        vrd = b_vctx
                                    else:
                                        vl = Vh[vi][:, ci - 2, :]
                                        vrd = b_Vh[vi]
                                    P.op("tensor", lambda e, vl=vl, j=j, ci=ci: e.matmul(ps[:, bo, 0:N], vl, pi[:, j, 0:N], start=(ci == 0), stop=(ci == nch - 1)),
                                         reads=[vrd, bp], **wr(b_ps[bo], ci == 0))
                                    P.op("tensor", lambda e, j=j, ci=ci: e.matmul(ps[:, bs, 0:N], onesb[:], pi[:, j, 0:N], start=(ci == 0), stop=(ci == nch - 1)),
                                         reads=[bp, b_onesb], **wr(b_ps[bs], ci == 0))
                            pend = {}
                            for cp in range(min(LOOK, npair)):
                                pend[cp] = mm1(cp)
                            for cp in range(npair):
                                pi, bp = rest(cp, pend.pop(cp))
                                if cp + LOOK < npair:
                                    pend[cp + LOOK] = mm1(cp + LOOK)
                                mm23(cp, pi, bp)
                            P.op("vector", lambda e: e.reciprocal(rs[:, 0:N], ps[:, bs, 0:N]), reads=[b_ps[bs]], writes=[b_rs])
                            if m == 0:
                                P.op("vector", lambda e: e.tensor_tensor(o0[:, c0:c0 + N], ps[:, bo, 0:N], rs[:, 0:N], ALU.mult),
                                     reads=[b_ps[bo], b_rs], pwrites=[b_o0])
                            else:
                                P.op("vector", lambda e: e.tensor_tensor(tt_[:, 0:N], ps[:, bo, 0:N], rs[:, 0:N], ALU.mult), reads=[b_ps[bo], b_rs], writes=[b_tt])
                                P.op("vector", lambda e: e.scalar_tensor_tensor(od[:, 0:N], tt_[:, 0:N], lam[:, 2:3], o0[:, c0:c0 + N], ALU.mult, ALU.add),
                                     reads=[b_tt, b_lam, b_o0], writes=[b_od])
                                P.op("scalar", lambda e: e.activation(out=sqb[:, 0:N], in_=od[:, 0:N], func=AF.Square), reads=[b_od], writes=[b_sqb])
                                bn = 0
                                P.op("tensor", lambda e: e.matmul(ps[:, bn, 0:N], onesb[:], sqb[:, 0:N], start=True, stop=True), reads=[b_sqb, b_onesb], writes=[b_ps[bn]])
                                stats_finish(bn, N, rn, b_rn, 1.0 / 128, NORM_EPS)
                                P.op("vector", lambda e: e.tensor_tensor(od[:, 0:N], od[:, 0:N], rn[:, 0:N], ALU.mult), reads=[b_od, b_rn], writes=[b_od])
                                bi, bb = bl.next()
                                P.op("scalar", lambda e, bi=bi: e.activation(out=bi[:, 0:N], in_=od[:, 0:N], func=AF.Identity, scale=gsub[:, 0:1]), reads=[b_od, b_gsub], writes=[bb])
                                P.dma("sync", lambda e, bi=bi, h=h: e.dma_start(out=mixT[0].ap()[8 + h, :, c0:c0 + N], in_=bi[:, 0:N]), reads=[bb], pwrites=[B["mixT0"]])
                        for (c0, N, own) in qblocks:
                            do_qblock(h, m, hm, ki, vi, c0, N, own)

        def tail_phase(l, blocks, src, b_src, dst, b_dst, final):
            with contextlib.ExitStack() as st:
                alloc_wp(st, "t%d" % l, 2)
                xTb = sbt(st, "t_xTb", [128, KC, 512], F32); b_xTb = Buf()
                hid = sbt(st, "t_hid", [128, 64, 512], BF16); b_hid = Buf()
                mixb = hid; b_mixb = b_hid
                yst = sbt(st, "t_yst", [128, KC, 512], F32); b_yst = Buf()
                hT = yst[:].bitcast(BF16); b_hT = b_yst
                rstd = sbt(st, "t_rstd", [128, 512], F32); b_rstd = Buf()
                tmpr = Rot([(sbt(st, "t_tmp%d" % i, [128, 512], F32), Buf()) for i in range(3)])
                sqr = Rot([(sbt(st, "t_sq%d" % i, [128, 512], BF16), Buf()) for i in range(3)])
                orow = sbt(st, "t_orow", [128, D], F32) if final else None
                b_orow = Buf()
                for (r, c0, N, frow) in blocks:
                    P.dma("sync", lambda e, c0=c0, N=N: e.dma_start(out=xTb[:, :, 0:N], in_=src.ap()[:, :, c0:c0 + N].rearrange("f p n -> p f n")),
                          reads=[b_src], writes=[b_xTb])
                    P.dma("sync", lambda e, c0=c0, N=N: e.dma_start(out=mixb[:, 0:KC, 0:N], in_=mixT[l].ap()[:, :, c0:c0 + N].rearrange("f p n -> p f n")),
                          reads=[B["mixT%d" % l]], writes=[b_mixb])

                    def lin_post(wd, nkc, rhs_fn, rhs_bufs, kp, ig, N=N, r=r):
                        ssb = prot.take()

                        def evac(fc, pap, pbuf):
                            P.op("scalar", lambda e: e.activation(out=yst[:, fc, 0:N], in_=pap, func=AF.Identity), reads=[pbuf], **wr(b_yst, fc == 0))
                            si, bs_ = sqr.next()
                            P.op("scalar", lambda e: e.activation(out=si[:, 0:N], in_=pap, func=AF.Square), reads=[pbuf], writes=[bs_])
                            P.op("tensor", lambda e: e.matmul(ps[:, ssb, 0:N], onesb[:], si[:, 0:N], start=(fc == 0), stop=(fc == KC - 1)),
                                 reads=[bs_, b_onesb], **wr(b_ps[ssb], fc == 0))
                        linear_fm(wd, 0, 4, rhs_fn, rhs_bufs, nkc, N, evac, kp=kp)
                        stats_finish(ssb, N, rstd, b_rstd, 1.0 / D, NORM_EPS)
                        prot.release(ssb)
                        for fc in range(KC):
                            ti, bt = tmpr.next()
                            P.op("vector", lambda e, fc=fc, ti=ti: e.tensor_tensor(ti[:, 0:N], yst[:, fc, 0:N], rstd[:, 0:N], ALU.mult),
                                 reads=[b_yst, b_rstd], writes=[bt])
                            P.op("vector", lambda e, fc=fc, ti=ti: e.scalar_tensor_tensor(xTb[:, fc, 0:N], ti[:, 0:N], tab[:, l, r, ig, fc:fc + 1],
                                                                                           xTb[:, fc, 0:N], ALU.mult, ALU.add),
                                 reads=[bt, b_tab, b_xTb], writes=[b_xTb])

                    lin_post(wsrc(l, "w_out"), KC, lambda kc, N=N: mixb[:, kc, 0:N], [b_mixb], 1, 2)
                    sandwich_in(xTb, b_xTb, N, l, r, 3, hT, b_hT, rstd, b_rstd, tmpr, sqr)

                    def evac_h(hc, pap, pbuf, N=N):
                        ti, bt = tmpr.next()
                        P.op("scalar", lambda e: e.activation(out=ti[:, 0:N], in_=pap, func=AF.Relu), reads=[pbuf], writes=[bt])
                        P.op("vector", lambda e: e.tensor_tensor(hid[:, hc, 0:N], ti[:, 0:N], ti[:, 0:N], ALU.mult), reads=[bt], **wr(b_hid, hc == 0))
                    linear_fm(wsrc(l, "w1"), 0, 16, lambda kc, N=N: hT[:, kc, 0:N], [b_hT], KC, N, evac_h)
                    lin_post(wsrc(l, "w2"), 64, lambda kc, N=N: hid[:, kc, 0:N], [b_hid], 4, 5)
                    if dst is not None:
                        P.dma("sync", lambda e, c0=c0, N=N: e.dma_start(out=dst.ap()[:, :, c0:c0 + N].rearrange("f p n -> p f n"), in_=xTb[:, :, 0:N]),
                              reads=[b_xTb], pwrites=[b_dst])
                    if final:
                        for t in range(N // 128):
                            for g4 in range(4):
                                bk = prot.next()
                                for j in range(4):
                                    fc = g4 * 4 + j
                                    P.op("tensor", lambda e, fc=fc, t=t, j=j, bk=bk: e.transpose(ps[:, bk, j * 128:(j + 1) * 128], xTb[:, fc, t * 128:(t + 1) * 128], identf[:]),
                                         reads=[b_xTb, b_identf], **wr(b_ps[bk], j == 0))
                                P.op("vector", lambda e, g4=g4, bk=bk: e.tensor_copy(orow[:, g4 * 512:(g4 + 1) * 512], ps[:, bk, :]),
                                     reads=[b_ps[bk]], **wr(b_orow, g4 == 0))
                            r0 = frow + t * 128
                            out_toks.append(P.dma("sync", lambda e, r0=r0: e.dma_start(out=out_ext[r0:r0 + 128, :], in_=orow[:]), reads=[b_orow]))

        def l1_inproj():
            with contextlib.ExitStack() as st:
                alloc_wp(st, "b")
                wpt = WP["t"]
                xTb = sbt(st, "b_xTb", [128, KC, 512], F32); b_xTb = Buf()
                hT = sbt(st, "b_hT", [128, KC, 512], BF16); b_hT = Buf()
                rstd = sbt(st, "b_rstd", [128, 512], F32); b_rstd = Buf()
                tmpr = Rot([(sbt(st, "b_tmp%d" % i, [128, 512], F32), Buf()) for i in range(3)])
                sqr = Rot([(sbt(st, "b_sq%d" % i, [128, 512], BF16), Buf()) for i in range(3)])
                csb = Rot([(sbt(st, "b_cs%d" % i, [128, 512], F32), Buf()) for i in range(2)])
                stg = Rot([(sbt(st, "b_stg%d" % i, [128, 512], F32), Buf()) for i in range(3)])
                qr = sbt(st, "b_qr", [128, 8, 64], F32); b_qr = Buf()
                sq = sbt(st, "b_sqq", [128, 8, 64], F32); b_sq = Buf()
                red = sbt(st, "b_red", [128, 8], F32); b_red = Buf()
                qa = sbt(st, "b_qa", [128, 8, 66], BF16); b_qa = Buf()
                qTs = sbt(st, "b_qTs", [66, 8, 128], BF16); b_qTs = Buf()
                kr = sbt(st, "b_kr", [128, 4, 64], F32); b_kr = Buf()
                ksq = sbt(st, "b_ksq", [128, 4, 64], F32); b_ksq = Buf()
                kred = sbt(st, "b_kred", [128, 8], F32); b_kred = Buf()
                kb_ = sbt(st, "b_kb", [128, 256], BF16); b_kb = Buf()
                kTs = sbt(st, "b_kTs", [128, 2, 128], BF16); b_kTs = Buf()
                vb = Rot([(sbt(st, "b_vb%d" % i, [128, 256], BF16), Buf()) for i in range(2)])
                rt = [(sbt(st, "b_rt%d" % i, [128, 256], F32), Buf()) for i in range(2)]
                P.op("vector", lambda e: e.memset(qa[:, :, 65:66], 1.0), writes=[b_qa])
                P.op("vector", lambda e: e.memset(kred[:], 0.0), writes=[b_kred])
                kfirst = [True]
                blocks = [("e", c0, N) for (c0, N) in EBLOCKS] + [("c", NE, NCTX)]
                def do_block(kind, col0, N):
                    ntile = N // 128
                    r = 1 if kind == "c" else 0
                    P.dma("sync", lambda e, col0=col0, N=N: e.dma_start(out=xTb[:, :, 0:N], in_=xT1.ap()[:, :, col0:col0 + N].rearrange("f p n -> p f n")),
                          reads=[B["xT1"]], writes=[b_xTb])
                    sandwich_in(xTb, b_xTb, N, 1, r, 0, hT, b_hT, rstd, b_rstd, tmpr, sqr)
                    if kind == "e":
                        def do_half(half):
                            sb_, bwb = load_wpiece(wsrc(1, "w_in"), 0, half * 512)
                            sc_, bwc = load_wpiece(wsrc(1, "w_in"), 0, 1024 + half * 512)
                            sx_, bwx = load_wpiece(wsrc(1, "w_in"), 0, 2048 + half * 512)
                            for oc in range(4):
                                i = half * 4 + oc
                                bks = []
                                for (sl, bw) in ((sb_, bwb), (sc_, bwc), (sx_, bwx)):
                                    bk = prot.next()
                                    bks.append(bk)
                                    for kc in range(KC):
                                        P.op("tensor", lambda e, kc=kc, oc=oc, bk=bk, sl=sl: e.matmul(ps[:, bk, 0:N], wpt[:, sl, kc, oc * 128:(oc + 1) * 128], hT[:, kc, 0:N],
                                                                                                   start=(kc == 0), stop=(kc == KC - 1)),
                                             reads=[bw, b_hT], **wr(b_ps[bk], kc == 0))
                                ci, bc = csb.next()
                                P.op("scalar", lambda e, ci=ci, bk=bks[1]: e.activation(out=ci[:, 0:N], in_=ps[:, bk, 0:N], func=AF.Identity), reads=[b_ps[bks[1]]], writes=[bc])
                                s1, bs1 = stg.next()
                                P.op("vector", lambda e, ci=ci, s1=s1, bk=bks[2]: e.tensor_tensor(s1[:, 0:N], ps[:, bk, 0:N], ci[:, 0:N], ALU.mult), reads=[b_ps[bks[2]], bc], writes=[bs1])
                                P.dma("sync", lambda e, i=i, s1=s1: e.dma_start(out=cxT.ap()[i, :, col0:col0 + N], in_=s1[:, 0:N]), reads=[bs1], pwrites=[B["cxT"]])
                                s2, bs2 = stg.next()
                                P.op("scalar", lambda e, s2=s2, bk=bks[0]: e.activation(out=s2[:, 0:N], in_=ps[:, bk, 0:N], func=AF.Identity), reads=[b_ps[bks[0]]], writes=[bs2])
                                P.dma("sync", lambda e, i=i, s2=s2: e.dma_start(out=bT.ap()[i, :, col0:col0 + N], in_=s2[:, 0:N]), reads=[bs2], pwrites=[B["bT"]])
                        for half in range(2):
                            do_half(half)
                        for p in range(2):
                            def evac_q(t, bk, p=p):
                                t_e = col0 // 128 + t
                                rope(bk, t_e, qr, b_qr, rt, True)
                                cc = col0 + t * 128
                                q_finish(qr, b_qr, sq, b_sq, red, b_red, qa, b_qa, 66, qTs, b_qTs,
                                         lambda: qT1.ap()[p * 8:(p + 1) * 8, :, cc:cc + 128].rearrange("h r n -> r h n"), B["qT1"])
                            linear_tm(wsrc(1, "w_in"), 3072 + p * 512, hT, b_hT, ntile, evac_q)
                    def evac_kv(t, bk):
                        cc = col0 + t * 128
                        if kind == "e":
                            t_e = col0 // 128 + t
                            pv = ps[:, bk, 0:256].rearrange("p (h two d) -> p h two d", two=2, d=32)
                            cosb = cs[:, t_e, 0:32].unsqueeze(1).broadcast_to([128, 4, 32])
                            sinb = cs[:, t_e, 32:64].unsqueeze(1).broadcast_to([128, 4, 32])
                            (t1, bt1), (t2, bt2) = rt
                            t1v = t1[:, 0:128].rearrange("p (h d) -> p h d", d=32)
                            t2v = t2[:, 0:128].rearrange("p (h d) -> p h d", d=32)
                            x1 = pv[:, :, 0, :]
                            x2 = pv[:, :, 1, :]
                            P.op("vector", lambda e: e.tensor_tensor(t1v, x1, cosb, ALU.mult), reads=[b_ps[bk], b_cs], writes=[bt1])
                            P.op("vector", lambda e: e.tensor_tensor(t2v, x2, sinb, ALU.mult), reads=[b_ps[bk], b_cs], writes=[bt2])
                            P.op("vector", lambda e: e.tensor_tensor(kr[:, :, 0:32], t1v, t2v, ALU.subtract), reads=[bt1, bt2], writes=[b_kr])
                            P.op("vector", lambda e: e.tensor_tensor(t1v, x2, cosb, ALU.mult), reads=[b_ps[bk], b_cs], writes=[bt1])
                            P.op("vector", lambda e: e.tensor_tensor(t2v, x1, sinb, ALU.mult), reads=[b_ps[bk], b_cs], writes=[bt2])
                            P.op("vector", lambda e: e.tensor_tensor(kr[:, :, 32:64], t1v, t2v, ALU.add), reads=[bt1, bt2], pwrites=[b_kr])
                        else:
                            P.op("vector", lambda e: e.tensor_copy(kr[:].rearrange("p h d -> p (h d)"), ps[:, bk, 0:256]), reads=[b_ps[bk]], writes=[b_kr])
                        P.op("vector", lambda e: e.tensor_tensor(ksq[:], kr[:], kr[:], ALU.mult), reads=[b_kr], writes=[b_ksq])
                        P.op("vector", lambda e: e.tensor_reduce(kred[:, 0:4], ksq[:], AX.X, ALU.add), reads=[b_ksq], writes=[b_kred])
                        kmax_update(kred, b_kred, kfirst)
                        P.op("scalar", lambda e: e.activation(out=kb_[:], in_=kr[:].rearrange("p h d -> p (h d)"), func=AF.Identity), reads=[b_kr], writes=[b_kb])
                        bk2 = prot.next()
                        for hh in range(2):
                            P.op("tensor", lambda e, hh=hh: e.transpose(psb(bk2)[:, hh * 128:(hh + 1) * 128], kb_[:, hh * 128:(hh + 1) * 128], identb[:]),
                                 reads=[b_kb, b_identb], **wr(b_ps[bk2], hh == 0))
                        P.op("vector", lambda e: e.tensor_copy(kTs[:].rearrange("q h n -> q (h n)"), psb(bk2)[:, 0:256]), reads=[b_ps[bk2]], writes=[b_kTs])
                        P.dma("sync", lambda e: e.dma_start(out=kt1.ap().rearrange("(h q) n -> q h n", q=128)[:, :, cc:cc + 128], in_=kTs[:]),
                              reads=[b_kTs], pwrites=[B["kt1"]])
                        vi, bv = vb.next()
                        P.op("scalar", lambda e: e.activation(out=vi[:], in_=ps[:, bk, 256:512], func=AF.Identity), reads=[b_ps[bk]], writes=[bv])
                        P.dma("sync", lambda e: e.dma_start(out=v1.ap()[cc:cc + 128, :], in_=vi[:]), reads=[bv], pwrites=[B["v1"]])
                    linear_tm(wsrc(1, "w_in"), 4096, hT, b_hT, ntile, evac_kv)
                for blk in blocks:
                    do_block(*blk)
                kb_finish(1)

        def l1_conv():
            with contextlib.ExitStack() as st:
                sw = sbt(st, "d_sw", [128, 8, 3], F32); b_sw = Buf()
                cxS = [sbt(st, "d_cx%d" % i, [128, NE], F32) for i in range(2)]; b_cx = [Buf(), Buf()]
                bS = [sbt(st, "d_b%d" % i, [128, NE], F32) for i in range(2)]; b_b = [Buf(), Buf()]
                acc = [sbt(st, "d_acc%d" % i, [128, TO], F32) for i in range(2)]; b_acc = [Buf(), Buf()]
                cl = [sbt(st, "d_cl%d" % i, [128, TO], BF16) for i in range(2)]; b_cl = [Buf(), Buf()]
                P.dma("sync", lambda e: e.dma_start(out=sw[:], in_=sconv_wT), writes=[b_sw])
                for i in range(8):
                    s = i % 2
                    eng = "vector"
                    P.dma("sync", lambda e, i=i, s=s: e.dma_start(out=cxS[s][:], in_=cxT.ap()[i]), reads=[B["cxT"]], writes=[b_cx[s]])
                    P.dma("sync", lambda e, i=i, s=s: e.dma_start(out=bS[s][:], in_=bT.ap()[i]), reads=[B["bT"]], writes=[b_b[s]])
                    P.op(eng, lambda e, s=s: e.tensor_scalar(cxS[s][:, 0:128], cxS[s][:, 0:128], flg[:, 0:1], None, ALU.mult), reads=[b_cx[s], b_flg], writes=[b_cx[s]])
                    P.op(eng, lambda e, s=s: e.tensor_scalar(cxS[s][:, NE - 128:NE], cxS[s][:, NE - 128:NE], flg[:, 1:2], None, ALU.mult), reads=[b_cx[s], b_flg], writes=[b_cx[s]])
                    P.op(eng, lambda e, s=s, i=i: e.tensor_scalar(acc[s][:], cxS[s][:, 127:127 + TO], sw[:, i, 0:1], None, ALU.mult), reads=[b_cx[s], b_sw], writes=[b_acc[s]])
                    for j in (1, 2):
                        P.op(eng, lambda e, s=s, i=i, j=j: e.scalar_tensor_tensor(acc[s][:], cxS[s][:, 127 + j:127 + j + TO], sw[:, i, j:j + 1], acc[s][:], ALU.mult, ALU.add),
                             reads=[b_cx[s], b_sw, b_acc[s]], writes=[b_acc[s]])
                    P.op(eng, lambda e, s=s: e.tensor_tensor(cl[s][:], acc[s][:], bS[s][:, 128:128 + TO], ALU.mult), reads=[b_acc[s], b_b[s]], writes=[b_cl[s]])
                    P.dma("sync", lambda e, i=i, s=s: e.dma_start(out=mixT[1].ap()[i, :, 128:128 + TO], in_=cl[s][:]), reads=[b_cl[s]], pwrites=[B["mixT1"]])

        def l1_attn():
            with contextlib.ExitStack() as st:
                KT = sbt(st, "f_KT", [66, 4, TT], BF16); b_KT = Buf()
                V = sbt(st, "f_V", [128, TT // 128, 256], BF16); b_V = Buf()
                qS = [sbt(st, "f_q%d" % i, [66, NE], BF16) for i in range(2)]; b_qS = [Buf(), Buf()]
                kfk = sbt(st, "f_kfk", [66, 16], BF16); b_kfk = Buf()
                msk = sbt(st, "f_msk", [128, 4, 128], BF16); b_msk = Buf()
                mskf = sbt(st, "f_mskf", [128, 2, 128], F32); b_mskf = Buf()
                pA = Rot([(sbt(st, "f_pA%d" % i, [128, 640], BF16), Buf()) for i in range(3)])
                pk = Rot([(sbt(st, "f_pk%d" % i, [1, 128], BF16), Buf()) for i in range(2)])
                rs = Rot([(sbt(st, "f_rs%d" % i, [64, 128], F32), Buf()) for i in range(2)])
                ost = [sbt(st, "f_ost%d" % i, [64, TO], BF16) for i in range(2)]; b_ost = [Buf(), Buf()]
                P.dma("sync", lambda e: e.dma_start(out=mskf[:], in_=masks_in), writes=[b_mskf])
                P.op("vector", lambda e: e.tensor_copy(msk[:, 0:2, :], mskf[:]), reads=[b_mskf], writes=[b_msk])
                P.op("vector", lambda e: e.tensor_scalar(msk[:, 2, :], mskf[:, 0, :], flg[:, 0:1], None, ALU.mult), reads=[b_mskf, b_flg], pwrites=[b_msk])
                P.op("vector", lambda e: e.tensor_scalar(msk[:, 3, :], mskf[:, 1, :], flg[:, 1:2], None, ALU.mult), reads=[b_mskf, b_flg], pwrites=[b_msk])
                P.op("vector", lambda e: e.memset(KT[64:66, :, :], 0.0), writes=[b_KT])
                P.op("vector", lambda e: e.memset(KT[64:65, :, :], 1.0), reads=[], writes=[b_KT])
                P.op("vector", lambda e: e.tensor_scalar(KT[64:65, :, :], KT[64:65, :, :], negkb[1][64:65, 0:1], None, ALU.mult), reads=[b_negkb[1], b_KT], writes=[b_KT])
                for kvh in range(4):
                    P.dma("sync", lambda e, kvh=kvh: e.dma_start(out=KT[0:64, kvh, :], in_=kt1.ap()[kvh * 64:(kvh + 1) * 64, :]), reads=[B["kt1"]], pwrites=[b_KT])
                P.dma("sync", lambda e: e.dma_start(out=V[:], in_=v1.ap().rearrange("(c p) e -> p c e", p=128)), reads=[B["v1"]], writes=[b_V])
                P.op("vector", lambda e: e.memset(kfk[:], 0.0), writes=[b_kfk])
                P.op("vector", lambda e: e.memset(kfk[64:65, :], 1.0), writes=[b_kfk])
                P.op("vector", lambda e: e.tensor_scalar(kfk[64:65, :], kfk[64:65, :], negkb[1][64:65, 0:1], None, ALU.mult), reads=[b_negkb[1], b_kfk], writes=[b_kfk])
                P.dma("gpsimd", lambda e: e.dma_start(out=kfk[65:66, :], in_=sink_in), reads=[], writes=[], pwrites=[b_kfk])
                NCH_CTX = NE // 128
                for head in range(16):
                    kvh = head // 4
                    qi = head % 2
                    P.dma("sync", lambda e, head=head, qi=qi: e.dma_start(out=qS[qi][:], in_=qT1.ap()[head]), reads=[B["qT1"]], writes=[b_qS[qi]])
                    def do_tile(head, kvh, qi, n):
                        qcol = qS[qi][:, n * 128:(n + 1) * 128]
                        bA = prot.next()
                        bB = prot.next()
                        chunks = [NCH_CTX, NCH_CTX + 1, n - 1, n, n + 1]
                        for ci, ch in enumerate(chunks):
                            dstp = ps[:, bA, ci * 128:(ci + 1) * 128] if ci < 4 else ps[:, bB, 0:128]
                            bkk = bA if ci < 4 else bB
                            P.op("tensor", lambda e, ch=ch, dstp=dstp, kvh=kvh: e.matmul(dstp, KT[:, kvh, ch * 128:(ch + 1) * 128], qcol, start=True, stop=True),
                                 reads=[b_KT, b_qS[qi]], **wr(b_ps[bkk], ci == 0 or ci == 4))
                        P.op("tensor", lambda e, head=head: e.matmul(ps[0:1, bB, 128:256], kfk[:, head:head + 1], qcol, start=True, stop=True),
                             reads=[b_kfk, b_qS[qi]], pwrites=[b_ps[bB]])
                        pa, bpa = pA.next()
                        P.op("scalar", lambda e, pa=pa: e.activation(out=pa[:, 0:512], in_=ps[:, bA, :], func=AF.Exp), reads=[b_ps[bA]], writes=[bpa])
                        P.op("scalar", lambda e, pa=pa: e.activation(out=pa[:, 512:640], in_=ps[:, bB, 0:128], func=AF.Exp), reads=[b_ps[bB]], pwrites=[bpa])
                        pki, bpk = pk.next()
                        P.op("scalar", lambda e, pki=pki: e.activation(out=pki[:], in_=ps[0:1, bB, 128:256], func=AF.Exp), reads=[b_ps[bB]], writes=[bpk])
                        mp = 2 if n == 1 else 0
                        mn = 3 if n == NT else 1
                        P.op("vector", lambda e, pa=pa, mp=mp: e.tensor_tensor(pa[:, 256:384], pa[:, 256:384], msk[:, mp, :], ALU.mult), reads=[bpa, b_msk], writes=[bpa])
                        P.op("vector", lambda e, pa=pa, mn=mn: e.tensor_tensor(pa[:, 512:640], pa[:, 512:640], msk[:, mn, :], ALU.mult), reads=[bpa, b_msk], writes=[bpa])
                        bo = prot.next()
                        bs = prot.next()
                        for ci, ch in enumerate(chunks):
                            P.op("tensor", lambda e, ci=ci, ch=ch, pa=pa, kvh=kvh: e.matmul(ps[0:64, bo, 0:128], V[:, ch, kvh * 64:(kvh + 1) * 64], pa[:, ci * 128:(ci + 1) * 128],
                                                                                              start=(ci == 0), stop=(ci == 4)),
                                 reads=[b_V, bpa], **wr(b_ps[bo], ci == 0))
                            P.op("tensor", lambda e, ci=ci, pa=pa: e.matmul(ps[0:64, bs, 0:128], onesb[:, 0:64], pa[:, ci * 128:(ci + 1) * 128], start=(ci == 0), stop=False),
                                 reads=[b_onesb, bpa], **wr(b_ps[bs], ci == 0))
                        P.op("tensor", lambda e, pki=pki: e.matmul(ps[0:64, bs, 0:128], onesb[0:1, 0:64], pki[:], start=False, stop=True),
                             reads=[b_onesb, bpk], pwrites=[b_ps[bs]])
                        ri, br = rs.next()
                        P.op("vector", lambda e, ri=ri: e.reciprocal(ri[:], ps[0:64, bs, 0:128]), reads=[b_ps[bs]], writes=[br])
                        P.op("vector", lambda e, ri=ri, n=n: e.tensor_tensor(ost[qi][:, (n - 1) * 128:n * 128], ps[0:64, bo, 0:128], ri[:], ALU.mult),
                             reads=[b_ps[bo], br], **wr(b_ost[qi], n == 1))
                    for n in range(1, NT + 1):
                        do_tile(head, kvh, qi, n)
                    P.dma("sync", lambda e, head=head, qi=qi: e.dma_start(out=mixT[1].ap()[8 + head // 2, (head % 2) * 64:(head % 2) * 64 + 64, 128:128 + TO], in_=ost[qi][:]),
                          reads=[b_ost[qi]], pwrites=[B["mixT1"]])

        if 'adaln' in phases:
            adaln_phase()
            P.barrier()
        if 'l0in' in phases:
            l0_inproj()
            P.barrier()
        if 'l0conv' in phases:
            l0_conv()
            P.barrier()
        if 'l0attn' in phases:
            l0_attn()
            P.barrier()
        if 'tail0' in phases:
            tail_phase(0, [(0, c0, N, None) for (c0, N) in EBLOCKS] + [(1, NE, NCTX, None)], xT0, B["xT0"], xT1, B["xT1"], False)
            P.barrier()
        if 'l1in' in phases:
            l1_inproj()
            P.barrier()
        if 'l1conv' in phases:
            l1_conv()
            P.barrier()
        if 'l1attn' in phases:
            l1_attn()
            P.barrier()
        if 'tail1' in phases:
            tail_phase(1, [(0, 128 + 512 * b, 512, 512 * b) for b in range(4)], xT1, B["xT1"], None, None, True)
        scr = dict(xT0=xT0, xT1=xT1, uT_e=uT_e, uT_c=uT_c, mixT0=mixT[0], mixT1=mixT[1], qT0=qT0, kt_all=kt_all,
                   v_all=v_all, kt_ctx=kt_ctx, cxT=cxT, bT=bT, qT1=qT1, kt1=kt1, v1=v1)
        for nm in dbg:
            if nm == 'tab':
                o = nc.dram_tensor("dbg_tab", [128, 2 * 2 * 6 * KC], F32, kind="ExternalOutput").ap()
                out_toks.append(P.dma("sync", lambda e, o=o: e.dma_start(out=o, in_=tab[:].rearrange("p a b c d -> p (a b c d)")), reads=[b_tab]))
            elif nm == 'negkb':
                for l in range(2):
                    o = nc.dram_tensor("dbg_negkb%d" % l, [128, 1], F32, kind="ExternalOutput").ap()
                    out_toks.append(P.dma("sync", lambda e, o=o, l=l: e.dma_start(out=o, in_=negkb[l][:]), reads=[b_negkb[l]]))
            else:
                t = scr[nm]
                o = nc.dram_tensor("dbg_" + nm, list(t.shape), t.dtype, kind="ExternalOutput").ap()
                out_toks.append(P.dma("sync", lambda e, o=o, t=t: e.dma_start(out=o, in_=t.ap()), reads=[B[nm]]))
        P.finish_wait("sync", out_toks)
        P.emit(top)
    return nc, declared


def _bind_tab(tab, modT, gn, l, r, i):
    def T():
        return tab[:, l, r, i, :]

    def M(v):
        return modT[:, v * KC:(v + 1) * KC, r]
    if i == 0:
        return lambda e: e.scalar_tensor_tensor(T(), M(1), 1.0, gn[:, 0, :], ALU.add, ALU.mult)
    if i == 1:
        return lambda e: e.tensor_copy(T(), M(0))
    if i == 2:
        return lambda e: e.tensor_tensor(T(), M(2), gn[:, 1, :], ALU.mult)
    if i == 3:
        return lambda e: e.scalar_tensor_tensor(T(), M(4), 1.0, gn[:, 2, :], ALU.add, ALU.mult)
    if i == 4:
        return lambda e: e.tensor_copy(T(), M(3))
    return lambda e: e.tensor_tensor(T(), M(5), gn[:, 3, :], ALU.mult)


_NC_CACHE = {}
_ALL_INPUTS = ("x", "c", "ctx", "c_ctx",
               "l0_mod_w", "l0_mod_b", "l0_norm_mix_pre", "l0_norm_mix_post", "l0_norm_mlp_pre", "l0_norm_mlp_post",
               "l0_w_in", "l0_conv_w", "l0_conv_b", "l0_ln_g", "l0_ln_b", "l0_lambda_q1", "l0_lambda_k1", "l0_lambda_q2",
               "l0_lambda_k2", "l0_subln_g", "l0_w_out", "l0_mlp_w1", "l0_mlp_w2",
               "l1_mod_w", "l1_mod_b", "l1_norm_mix_pre", "l1_norm_mix_post", "l1_norm_mlp_pre", "l1_norm_mlp_post",
               "l1_w_in", "l1_sconv_w", "l1_sink", "l1_w_out", "l1_mlp_w1", "l1_mlp_w2")


def _fm(v, nchunk):
    return np.ascontiguousarray(np.asarray(v, np.float32).reshape(nchunk, 128).T)


def prep_inputs(inp):
    f = lambda k: np.asarray(inp[k], np.float32)
    x = f("x")[0]
    ctx = f("ctx")[0]
    inv = np.power(10000.0, -np.arange(16, dtype=np.float32) / 16).astype(np.float32)
    ident = np.eye(128, dtype=np.float32)
    jj = np.arange(128)[:, None]
    ii = np.arange(128)[None, :]
    masks = np.stack([(jj >= ii), (jj <= ii)], axis=1).astype(np.float32)
    shared = dict(ctx=np.ascontiguousarray(ctx), ident=ident, masks=np.ascontiguousarray(masks), x_all=np.ascontiguousarray(x))
    pa = np.arange(SEQ)
    anga = np.concatenate([(pa // 64).astype(np.float32)[:, None] * inv, (pa % 64).astype(np.float32)[:, None] * inv], axis=-1).astype(np.float32)
    shared["cs_all"] = np.concatenate([np.cos(anga), np.sin(anga)], axis=-1).astype(np.float32)
    cv = np.stack([f("c")[0], f("c_ctx")], axis=-1)
    shared["cvT"] = np.ascontiguousarray(cv.reshape(KC, 128, 2).transpose(1, 0, 2))
    for l in range(2):
        pre = "l%d_" % l
        shared[pre + "mod_w"] = f(pre + "mod_w")
        shared[pre + "mod_bT"] = _fm(f(pre + "mod_b"), 96)
        shared[pre + "gains"] = np.ascontiguousarray(np.stack(
            [_fm(f(pre + k), KC) for k in ("norm_mix_pre", "norm_mix_post", "norm_mlp_pre", "norm_mlp_post")], axis=1))
        for k in ("w_in", "w_out", "mlp_w1", "mlp_w2"):
            shared[pre + k] = f(pre + k)
    shared["l0_conv_wT"] = np.ascontiguousarray(f("l0_conv_w").T.reshape(8, 128, 31).transpose(1, 0, 2))
    shared["l0_conv_misc"] = np.ascontiguousarray(np.stack([_fm(f("l0_" + k), 8) for k in ("conv_b", "ln_g", "ln_b")], axis=1))
    shared["l0_lam"] = np.stack([f("l0_lambda_q1"), f("l0_lambda_k1"), f("l0_lambda_q2"), f("l0_lambda_k2")], axis=0)
    shared["l0_subln"] = f("l0_subln_g").reshape(128, 1)
    shared["l1_sconv_wT"] = np.ascontiguousarray(f("l1_sconv_w").T.reshape(8, 128, 3).transpose(1, 0, 2))
    shared["l1_sink"] = f("l1_sink").reshape(1, 16)
    in_maps = []
    for r in range(NCORES):
        s = r * TO
        xe = np.zeros((NE + 256, D), np.float32)
        lo = s - 256
        hi = s + TO + 256
        a = max(lo, 0)
        b = min(hi, SEQ)
        xe[a - lo:b - lo] = x[a:b]
        pos = np.arange(s - 128, s + TO + 128)
        pos = np.clip(pos, 0, SEQ - 1)
        row = (pos // 64).astype(np.float32)
        col = (pos % 64).astype(np.float32)
        ang = np.concatenate([row[:, None] * inv, col[:, None] * inv], axis=-1).astype(np.float32)
        cs = np.concatenate([np.cos(ang), np.sin(ang)], axis=-1).astype(np.float32)
        fl = np.zeros((128, 2), np.float32)
        fl[:, 0] = 1.0 if r > 0 else 0.0
        fl[:, 1] = 1.0 if r < NCORES - 1 else 0.0
        m = dict(shared)
        m["x_ext"] = xe
        m["cs_ext"] = cs
        m["flags"] = fl
        in_maps.append(m)
    return in_maps


def kernel(**inp):
    if "nc" not in _NC_CACHE:
        _NC_CACHE["nc"] = build()
    nc, declared = _NC_CACHE["nc"]
    in_maps = prep_inputs(inp)
    in_maps = [{k: v for k, v in m.items() if k in declared} for m in in_maps]
    res = run_bass_kernel_spmd(nc, in_maps, core_ids=list(range(NCORES)))
    out = np.concatenate([np.asarray(res.results[r]["out"], np.float32) for r in range(NCORES)], axis=0)
    return out[None]
```

```python
import contextlib
import math
import numpy as np
import ml_dtypes
import concourse.bass as bass
import concourse.mybir as mybir
from concourse.bass_utils import run_bass_kernel_spmd

F32 = mybir.dt.float32
BF16 = mybir.dt.bfloat16
AF = mybir.ActivationFunctionType
ALU = mybir.AluOpType
AX = mybir.AxisListType

NCORES = 8
D = 2048
KC = 16
SEQ = 16384
TO = SEQ // NCORES
NT = TO // 128
NCTX = 256
TT = TO + NCTX
HD = 64
SCALE = HD ** -0.5
NORM_EPS = 1e-6
LN_EPS = 1e-5
DFF = 8192
W0 = 5120
W1 = 4608
LAM_INIT0 = 0.8 - 0.6 * math.exp(0.0)
KB_MARGIN = 1.25

ENGS = ["tensor", "vector", "scalar", "gpsimd", "sync"]
NDMASEM = 8


class Buf:
    __slots__ = ("writer", "pw", "readers")

    def __init__(self):
        self.writer = None
        self.pw = []
        self.readers = []


class Prog:
    def __init__(self, nc):
        self.nc = nc
        self.ops = {e: [] for e in ENGS}
        self.dma_cnt = {}
        self.dma_rr = {e: 0 for e in ENGS}
        self.known = {e: {} for e in ENGS}
        self.need_inc = {e: set() for e in ENGS}

    def _add_wait(self, eng, waits, tok, is_raw):
        if tok is None:
            return
        if tok[0] == "E":
            _, e2, idx = tok
            if e2 == eng and (eng == "tensor" or not is_raw):
                return
            key = ("E", e2)
            val = idx
        else:
            _, q, slot, cnt = tok
            key = ("D", q, slot)
            val = cnt
        if self.known[eng].get(key, -1) >= val:
            return
        self.known[eng][key] = val
        waits.append(tok)
        if tok[0] == "E":
            self.need_inc[tok[1]].add(tok[2])

    def _deps(self, eng, reads, writes, pwrites, dma):
        waits = []
        for b in reads:
            self._add_wait(eng, waits, b.writer, True)
            for w in b.pw:
                self._add_wait(eng, waits, w, True)
        for b in writes:
            self._add_wait(eng, waits, b.writer, dma)
            for w in b.pw:
                self._add_wait(eng, waits, w, dma)
            for r in b.readers:
                self._add_wait(eng, waits, r, dma)
        for b in pwrites:
            self._add_wait(eng, waits, b.writer, dma)
            for r in b.readers:
                self._add_wait(eng, waits, r, dma)
        return waits

    def _commit(self, tok, reads, writes, pwrites):
        for b in reads:
            b.readers.append(tok)
            if len(b.readers) > 24:
                b.readers = b.readers[-24:] if False else b.readers
        for b in writes:
            b.writer = tok
            b.pw = []
            b.readers = []
        for b in pwrites:
            b.pw.append(tok)

    def op(self, eng, fn, reads=(), writes=(), pwrites=()):
        waits = self._deps(eng, reads, writes, pwrites, False)
        idx = len(self.ops[eng])
        tok = ("E", eng, idx)
        self.ops[eng].append(dict(fn=fn, waits=waits, dma=None))
        self._commit(tok, reads, writes, pwrites)
        return tok

    def dma(self, queue, fn, reads=(), writes=(), pwrites=()):
        waits = self._deps(queue, reads, writes, pwrites, True)
        slot = self.dma_rr[queue] % NDMASEM
        self.dma_rr[queue] += 1
        prev = self.dma_cnt.get((queue, slot), 0)
        if prev > 0:
            self._add_wait(queue, waits, ("D", queue, slot, prev), True)
        cnt = prev + 1
        self.dma_cnt[(queue, slot)] = cnt
        tok = ("D", queue, slot, cnt)
        self.ops[queue].append(dict(fn=fn, waits=waits, dma=(slot, cnt)))
        self._commit(tok, reads, writes, pwrites)
        return tok

    def barrier(self):
        toks = []
        for e in ENGS:
            for i in range(len(self.ops[e]) - 1, -1, -1):
                o = self.ops[e][i]
                if o["fn"] is not None and o["dma"] is None:
                    toks.append(("E", e, i))
                    break
        for (q, slot), cnt in self.dma_cnt.items():
            toks.append(("D", q, slot, cnt))
        for e in ENGS:
            waits = []
            for t in toks:
                self._add_wait(e, waits, t, True)
            if waits:
                self.ops[e].append(dict(fn=None, waits=waits, dma=None))

    def finish_wait(self, eng, toks):
        waits = []
        for t in toks:
            self._add_wait(eng, waits, t, True)
        self.ops[eng].append(dict(fn=None, waits=waits, dma=None))

    def emit(self, st):
        nc = self.nc
        esem = {e: st.enter_context(nc.semaphore("es_" + e)) for e in ENGS}
        dsem = {}
        for q in ENGS:
            for s in range(NDMASEM):
                if (q, s) in self.dma_cnt:
                    dsem[(q, s)] = st.enter_context(nc.semaphore("ds_%s_%d" % (q, s)))
        incval = {}
        for e in ENGS:
            c = 0
            m = {}
            for i in range(len(self.ops[e])):
                if i in self.need_inc[e]:
                    c += 1
                    m[i] = c
            incval[e] = m
        block = st.enter_context(nc.Block())

        def run(ename):
            def body(eng):
                for i, o in enumerate(self.ops[ename]):
                    for t in o["waits"]:
                        if t[0] == "E":
                            eng.wait_ge(esem[t[1]], incval[t[1]][t[2]])
                        else:
                            eng.wait_ge(dsem[(t[1], t[2])], 16 * t[3])
                    if o["fn"] is None:
                        continue
                    ins = o["fn"](eng)
                    if o["dma"] is not None:
                        ins.then_inc(dsem[(ename, o["dma"][0])], 16)
                    elif i in incval[ename]:
                        ins.then_inc(esem[ename], 1)
            return body

        for e in ENGS:
            if self.ops[e]:
                getattr(block, e)(run(e))


class Rot:
    def __init__(self, items):
        self.items = items
        self.i = 0
        self.reserved = set()

    def next(self):
        for _ in range(2 * len(self.items)):
            it = self.items[self.i % len(self.items)]
            self.i += 1
            if not isinstance(it, int) or it not in self.reserved:
                return it
        raise RuntimeError("no free item")

    def take(self):
        it = self.next()
        self.reserved.add(it)
        return it

    def release(self, it):
        self.reserved.discard(it)


NE = TO + 256
NET = NE // 128
TT = NE + NCTX
UW = 15 + NE + 15
EBLOCKS = [(0, 512), (512, 512), (1024, 512), (1536, 512), (2048, 256)]


def build(phases=None, dbg=()):
    ALLP = ['adaln', 'l0in', 'l0conv', 'l0attn', 'tail0', 'l1in', 'l1conv', 'l1attn', 'tail1']
    phases = ALLP if phases is None else phases
    declared = []
    nc = bass.Bass("TRN2", target_bir_lowering=False)
    P = Prog(nc)

    def din(name, shape, dt=F32):
        declared.append(name)
        return nc.dram_tensor(name, list(shape), dt, kind="ExternalInput").ap()

    x_ext = din("x_ext", [NE + 256, D])
    x_all = din("x_all", [SEQ, D])
    cs_all = din("cs_all", [SEQ, 64])
    ctx_in = din("ctx", [NCTX, D])
    cvT = din("cvT", [128, KC, 2])
    cs_ext = din("cs_ext", [NE, 64])
    ident_in = din("ident", [128, 128])
    flags = din("flags", [128, 2])
    masks_in = din("masks", [128, 2, 128])
    class LazyW(dict):
        def __init__(self, l, wcols):
            self.l = l
            self.shapes = dict(mod_w=("l%d_mod_w", [D, 6 * D]), mod_bT=("l%d_mod_bT", [128, 96]), gains=("l%d_gains", [128, 4, KC]),
                               w_in=("l%d_w_in", [D, wcols]), w_out=("l%d_w_out", [D, D]), w1=("l%d_mlp_w1", [D, DFF]), w2=("l%d_mlp_w2", [DFF, D]))

        def __missing__(self, k):
            nm, shp = self.shapes[k]
            v = din(nm % self.l, shp)
            self[k] = v
            return v
    W = {0: LazyW(0, W0), 1: LazyW(1, W1)}
    conv_wT = din("l0_conv_wT", [128, 8, 31])
    conv_misc = din("l0_conv_misc", [128, 3, 8])
    lam_in = din("l0_lam", [4, 64])
    subln = din("l0_subln", [128, 1])
    sconv_wT = din("l1_sconv_wT", [128, 8, 3])
    sink_in = din("l1_sink", [1, 16])
    out_ext = nc.dram_tensor("out", [TO, D], F32, kind="ExternalOutput").ap()

    def dscr(name, shape, dt):
        return nc.dram_tensor(name, list(shape), dt)

    xT0 = dscr("xT0", [KC, 128, TT], F32)
    xT1 = dscr("xT1", [KC, 128, TT], F32)
    uT_e = dscr("uT_e", [8, 128, UW], F32)
    uT_c = dscr("uT_c", [8, 128, NCTX], F32)
    mixT = [dscr("mixT%d" % l, [KC, 128, TT], BF16) for l in range(2)]
    qT0 = dscr("qT0", [16, 65, TT], BF16)
    kt_all = dscr("kt_all", [8 * 128, SEQ], BF16)
    v_all = dscr("v_all", [8 * 128, SEQ], BF16)
    kt_ctx = dscr("kt_ctx", [8, 128, NCTX], BF16)
    cxT = dscr("cxT", [8, 128, NE], F32)
    bT = dscr("bT", [8, 128, NE], F32)
    qT1 = dscr("qT1", [16, 66, NE], BF16)
    kt1 = dscr("kt1", [256, TT], BF16)
    v1 = dscr("v1", [TT, 256], BF16)
    B = {nm: Buf() for nm in ["xT0", "xT1", "uT_e", "uT_c", "mixT0", "mixT1", "qT0", "kt_all", "v_all",
                              "kt_ctx", "cxT", "bT", "qT1", "kt1", "v1"]}
    out_toks = []

    top = contextlib.ExitStack()
    with top:
        _uniq = [0]

        def sbt(st, name, shape, dt):
            _uniq[0] += 1
            return st.enter_context(nc.sbuf_tensor("%s_%d" % (name, _uniq[0]), list(shape), dt))

        identf = sbt(top, "identf", [128, 128], F32); b_identf = Buf()
        identb = sbt(top, "identb", [128, 128], BF16); b_identb = Buf()
        onesb = sbt(top, "onesb", [128, 128], BF16); b_onesb = Buf()
        onesf = sbt(top, "onesf", [128, 128], F32); b_onesf = Buf()
        tab = sbt(top, "tab", [128, 2, 2, 6, KC], F32); b_tab = Buf()
        flg = sbt(top, "flg", [128, 2], F32); b_flg = Buf()
        cs = sbt(top, "cs", [128, NET, 64], F32); b_cs = Buf()
        negkb = [sbt(top, "negkb%d" % l, [128, 1], F32) for l in range(2)]
        b_negkb = [Buf(), Buf()]
        kmax = sbt(top, "kmax", [128, 8], F32); b_kmax = Buf()
        WP = {}

        def alloc_wp(st, tag, nslots=3):
            WP["t"] = sbt(st, "wp_" + tag, [128, nslots, KC, 512], BF16)
            WP["rot"] = Rot([(i, Buf()) for i in range(nslots)])
        ps = top.enter_context(nc.psum_tensor("ps", [128, 8, 512], F32))
        b_ps = [Buf() for _ in range(8)]
        prot = Rot(list(range(8)))

        def psb(bk):
            return ps[:, bk, :].bitcast(BF16)

        P.dma("sync", lambda e: e.dma_start(out=identf[:], in_=ident_in), writes=[b_identf])
        P.op("vector", lambda e: e.tensor_copy(identb[:], identf[:]), reads=[b_identf], writes=[b_identb])
        P.op("vector", lambda e: e.memset(onesb[:], 1.0), writes=[b_onesb])
        P.op("vector", lambda e: e.memset(onesf[:], 1.0), writes=[b_onesf])
        P.dma("sync", lambda e: e.dma_start(out=flg[:], in_=flags), writes=[b_flg])
        P.dma("sync", lambda e: e.dma_start(out=cs[:], in_=cs_ext.rearrange("(t p) c -> p t c", p=128)), writes=[b_cs])

        def wr(b, first):
            return dict(writes=[b] if first else [], pwrites=[] if first else [b])

        WB = {}

        def wsrc(l, name):
            if (l, name) in WB:
                t, b = WB[(l, name)]
                return t.ap(), [b]
            return W[l][name], []

        def precast(l, name, rows_per):
            src = W[l][name]
            rows, cols = src.shape
            t = dscr("wbf%d_%s" % (l, name), [rows, cols], BF16)
            b = Buf()
            for r0 in range(0, rows, rows_per):
                P.dma("gpsimd", lambda e, r0=r0: e.dma_start(out=t.ap()[r0:r0 + rows_per, :], in_=src[r0:r0 + rows_per, :]), pwrites=[b])
            WB[(l, name)] = (t, b)

        def load_wpiece(wdb, r0, c0, nk=KC, ncol=512):
            wd, rb = wdb if isinstance(wdb, tuple) else (wdb, [])
            slot, bw = WP["rot"].next()
            wp = WP["t"]
            src = wd[r0:r0 + nk * 128, c0:c0 + ncol].rearrange("(kc p) n -> p kc n", p=128)
            P.dma("gpsimd", lambda e: e.dma_start(out=wp[:, slot, 0:nk, 0:ncol], in_=src), reads=rb, writes=[bw])
            return slot, bw

        def mm_acc(bk, ncols_ap, lhsT_fn, rhs_fn, n, reads):
            for k in range(n):
                P.op("tensor", lambda e, k=k: e.matmul(ncols_ap, lhsT_fn(k), rhs_fn(k), start=(k == 0), stop=(k == n - 1)),
                     reads=reads, **wr(b_ps[bk], k == 0))

        def linear_fm(wd, col0, ncolg, rhs_fn, rhs_bufs, nkc, N, evac, kp=1):
            wpt = WP["t"]
            kper = nkc // kp
            for g in range(ncolg):
                banks = [prot.next() for _ in range(4)]
                for kpi in range(kp):
                    slot, bw = load_wpiece(wd, kpi * kper * 128, col0 + g * 512, nk=kper)
                    for oc in range(4):
                        bk = banks[oc]
                        for k in range(kper):
                            kc = kpi * kper + k
                            P.op("tensor", lambda e, bk=bk, slot=slot, k=k, oc=oc, kc=kc: e.matmul(
                                ps[:, bk, 0:N], wpt[:, slot, k, oc * 128:(oc + 1) * 128], rhs_fn(kc),
                                start=(kc == 0), stop=(kc == nkc - 1)),
                                reads=[bw] + rhs_bufs, **wr(b_ps[bk], kc == 0))
                for oc in range(4):
                    evac(g * 4 + oc, ps[:, banks[oc], 0:N], b_ps[banks[oc]])

        def linear_tm(wd, col0, hT, b_hT, ntile, evac):
            wpt = WP["t"]
            slot, bw = load_wpiece(wd, 0, col0)
            for t in range(ntile):
                bk = prot.next()
                for kc in range(KC):
                    P.op("tensor", lambda e, bk=bk, kc=kc, t=t: e.matmul(ps[:, bk, :], hT[:, kc, t * 128:(t + 1) * 128], wpt[:, slot, kc, :],
                                                                         start=(kc == 0), stop=(kc == KC - 1)),
                         reads=[bw, b_hT], **wr(b_ps[bk], kc == 0))
                evac(t, bk)

        def stats_finish(bk, N, rstd, b_rstd, scale, eps):
            P.op("scalar", lambda e: e.activation(out=rstd[:, 0:N], in_=ps[:, bk, 0:N], func=AF.Sqrt, scale=scale, bias=eps),
                 reads=[b_ps[bk]], writes=[b_rstd])
            P.op("vector", lambda e: e.reciprocal(rstd[:, 0:N], rstd[:, 0:N]), reads=[b_rstd], writes=[b_rstd])

        def sandwich_in(xT, b_xT, N, l, r, ia, hT, b_hT, rstd, b_rstd, tmpr, sqr):
            bk = prot.next()
            for c in range(KC):
                si, bs_ = sqr.next()
                P.op("scalar", lambda e, c=c, si=si: e.activation(out=si[:, 0:N], in_=xT[:, c, 0:N], func=AF.Square),
                     reads=[b_xT], writes=[bs_])
                P.op("tensor", lambda e, c=c, si=si: e.matmul(ps[:, bk, 0:N], onesb[:], si[:, 0:N], start=(c == 0), stop=(c == KC - 1)),
                     reads=[bs_, b_onesb], **wr(b_ps[bk], c == 0))
            stats_finish(bk, N, rstd, b_rstd, 1.0 / D, NORM_EPS)
            for fc in range(KC):
                ti, bt = tmpr.next()
                P.op("vector", lambda e, fc=fc, ti=ti: e.tensor_tensor(ti[:, 0:N], xT[:, fc, 0:N], rstd[:, 0:N], ALU.mult),
                     reads=[b_xT, b_rstd], writes=[bt])
                P.op("scalar", lambda e, fc=fc, ti=ti: e.activation(out=hT[:, fc, 0:N], in_=ti[:, 0:N], func=AF.Identity,
                                                                     scale=tab[:, l, r, ia, fc:fc + 1], bias=tab[:, l, r, ia + 1, fc:fc + 1]),
                     reads=[bt, b_tab], **wr(b_hT, fc == 0))

        def adaln_phase():
            with contextlib.ExitStack() as st:
                alloc_wp(st, "m")
                wpt = WP["t"]
                scT = sbt(st, "scT", [128, KC, 2], BF16); b_scT = Buf()
                cvs = sbt(st, "cvs", [128, KC, 2], F32); b_cvs = Buf()
                msb = sbt(st, "msb", [2, 6 * D], F32); b_msb = Buf()
                modT = sbt(st, "modT", [128, 96, 2], F32); b_modT = Buf()
                modb = sbt(st, "modb", [128, 96], F32); b_modb = Buf()
                gn = sbt(st, "gn", [128, 4, KC], F32); b_gn = Buf()
                P.dma("sync", lambda e: e.dma_start(out=cvs[:], in_=cvT), writes=[b_cvs])
                P.op("scalar", lambda e: e.activation(out=scT[:], in_=cvs[:], func=AF.Silu), reads=[b_cvs], writes=[b_scT])

                def do_layer(l):
                    P.dma("sync", lambda e: e.dma_start(out=modb[:], in_=W[l]["mod_bT"]), writes=[b_modb])
                    P.dma("sync", lambda e: e.dma_start(out=gn[:], in_=W[l]["gains"]), writes=[b_gn])

                    def do_cg(cg):
                        slot, bw = load_wpiece(W[l]["mod_w"], 0, cg * 512)
                        bk = prot.next()
                        for kc in range(KC):
                            P.op("tensor", lambda e, kc=kc: e.matmul(ps[0:2, bk, :], scT[:, kc, :], wpt[:, slot, kc, :], start=(kc == 0), stop=(kc == KC - 1)),
                                 reads=[bw, b_scT], **wr(b_ps[bk], kc == 0))
                        P.op("vector", lambda e: e.tensor_copy(msb[:, cg * 512:(cg + 1) * 512], ps[0:2, bk, :]), reads=[b_ps[bk]], **wr(b_msb, cg == 0))
                    for cg in range(24):
                        do_cg(cg)
                    bkT = prot.next()
                    for j in range(96):
                        P.op("tensor", lambda e, j=j: e.transpose(ps[:, bkT, 2 * j:2 * j + 2], msb[0:2, j * 128:(j + 1) * 128], identf[0:2, 0:2]),
                             reads=[b_msb, b_identf], **wr(b_ps[bkT], j == 0))
                    psv = ps[:, bkT, 0:192].rearrange("p (j r) -> p j r", r=2)
                    for r in range(2):
                        P.op("vector", lambda e, r=r: e.tensor_tensor(modT[:, :, r], psv[:, :, r], modb[:], ALU.add),
                             reads=[b_ps[bkT], b_modb], **wr(b_modT, r == 0))
                    for r in range(2):
                        for i in range(6):
                            P.op("vector", _bind_tab(tab, modT, gn, l, r, i), reads=[b_modT, b_gn], pwrites=[b_tab])
                for l in range(2):
                    do_layer(l)

        def rope(bk, t_e, dst, b_dst, tmps, do_rope, cst=None, b_cst=None):
            pv = ps[:, bk, :].rearrange("p (h two d) -> p h two d", two=2, d=32)
            if not do_rope:
                P.op("vector", lambda e: e.tensor_copy(dst[:].rearrange("p h d -> p (h d)"), ps[:, bk, :]), reads=[b_ps[bk]], writes=[b_dst])
                return
            if cst is None:
                cosb = cs[:, t_e, 0:32].unsqueeze(1).broadcast_to([128, 8, 32])
                sinb = cs[:, t_e, 32:64].unsqueeze(1).broadcast_to([128, 8, 32])
                b_csx = b_cs
            else:
                cosb = cst[:, 0:32].unsqueeze(1).broadcast_to([128, 8, 32])
                sinb = cst[:, 32:64].unsqueeze(1).broadcast_to([128, 8, 32])
                b_csx = b_cst
            (t1, bt1), (t2, bt2) = tmps
            t1v = t1[:, 0:256].rearrange("p (h d) -> p h d", d=32)
            t2v = t2[:, 0:256].rearrange("p (h d) -> p h d", d=32)
            x1 = pv[:, :, 0, :]
            x2 = pv[:, :, 1, :]
            P.op("vector", lambda e: e.tensor_tensor(t1v, x1, cosb, ALU.mult), reads=[b_ps[bk], b_csx], writes=[bt1])
            P.op("vector", lambda e: e.tensor_tensor(t2v, x2, sinb, ALU.mult), reads=[b_ps[bk], b_csx], writes=[bt2])
            P.op("vector", lambda e: e.tensor_tensor(dst[:, :, 0:32], t1v, t2v, ALU.subtract), reads=[bt1, bt2], writes=[b_dst])
            P.op("vector", lambda e: e.tensor_tensor(t1v, x2, cosb, ALU.mult), reads=[b_ps[bk], b_csx], writes=[bt1])
            P.op("vector", lambda e: e.tensor_tensor(t2v, x1, sinb, ALU.mult), reads=[b_ps[bk], b_csx], writes=[bt2])
            P.op("vector", lambda e: e.tensor_tensor(dst[:, :, 32:64], t1v, t2v, ALU.add), reads=[bt1, bt2], pwrites=[b_dst])

        def sqnorm(src, b_src, sq, b_sq, red, b_red):
            P.op("vector", lambda e: e.tensor_tensor(sq[:], src[:], src[:], ALU.mult), reads=[b_src], writes=[b_sq])
            P.op("vector", lambda e: e.tensor_reduce(red[:], sq[:], AX.X, ALU.add), reads=[b_sq], writes=[b_red])

        def q_finish(qr, b_qr, sq, b_sq, red, b_red, qa, b_qa, naug, qTs, b_qTs, dst_ap_fn, b_dstbuf):
            sqnorm(qr, b_qr, sq, b_sq, red, b_red)
            P.op("scalar", lambda e: e.activation(out=qa[:, :, 64:65], in_=red[:].unsqueeze(2), func=AF.Sqrt, scale=SCALE * SCALE),
                 reads=[b_red], writes=[b_qa])
            P.op("scalar", lambda e: e.activation(out=qa[:, :, 0:64], in_=qr[:], func=AF.Identity, scale=SCALE), reads=[b_qr], pwrites=[b_qa])
            bk = prot.next()
            for hm in range(8):
                P.op("tensor", lambda e, hm=hm: e.transpose(psb(bk)[0:naug, hm * 128:(hm + 1) * 128], qa[:, hm, :], identb[:]),
                     reads=[b_qa, b_identb], **wr(b_ps[bk], hm == 0))
            P.op("vector", lambda e: e.tensor_copy(qTs[:].rearrange("r h n -> r (h n)"), psb(bk)[0:naug, :]), reads=[b_ps[bk]], writes=[b_qTs])
            P.dma("sync", lambda e: e.dma_start(out=dst_ap_fn(), in_=qTs[:]), reads=[b_qTs], pwrites=[b_dstbuf])

        def kmax_update(red, b_red, first):
            if first[0]:
                P.op("vector", lambda e: e.tensor_copy(kmax[:], red[:]), reads=[b_red], writes=[b_kmax])
                first[0] = False
            else:
                P.op("vector", lambda e: e.tensor_tensor(kmax[:], kmax[:], red[:], ALU.max), reads=[b_red, b_kmax], writes=[b_kmax])

        def kb_finish(l):
            with contextlib.ExitStack() as st:
                m1 = sbt(st, "kbm1", [128, 1], F32); b1 = Buf()
                row = sbt(st, "kbrow", [1, 128], F32); b2 = Buf()
                one = sbt(st, "kbone", [1, 1], F32); b3 = Buf()
                P.op("vector", lambda e: e.tensor_reduce(m1[:], kmax[:], AX.X, ALU.max), reads=[b_kmax], writes=[b1])
                bk = prot.next()
                P.op("tensor", lambda e: e.transpose(ps[0:1, bk, 0:128], m1[:], identf[:]), reads=[b1, b_identf], writes=[b_ps[bk]])
                P.op("vector", lambda e: e.tensor_copy(row[:], ps[0:1, bk, 0:128]), reads=[b_ps[bk]], writes=[b2])
                P.op("vector", lambda e: e.tensor_reduce(one[:], row[:], AX.X, ALU.max), reads=[b2], writes=[b3])
                P.op("scalar", lambda e: e.activation(out=one[:], in_=one[:], func=AF.Sqrt), reads=[b3], writes=[b3])
                P.op("vector", lambda e: e.tensor_scalar(one[:], one[:], -KB_MARGIN, None, ALU.mult), reads=[b3], writes=[b3])
                bk2 = prot.next()
                P.op("tensor", lambda e: e.matmul(ps[:, bk2, 0:1], onesf[0:1, :], one[:], start=True, stop=True),
                     reads=[b3, b_onesf], writes=[b_ps[bk2]])
                P.op("vector", lambda e: e.tensor_copy(negkb[l][:], ps[:, bk2, 0:1]), reads=[b_ps[bk2]], writes=[b_negkb[l]])

        def l0_inproj():
            with contextlib.ExitStack() as st:
                alloc_wp(st, "a", 2)
                wpt = WP["t"]
                xrow = [sbt(st, "xrow%d" % i, [128, D], F32) for i in range(2)]
                b_xrow = [Buf(), Buf()]
                xTbs = Rot([(sbt(st, "a_xTb%d" % i, [128, KC, 512], F32), Buf()) for i in range(2)])
                hTs = Rot([(sbt(st, "a_hT%d" % i, [128, KC, 512], BF16), Buf()) for i in range(2)])
                rstd = sbt(st, "a_rstd", [128, 512], F32); b_rstd = Buf()
                tmpr = Rot([(sbt(st, "a_tmp%d" % i, [128, 512], F32), Buf()) for i in range(3)])
                sqr = Rot([(sbt(st, "a_sq%d" % i, [128, 512], BF16), Buf()) for i in range(3)])
                sig = Rot([(sbt(st, "a_sig%d" % i, [128, 512], F32), Buf()) for i in range(2)])
                ust = Rot([(sbt(st, "a_ust%d" % i, [128, 512], F32), Buf()) for i in range(2)])
                qr = sbt(st, "a_qr", [128, 8, 64], F32); b_qr = Buf()
                sq = sbt(st, "a_sqq", [128, 8, 64], F32); b_sq = Buf()
                red = sbt(st, "a_red", [128, 8], F32); b_red = Buf()
                qa = sbt(st, "a_qa", [128, 8, 65], BF16); b_qa = Buf()
                qTs = sbt(st, "a_qTs", [65, 8, 128], BF16); b_qTs = Buf()
                kbr = Rot([(sbt(st, "a_kb%d" % i, [128, 512], BF16), Buf()) for i in range(4)])
                kTr = Rot([(sbt(st, "a_kTs%d" % i, [128, 4, 128], BF16), Buf()) for i in range(2)])
                kpend = []
                vb = Rot([(sbt(st, "a_vb%d" % i, [128, 512], BF16), Buf()) for i in range(2)])
                rt = [(sbt(st, "a_rt%d" % i, [128, 256], F32), Buf()) for i in range(2)]
                kfirst = [True]
                blocks = [("e", 128 + c0, N, c0) for (c0, N) in EBLOCKS] + [("c", 0, NCTX, NE), ("h", 0, 256, None)]
                blocks += [("k", 512 * b, 512, 512 * b) for b in range(SEQ // 512)]
                csk = Rot([(sbt(st, "a_csk%d" % i, [128, 64], F32), Buf()) for i in range(2)])
                def do_block(kind, row0, N, col0):
                    ntile = N // 128
                    r = 1 if kind == "c" else 0
                    xTb, b_xTb = xTbs.next()
                    hT, b_hT = hTs.next()
                    for t in range(ntile):
                        xi = t % 2
                        if kind == "c":
                            src = ctx_in[t * 128:(t + 1) * 128, :]
                        elif kind == "h":
                            src = x_ext[0:128, :] if t == 0 else x_ext[NE + 128:NE + 256, :]
                        elif kind == "k":
                            src = x_all[row0 + t * 128:row0 + (t + 1) * 128, :]
                        else:
                            src = x_ext[row0 + t * 128:row0 + (t + 1) * 128, :]
                        P.dma("sync", lambda e, xi=xi, src=src: e.dma_start(out=xrow[xi][:], in_=src), writes=[b_xrow[xi]])
                        for g4 in range(4):
                            bk = prot.next()
                            for j in range(4):
                                fc = g4 * 4 + j
                                P.op("tensor", lambda e, fc=fc, xi=xi, j=j, bk=bk: e.transpose(ps[:, bk, j * 128:(j + 1) * 128], xrow[xi][:, fc * 128:(fc + 1) * 128], identf[:]),
                                     reads=[b_xrow[xi], b_identf], **wr(b_ps[bk], j == 0))
                            P.op("vector", lambda e, g4=g4, t=t, bk=bk: e.tensor_copy(xTb[:, g4 * 4:(g4 + 1) * 4, t * 128:(t + 1) * 128],
                                                                                       ps[:, bk, :].rearrange("p (j n) -> p j n", n=128)),
                                 reads=[b_ps[bk]], **wr(b_xTb, t == 0 and g4 == 0))
                    if kind in ("e", "c"):
                        P.dma("sync", lambda e, N=N, col0=col0: e.dma_start(out=xT0.ap()[:, :, col0:col0 + N].rearrange("f p n -> p f n"), in_=xTb[:, :, 0:N]),
                              reads=[b_xTb], pwrites=[B["xT0"]])
                    yield "T"
                    sandwich_in(xTb, b_xTb, N, 0, r, 0, hT, b_hT, rstd, b_rstd, tmpr, sqr)
                    yield "S"
                    def do_half(half):
                        sa, bwa = load_wpiece(W[0]["w_in"], 0, half * 512)
                        sg, bwg = load_wpiece(W[0]["w_in"], 0, 1024 + half * 512)
                        for oc in range(4):
                            i = half * 4 + oc
                            bkg = prot.next()
                            bka = prot.next()
                            for kc in range(KC):
                                P.op("tensor", lambda e, kc=kc, oc=oc, bkg=bkg: e.matmul(ps[:, bkg, 0:N], wpt[:, sg, kc, oc * 128:(oc + 1) * 128], hT[:, kc, 0:N],
                                                                                           start=(kc == 0), stop=(kc == KC - 1)),
                                     reads=[bwg, b_hT], **wr(b_ps[bkg], kc == 0))
                            for kc in range(KC):
                                P.op("tensor", lambda e, kc=kc, oc=oc, bka=bka: e.matmul(ps[:, bka, 0:N], wpt[:, sa, kc, oc * 128:(oc + 1) * 128], hT[:, kc, 0:N],
                                                                                           start=(kc == 0), stop=(kc == KC - 1)),
                                     reads=[bwa, b_hT], **wr(b_ps[bka], kc == 0))
                            si, bs_ = sig.next()
                            P.op("scalar", lambda e, bkg=bkg, si=si: e.activation(out=si[:, 0:N], in_=ps[:, bkg, 0:N], func=AF.Sigmoid),
                                 reads=[b_ps[bkg]], writes=[bs_])
                            ui, bu = ust.next()
                            P.op("vector", lambda e, bka=bka, si=si, ui=ui: e.tensor_tensor(ui[:, 0:N], ps[:, bka, 0:N], si[:, 0:N], ALU.mult),
                                 reads=[b_ps[bka], bs_], writes=[bu])
                            if kind == "e":
                                P.dma("sync", lambda e, i=i, ui=ui: e.dma_start(out=uT_e.ap()[i, :, 15 + col0:15 + col0 + N], in_=ui[:, 0:N]),
                                      reads=[bu], pwrites=[B["uT_e"]])
                            elif kind == "c":
                                P.dma("sync", lambda e, i=i, ui=ui: e.dma_start(out=uT_c.ap()[i, :, :], in_=ui[:, 0:N]), reads=[bu], pwrites=[B["uT_c"]])
                            else:
                                P.dma("sync", lambda e, i=i, ui=ui: e.dma_start(out=uT_e.ap()[i, :, 0:15], in_=ui[:, 113:128]), reads=[bu], pwrites=[B["uT_e"]])
                                P.dma("sync", lambda e, i=i, ui=ui: e.dma_start(out=uT_e.ap()[i, :, UW - 15:UW], in_=ui[:, 128:143]), reads=[bu], pwrites=[B["uT_e"]])
                    if kind != "k":
                        for half in range(2):
                            do_half(half)
                    if kind == "h":
                        return
                    for p in range(2 if kind != "k" else 0):
                        def evac_q(t, bk, p=p):
                            t_e = (col0 // 128 + t) if kind == "e" else 0
                            rope(bk, t_e, qr, b_qr, rt, kind == "e")
                            cc = col0 + t * 128
                            q_finish(qr, b_qr, sq, b_sq, red, b_red, qa, b_qa, 65, qTs, b_qTs,
                                     lambda: qT0.ap()[p * 8:(p + 1) * 8, :, cc:cc + 128].rearrange("h r n -> r h n"), B["qT0"])
                        linear_tm(W[0]["w_in"], 2048 + p * 512, hT, b_hT, ntile, evac_q)
                    for p in range(2 if kind != "e" else 0):
                        def evac_k(t, bk, p=p):
                            if kind == "k":
                                ci_, bci = csk.next()
                                g0 = row0 + t * 128
                                P.dma("sync", lambda e: e.dma_start(out=ci_[:], in_=cs_all[g0:g0 + 128, :]), writes=[bci])
                                rope(bk, 0, qr, b_qr, rt, True, cst=ci_, b_cst=bci)
                            else:
                                rope(bk, 0, qr, b_qr, rt, False)
                            sqnorm(qr, b_qr, sq, b_sq, red, b_red)
                            kmax_update(red, b_red, kfirst)
                            kb_, b_kb = kbr.next()
                            P.op("scalar", lambda e: e.activation(out=kb_[:], in_=qr[:].rearrange("p h d -> p (h d)"), func=AF.Identity), reads=[b_qr], writes=[b_kb])

                            def part2():
                                bk2 = prot.next()
                                kTs, b_kTs = kTr.next()
                                for hh in range(4):
                                    P.op("tensor", lambda e, hh=hh: e.transpose(psb(bk2)[:, hh * 128:(hh + 1) * 128], kb_[:, hh * 128:(hh + 1) * 128], identb[:]),
                                         reads=[b_kb, b_identb], **wr(b_ps[bk2], hh == 0))
                                P.op("vector", lambda e: e.tensor_copy(kTs[:].rearrange("q h n -> q (h n)"), psb(bk2)[:, 0:512]), reads=[b_ps[bk2]], writes=[b_kTs])
                                if kind == "c":
                                    P.dma("sync", lambda e: e.dma_start(out=kt_ctx.ap()[p * 4:(p + 1) * 4, :, t * 128:(t + 1) * 128].rearrange("h q n -> q h n"), in_=kTs[:]),
                                          reads=[b_kTs], pwrites=[B["kt_ctx"]])
                                else:
                                    oc0 = row0 + t * 128
                                    P.dma("sync", lambda e: e.dma_start(
                                        out=kt_all.ap().rearrange("(h q) n -> q h n", q=128)[:, p * 4:(p + 1) * 4, oc0:oc0 + 128], in_=kTs[:]),
                                        reads=[b_kTs], pwrites=[B["kt_all"]])
                            kpend.append(part2)
                            if len(kpend) > 2:
                                kpend.pop(0)()
                        linear_tm(W[0]["w_in"], 3072 + p * 512, hT, b_hT, ntile, evac_k)
                    while kpend:
                        kpend.pop(0)()
                    yield "K"
                    for p in range(2 if kind != "e" else 0):
                        def evac_v(t, bk, p=p):
                            vi, bv = vb.next()
                            P.op("scalar", lambda e: e.activation(out=vi[:], in_=ps[:, bk, :], func=AF.Identity), reads=[b_ps[bk]], writes=[bv])
                            if kind == "c":
                                P.dma("sync", lambda e: e.dma_start(out=vctx[:, t, p * 512:(p + 1) * 512], in_=vi[:]), reads=[bv], pwrites=[b_vctx])
                            else:
                                c = row0 // 128 + t
                                P.dma("sync", lambda e: e.dma_start(
                                    out=v_all.ap().rearrange("(h q) (c e) -> q h c e", q=128, e=128)[:, p * 4:(p + 1) * 4, c, :],
                                    in_=vi[:].rearrange("q (h e) -> q h e", e=128)),
                                    reads=[bv], pwrites=[B["v_all"]])
                        linear_tm(W[0]["w_in"], 4096 + p * 512, hT, b_hT, ntile, evac_v)
                    yield "V"
                for blk in blocks:
                    if blk[0] != "k":
                        for _ in do_block(*blk):
                            pass
                gens = [do_block(*blk) for blk in blocks if blk[0] == "k"]
                next(gens[0])
                next(gens[0])
                for i in range(len(gens)):
                    gn = gens[i + 1] if i + 1 < len(gens) else None
                    if gn is not None:
                        next(gn)
                    next(gens[i])
                    if gn is not None:
                        next(gn)
                    next(gens[i])
                kb_finish(0)

        vctx = sbt(top, "vctx", [128, 2, 1024], BF16); b_vctx = Buf()

        def l0_conv():
            with contextlib.ExitStack() as st:
                cw = sbt(st, "c_cw", [128, 8, 31], F32); b_cw = Buf()
                cm = sbt(st, "c_cm", [128, 3, 8], F32); b_cm = Buf()
                vT = sbt(st, "c_vT", [128, 8, TT], F32); b_vT = [Buf() for _ in range(8)]
                uS = [sbt(st, "c_uS%d" % i, [128, UW], F32) for i in range(2)]; b_uS = [Buf(), Buf()]
                uC = [sbt(st, "c_uC%d" % i, [128, NCTX + 30], F32) for i in range(2)]; b_uC = [Buf(), Buf()]
                mean = sbt(st, "c_mean", [128, 512], F32); b_mean = Buf()
                msq = sbt(st, "c_msq", [128, 512], F32); b_msq = Buf()
                rstd = sbt(st, "c_rstd", [128, 512], F32); b_rstd = Buf()
                tmpr = Rot([(sbt(st, "c_tmp%d" % i, [128, 512], F32), Buf()) for i in range(3)])
                vbr = Rot([(sbt(st, "c_vb%d" % i, [128, 512], BF16), Buf()) for i in range(3)])
                ast = Rot([(sbt(st, "c_ast%d" % i, [128, 512], BF16), Buf()) for i in range(2)])
                P.dma("sync", lambda e: e.dma_start(out=cw[:], in_=conv_wT), writes=[b_cw])
                P.dma("sync", lambda e: e.dma_start(out=cm[:], in_=conv_misc), writes=[b_cm])
                for i in range(2):
                    P.op("vector", lambda e, i=i: e.memset(uC[i][:], 0.0), writes=[b_uC[i]])
                for i in range(8):
                    eng = "vector"
                    s = i % 2
                    P.dma("sync", lambda e, i=i, s=s: e.dma_start(out=uS[s][:], in_=uT_e.ap()[i]), reads=[B["uT_e"]], writes=[b_uS[s]])
                    P.dma("sync", lambda e, i=i, s=s: e.dma_start(out=uC[s][:, 15:15 + NCTX], in_=uT_c.ap()[i]), reads=[B["uT_c"]], pwrites=[b_uC[s]])
                    P.op(eng, lambda e, s=s: e.tensor_scalar(uS[s][:, 0:15 + 128], uS[s][:, 0:15 + 128], flg[:, 0:1], None, ALU.mult),
                         reads=[b_uS[s], b_flg], writes=[b_uS[s]])
                    P.op(eng, lambda e, s=s: e.tensor_scalar(uS[s][:, UW - 15 - 128:UW], uS[s][:, UW - 15 - 128:UW], flg[:, 1:2], None, ALU.mult),
                         reads=[b_uS[s], b_flg], writes=[b_uS[s]])
                    for (src, bsrc, c0, n) in ((uS[s], b_uS[s], 0, NE), (uC[s], b_uC[s], NE, NCTX)):
                        dst = vT[:, i, c0:c0 + n]
                        P.op(eng, lambda e, src=src, dst=dst, i=i, n=n: e.tensor_scalar(dst, src[:, 0:n], cw[:, i, 0:1], cm[:, 0, i:i + 1], ALU.mult, ALU.add),
                             reads=[bsrc, b_cw, b_cm], writes=[b_vT[i]] if c0 == 0 else [], pwrites=[b_vT[i]] if c0 else [])
                        for j in range(1, 31):
                            P.op(eng, lambda e, src=src, dst=dst, i=i, n=n, j=j: e.scalar_tensor_tensor(dst, src[:, j:j + n], cw[:, i, j:j + 1], dst, ALU.mult, ALU.add),
                                 reads=[bsrc, b_cw], pwrites=[b_vT[i]])
                def do_ln(c0, N):
                    b1 = prot.next()
                    b2 = prot.next()
                    for i in range(8):
                        v1i, bv1 = vbr.next()
                        P.op("scalar", lambda e, i=i, v1i=v1i: e.activation(out=v1i[:, 0:N], in_=vT[:, i, c0:c0 + N], func=AF.Identity), reads=[b_vT[i]], writes=[bv1])
                        P.op("tensor", lambda e, i=i, v1i=v1i: e.matmul(ps[:, b1, 0:N], onesb[:], v1i[:, 0:N], start=(i == 0), stop=(i == 7)),
                             reads=[bv1, b_onesb], **wr(b_ps[b1], i == 0))
                        v2i, bv2 = vbr.next()
                        P.op("scalar", lambda e, i=i, v2i=v2i: e.activation(out=v2i[:, 0:N], in_=vT[:, i, c0:c0 + N], func=AF.Square), reads=[b_vT[i]], writes=[bv2])
                        P.op("tensor", lambda e, i=i, v2i=v2i: e.matmul(ps[:, b2, 0:N], onesb[:], v2i[:, 0:N], start=(i == 0), stop=(i == 7)),
                             reads=[bv2, b_onesb], **wr(b_ps[b2], i == 0))
                    P.op("scalar", lambda e: e.activation(out=mean[:, 0:N], in_=ps[:, b1, 0:N], func=AF.Identity, scale=1.0 / 1024), reads=[b_ps[b1]], writes=[b_mean])
                    P.op("vector", lambda e: e.tensor_tensor(msq[:, 0:N], mean[:, 0:N], mean[:, 0:N], ALU.mult), reads=[b_mean], writes=[b_msq])
                    P.op("vector", lambda e: e.scalar_tensor_tensor(msq[:, 0:N], ps[:, b2, 0:N], 1.0 / 1024, msq[:, 0:N], ALU.mult, ALU.subtract),
                         reads=[b_ps[b2], b_msq], writes=[b_msq])
                    P.op("scalar", lambda e: e.activation(out=rstd[:, 0:N], in_=msq[:, 0:N], func=AF.Sqrt, bias=LN_EPS), reads=[b_msq], writes=[b_rstd])
                    P.op("vector", lambda e: e.reciprocal(rstd[:, 0:N], rstd[:, 0:N]), reads=[b_rstd], writes=[b_rstd])
                    for i in range(8):
                        ti, bt = tmpr.next()
                        P.op("vector", lambda e, i=i, ti=ti: e.tensor_tensor(ti[:, 0:N], vT[:, i, c0:c0 + N], mean[:, 0:N], ALU.subtract), reads=[b_vT[i], b_mean], writes=[bt])
                        P.op("vector", lambda e, ti=ti: e.tensor_tensor(ti[:, 0:N], ti[:, 0:N], rstd[:, 0:N], ALU.mult), reads=[bt, b_rstd], writes=[bt])
                        ai, ba = ast.next()
                        P.op("scalar", lambda e, i=i, ti=ti, ai=ai: e.activation(out=ai[:, 0:N], in_=ti[:, 0:N], func=AF.Silu, scale=cm[:, 1, i:i + 1], bias=cm[:, 2, i:i + 1]),
                             reads=[bt, b_cm], writes=[ba])
                        P.dma("sync", lambda e, i=i, ai=ai: e.dma_start(out=mixT[0].ap()[i, :, c0:c0 + N], in_=ai[:, 0:N]), reads=[ba], pwrites=[B["mixT0"]])
                for (c0, N) in EBLOCKS + [(NE, NCTX)]:
                    do_ln(c0, N)

        def l0_attn():
            precast(0, "w_out", 512)
            precast(0, "w1", 128)
            precast(0, "w2", 512)
            precast(1, "w_in", 256)
            precast(1, "w_out", 512)
            precast(1, "w1", 128)
            precast(1, "w2", 512)
            with contextlib.ExitStack() as st:
                NKC = NCORES * NT
                KT = [sbt(st, "e_KT%d" % i, [65, SEQ + NCTX], BF16) for i in range(2)]; b_KT = [Buf(), Buf()]
                Vh = [sbt(st, "e_V0", [128, NKC, 128], BF16)]; b_Vh = [Buf()]
                qS = [sbt(st, "e_q%d" % i, [65, TT], BF16) for i in range(2)]; b_qS = [Buf(), Buf()]
                pT = Rot([(sbt(st, "e_pT%d" % i, [128, 2, 512], BF16), Buf()) for i in range(3)])
                accr = Rot([(sbt(st, "e_acc%d" % i, [128, 512], F32), Buf()) for i in range(2)])
                o0 = sbt(st, "e_o0", [128, TT], F32); b_o0 = Buf()
                rs = sbt(st, "e_rs", [128, 512], F32); b_rs = Buf()
                od = sbt(st, "e_od", [128, 512], F32); b_od = Buf()
                tt_ = sbt(st, "e_tt", [128, 512], F32); b_tt = Buf()
                sqb = sbt(st, "e_sqb", [128, 512], BF16); b_sqb = Buf()
                rn = sbt(st, "e_rn", [128, 512], F32); b_rn = Buf()
                bl = Rot([(sbt(st, "e_bl%d" % i, [128, 512], BF16), Buf()) for i in range(2)])
                lamv = sbt(st, "e_lamv", [128, 4, 64], F32); b_lamv = Buf()
                lam = sbt(st, "e_lam", [128, 4], F32); b_lam = Buf()
                gsub = sbt(st, "e_gsub", [128, 1], F32); b_gsub = Buf()
                P.dma("sync", lambda e: e.dma_start(out=lamv[:].rearrange("p a d -> p (a d)"), in_=lam_in.rearrange("a d -> (a d)").partition_broadcast(128)), writes=[b_lamv])
                P.op("vector", lambda e: e.tensor_tensor(lamv[:, 0, :], lamv[:, 0, :], lamv[:, 1, :], ALU.mult), reads=[b_lamv], writes=[b_lamv])
                P.op("vector", lambda e: e.tensor_tensor(lamv[:, 2, :], lamv[:, 2, :], lamv[:, 3, :], ALU.mult), reads=[b_lamv], writes=[b_lamv])
                P.op("vector", lambda e: e.tensor_reduce(lam[:, 0:1], lamv[:, 0, :], AX.X, ALU.add), reads=[b_lamv], writes=[b_lam])
                P.op("vector", lambda e: e.tensor_reduce(lam[:, 1:2], lamv[:, 2, :], AX.X, ALU.add), reads=[b_lamv], writes=[b_lam])
                P.op("scalar", lambda e: e.activation(out=lam[:, 0:2], in_=lam[:, 0:2], func=AF.Exp), reads=[b_lam], writes=[b_lam])
                P.op("vector", lambda e: e.tensor_tensor(lam[:, 2:3], lam[:, 1:2], lam[:, 0:1], ALU.subtract), reads=[b_lam], writes=[b_lam])
                P.op("vector", lambda e: e.tensor_scalar(lam[:, 2:3], lam[:, 2:3], -LAM_INIT0, None, ALU.add), reads=[b_lam], writes=[b_lam])
                P.dma("sync", lambda e: e.dma_start(out=gsub[:], in_=subln), writes=[b_gsub])
                P.op("vector", lambda e: e.tensor_scalar(gsub[:], gsub[:], 1.0 - LAM_INIT0, None, ALU.mult), reads=[b_gsub], writes=[b_gsub])
                for i in range(2):
                    P.op("vector", lambda e, i=i: e.memset(KT[i][64:65, :], 1.0), writes=[b_KT[i]])
                    P.op("vector", lambda e, i=i: e.tensor_scalar(KT[i][64:65, :], KT[i][64:65, :], negkb[0][64:65, 0:1], None, ALU.mult),
                         reads=[b_KT[i], b_negkb[0]], writes=[b_KT[i]])
                qblocks = [(c0, N, True) for (c0, N) in EBLOCKS] + [(NE, NCTX, False)]
                for h in range(8):
                    vi = 0
                    P.dma("sync", lambda e, h=h, vi=vi: e.dma_start(
                        out=Vh[vi][:], in_=v_all.ap().rearrange("(h p) (c e) -> h p c e", p=128, e=128)[h]),
                        reads=[B["v_all"]], writes=[b_Vh[vi]])
                    for m in range(2):
                        hm = 2 * h + m
                        ki = hm % 2
                        P.dma("sync", lambda e, h=h, m=m, ki=ki: e.dma_start(
                            out=KT[ki][0:64, NCTX:], in_=kt_all.ap()[h * 128 + m * 64:h * 128 + (m + 1) * 64, :]),
                            reads=[B["kt_all"]], pwrites=[b_KT[ki]])
                        P.dma("sync", lambda e, h=h, m=m, ki=ki: e.dma_start(out=KT[ki][0:64, 0:NCTX], in_=kt_ctx.ap()[h, m * 64:(m + 1) * 64, :]),
                              reads=[B["kt_ctx"]], pwrites=[b_KT[ki]])
                        P.dma("sync", lambda e, hm=hm, ki=ki: e.dma_start(out=qS[ki][:], in_=qT0.ap()[hm]), reads=[B["qT0"]], writes=[b_qS[ki]])
                        def do_qblock(h, m, hm, ki, vi, c0, N, own):
                            bo, bs = 6, 7
                            nch = 2 + (NKC if own else 0)
                            npair = nch // 2
                            PAIRS = [(0, 1), (2, 3), (4, 5)]
                            LOOK = 2
                            acc, b_acc = accr.next()

                            def mm1(cp):
                                b0, b1 = PAIRS[cp % 3]
                                for j, bb in ((0, b0), (1, b1)):
                                    ci = 2 * cp + j
                                    P.op("tensor", lambda e, ci=ci, bb=bb: e.matmul(ps[:, bb, 0:N], KT[ki][:, ci * 128:(ci + 1) * 128], qS[ki][:, c0:c0 + N], start=True, stop=True),
                                         reads=[b_KT[ki], b_qS[ki]], writes=[b_ps[bb]])
                                return b0

                            def rest(cp, b0):
                                pi, bp = pT.next()
                                P.op("scalar", lambda e: e.activation(out=pi[:, :, 0:N], in_=ps[:, b0:b0 + 2, 0:N], func=AF.Exp),
                                     reads=[b_ps[b0], b_ps[b0 + 1]], writes=[bp])
                                return pi, bp

                            def mm23(cp, pi, bp):
                                for j in range(2):
                                    ci = 2 * cp + j
                                    if ci < 2:
                                        vl = vctx[:, ci, h * 128:(h + 1) * 128]
                                        vrd = b_vctx
                                    else:
                                        vl = Vh[vi][:, ci - 2, :]
                                        vrd = b_Vh[vi]
                                    P.op("tensor", lambda e, vl=vl, j=j, ci=ci: e.matmul(ps[:, bo, 0:N], vl, pi[:, j, 0:N], start=(ci == 0), stop=(ci == nch - 1)),
                                         reads=[vrd, bp], **wr(b_ps[bo], ci == 0))
                                    if j == 0:
                                        P.op("tensor", lambda e, j=j, ci=ci: e.matmul(ps[:, bs, 0:N], onesb[:], pi[:, j, 0:N], start=(ci == 0), stop=False),
                                             reads=[bp, b_onesb], **wr(b_ps[bs], ci == 0))
                                    elif cp == 0:
                                        P.op("vector", lambda e, j=j: e.tensor_copy(acc[:, 0:N], pi[:, j, 0:N]), reads=[bp], writes=[b_acc])
                                    else:
                                        P.op("vector", lambda e, j=j: e.tensor_tensor(acc[:, 0:N], acc[:, 0:N], pi[:, j, 0:N], ALU.add), reads=[bp, b_acc], writes=[b_acc])
                            pend = {}
                            for cp in range(min(LOOK, npair)):
                                pend[cp] = mm1(cp)
                            for cp in range(npair):
                                pi, bp = rest(cp, pend.pop(cp))
                                if cp + LOOK < npair:
                                    pend[cp + LOOK] = mm1(cp + LOOK)
                                mm23(cp, pi, bp)
                            P.op("tensor", lambda e: e.matmul(ps[:, bs, 0:N], onesf[:], acc[:, 0:N], start=False, stop=True),
                                 reads=[b_acc, b_onesf], pwrites=[b_ps[bs]])
                            P.op("vector", lambda e: e.reciprocal(rs[:, 0:N], ps[:, bs, 0:N]), reads=[b_ps[bs]], writes=[b_rs])
                            if m == 0:
                                P.op("vector", lambda e: e.tensor_tensor(o0[:, c0:c0 + N], ps[:, bo, 0:N], rs[:, 0:N], ALU.mult),
                                     reads=[b_ps[bo], b_rs], pwrites=[b_o0])
                            else:
                                P.op("vector", lambda e: e.tensor_tensor(tt_[:, 0:N], ps[:, bo, 0:N], rs[:, 0:N], ALU.mult), reads=[b_ps[bo], b_rs], writes=[b_tt])
                                P.op("vector", lambda e: e.scalar_tensor_tensor(od[:, 0:N], tt_[:, 0:N], lam[:, 2:3], o0[:, c0:c0 + N], ALU.mult, ALU.add),
                                     reads=[b_tt, b_lam, b_o0], writes=[b_od])
                                P.op("scalar", lambda e: e.activation(out=sqb[:, 0:N], in_=od[:, 0:N], func=AF.Square), reads=[b_od], writes=[b_sqb])
                                bn = 0
                                P.op("tensor", lambda e: e.matmul(ps[:, bn, 0:N], onesb[:], sqb[:, 0:N], start=True, stop=True), reads=[b_sqb, b_onesb], writes=[b_ps[bn]])
                                stats_finish(bn, N, rn, b_rn, 1.0 / 128, NORM_EPS)
                                P.op("vector", lambda e: e.tensor_tensor(od[:, 0:N], od[:, 0:N], rn[:, 0:N], ALU.mult), reads=[b_od, b_rn], writes=[b_od])
                                bi, bb = bl.next()
                                P.op("scalar", lambda e, bi=bi: e.activation(out=bi[:, 0:N], in_=od[:, 0:N], func=AF.Identity, scale=gsub[:, 0:1]), reads=[b_od, b_gsub], writes=[bb])
                                P.dma("sync", lambda e, bi=bi, h=h: e.dma_start(out=mixT[0].ap()[8 + h, :, c0:c0 + N], in_=bi[:, 0:N]), reads=[bb], pwrites=[B["mixT0"]])
                        for (c0, N, own) in qblocks:
                            do_qblock(h, m, hm, ki, vi, c0, N, own)

        def tail_phase(l, blocks, src, b_src, dst, b_dst, final):
            with contextlib.ExitStack() as st:
                alloc_wp(st, "t%d" % l, 2)
                xTb = sbt(st, "t_xTb", [128, KC, 512], F32); b_xTb = Buf()
                hid = sbt(st, "t_hid", [128, 64, 512], BF16); b_hid = Buf()
                mixb = hid; b_mixb = b_hid
                yst = sbt(st, "t_yst", [128, KC, 512], F32); b_yst = Buf()
                hT = yst[:].bitcast(BF16); b_hT = b_yst
                rstd = sbt(st, "t_rstd", [128, 512], F32); b_rstd = Buf()
                tmpr = Rot([(sbt(st, "t_tmp%d" % i, [128, 512], F32), Buf()) for i in range(3)])
                sqr = Rot([(sbt(st, "t_sq%d" % i, [128, 512], BF16), Buf()) for i in range(3)])
                orow = sbt(st, "t_orow", [128, D], F32) if final else None
                b_orow = Buf()
                for (r, c0, N, frow) in blocks:
                    P.dma("sync", lambda e, c0=c0, N=N: e.dma_start(out=xTb[:, :, 0:N], in_=src.ap()[:, :, c0:c0 + N].rearrange("f p n -> p f n")),
                          reads=[b_src], writes=[b_xTb])
                    P.dma("sync", lambda e, c0=c0, N=N: e.dma_start(out=mixb[:, 0:KC, 0:N], in_=mixT[l].ap()[:, :, c0:c0 + N].rearrange("f p n -> p f n")),
                          reads=[B["mixT%d" % l]], writes=[b_mixb])

                    def lin_post(wd, nkc, rhs_fn, rhs_bufs, kp, ig, N=N, r=r):
                        ssb = prot.take()

                        def evac(fc, pap, pbuf):
                            P.op("scalar", lambda e: e.activation(out=yst[:, fc, 0:N], in_=pap, func=AF.Identity), reads=[pbuf], **wr(b_yst, fc == 0))
                            si, bs_ = sqr.next()
                            P.op("scalar", lambda e: e.activation(out=si[:, 0:N], in_=pap, func=AF.Square), reads=[pbuf], writes=[bs_])
                            P.op("tensor", lambda e: e.matmul(ps[:, ssb, 0:N], onesb[:], si[:, 0:N], start=(fc == 0), stop=(fc == KC - 1)),
                                 reads=[bs_, b_onesb], **wr(b_ps[ssb], fc == 0))
                        linear_fm(wd, 0, 4, rhs_fn, rhs_bufs, nkc, N, evac, kp=kp)
                        stats_finish(ssb, N, rstd, b_rstd, 1.0 / D, NORM_EPS)
                        prot.release(ssb)
                        for fc in range(KC):
                            ti, bt = tmpr.next()
                            P.op("vector", lambda e, fc=fc, ti=ti: e.tensor_tensor(ti[:, 0:N], yst[:, fc, 0:N], rstd[:, 0:N], ALU.mult),
                                 reads=[b_yst, b_rstd], writes=[bt])
                            P.op("vector", lambda e, fc=fc, ti=ti: e.scalar_tensor_tensor(xTb[:, fc, 0:N], ti[:, 0:N], tab[:, l, r, ig, fc:fc + 1],
                                                                                           xTb[:, fc, 0:N], ALU.mult, ALU.add),
                                 reads=[bt, b_tab, b_xTb], writes=[b_xTb])

                    lin_post(wsrc(l, "w_out"), KC, lambda kc, N=N: mixb[:, kc, 0:N], [b_mixb], 1, 2)
                    sandwich_in(xTb, b_xTb, N, l, r, 3, hT, b_hT, rstd, b_rstd, tmpr, sqr)

                    def evac_h(hc, pap, pbuf, N=N):
                        ti, bt = tmpr.next()
                        P.op("scalar", lambda e: e.activation(out=ti[:, 0:N], in_=pap, func=AF.Relu), reads=[pbuf], writes=[bt])
                        P.op("vector", lambda e: e.tensor_tensor(hid[:, hc, 0:N], ti[:, 0:N], ti[:, 0:N], ALU.mult), reads=[bt], **wr(b_hid, hc == 0))
                    linear_fm(wsrc(l, "w1"), 0, 16, lambda kc, N=N: hT[:, kc, 0:N], [b_hT], KC, N, evac_h)
                    lin_post(wsrc(l, "w2"), 64, lambda kc, N=N: hid[:, kc, 0:N], [b_hid], 4, 5)
                    if dst is not None:
                        P.dma("sync", lambda e, c0=c0, N=N: e.dma_start(out=dst.ap()[:, :, c0:c0 + N].rearrange("f p n -> p f n"), in_=xTb[:, :, 0:N]),
                              reads=[b_xTb], pwrites=[b_dst])
                    if final:
                        for t in range(N // 128):
                            for g4 in range(4):
                                bk = prot.next()
                                for j in range(4):
                                    fc = g4 * 4 + j
                                    P.op("tensor", lambda e, fc=fc, t=t, j=j, bk=bk: e.transpose(ps[:, bk, j * 128:(j + 1) * 128], xTb[:, fc, t * 128:(t + 1) * 128], identf[:]),
                                         reads=[b_xTb, b_identf], **wr(b_ps[bk], j == 0))
                                P.op("vector", lambda e, g4=g4, bk=bk: e.tensor_copy(orow[:, g4 * 512:(g4 + 1) * 512], ps[:, bk, :]),
                                     reads=[b_ps[bk]], **wr(b_orow, g4 == 0))
                            r0 = frow + t * 128
                            out_toks.append(P.dma("sync", lambda e, r0=r0: e.dma_start(out=out_ext[r0:r0 + 128, :], in_=orow[:]), reads=[b_orow]))

        def l1_inproj():
            with contextlib.ExitStack() as st:
                alloc_wp(st, "b")
                wpt = WP["t"]
                xTb = sbt(st, "b_xTb", [128, KC, 512], F32); b_xTb = Buf()
                hT = sbt(st, "b_hT", [128, KC, 512], BF16); b_hT = Buf()
                rstd = sbt(st, "b_rstd", [128, 512], F32); b_rstd = Buf()
                tmpr = Rot([(sbt(st, "b_tmp%d" % i, [128, 512], F32), Buf()) for i in range(3)])
                sqr = Rot([(sbt(st, "b_sq%d" % i, [128, 512], BF16), Buf()) for i in range(3)])
                csb = Rot([(sbt(st, "b_cs%d" % i, [128, 512], F32), Buf()) for i in range(2)])
                stg = Rot([(sbt(st, "b_stg%d" % i, [128, 512], F32), Buf()) for i in range(3)])
                qr = sbt(st, "b_qr", [128, 8, 64], F32); b_qr = Buf()
                sq = sbt(st, "b_sqq", [128, 8, 64], F32); b_sq = Buf()
                red = sbt(st, "b_red", [128, 8], F32); b_red = Buf()
                qa = sbt(st, "b_qa", [128, 8, 66], BF16); b_qa = Buf()
                qTs = sbt(st, "b_qTs", [66, 8, 128], BF16); b_qTs = Buf()
                kr = sbt(st, "b_kr", [128, 4, 64], F32); b_kr = Buf()
                ksq = sbt(st, "b_ksq", [128, 4, 64], F32); b_ksq = Buf()
                kred = sbt(st, "b_kred", [128, 8], F32); b_kred = Buf()
                kb_ = sbt(st, "b_kb", [128, 256], BF16); b_kb = Buf()
                kTs = sbt(st, "b_kTs", [128, 2, 128], BF16); b_kTs = Buf()
                vb = Rot([(sbt(st, "b_vb%d" % i, [128, 256], BF16), Buf()) for i in range(2)])
                rt = [(sbt(st, "b_rt%d" % i, [128, 256], F32), Buf()) for i in range(2)]
                P.op("vector", lambda e: e.memset(qa[:, :, 65:66], 1.0), writes=[b_qa])
                P.op("vector", lambda e: e.memset(kred[:], 0.0), writes=[b_kred])
                kfirst = [True]
                blocks = [("e", c0, N) for (c0, N) in EBLOCKS] + [("c", NE, NCTX)]
                def do_block(kind, col0, N):
                    ntile = N // 128
                    r = 1 if kind == "c" else 0
                    P.dma("sync", lambda e, col0=col0, N=N: e.dma_start(out=xTb[:, :, 0:N], in_=xT1.ap()[:, :, col0:col0 + N].rearrange("f p n -> p f n")),
                          reads=[B["xT1"]], writes=[b_xTb])
                    sandwich_in(xTb, b_xTb, N, 1, r, 0, hT, b_hT, rstd, b_rstd, tmpr, sqr)
                    if kind == "e":
                        def do_half(half):
                            sb_, bwb = load_wpiece(wsrc(1, "w_in"), 0, half * 512)
                            sc_, bwc = load_wpiece(wsrc(1, "w_in"), 0, 1024 + half * 512)
                            sx_, bwx = load_wpiece(wsrc(1, "w_in"), 0, 2048 + half * 512)
                            for oc in range(4):
                                i = half * 4 + oc
                                bks = []
                                for (sl, bw) in ((sb_, bwb), (sc_, bwc), (sx_, bwx)):
                                    bk = prot.next()
                                    bks.append(bk)
                                    for kc in range(KC):
                                        P.op("tensor", lambda e, kc=kc, oc=oc, bk=bk, sl=sl: e.matmul(ps[:, bk, 0:N], wpt[:, sl, kc, oc * 128:(oc + 1) * 128], hT[:, kc, 0:N],
                                                                                                   start=(kc == 0), stop=(kc == KC - 1)),
                                             reads=[bw, b_hT], **wr(b_ps[bk], kc == 0))
                                ci, bc = csb.next()
                                P.op("scalar", lambda e, ci=ci, bk=bks[1]: e.activation(out=ci[:, 0:N], in_=ps[:, bk, 0:N], func=AF.Identity), reads=[b_ps[bks[1]]], writes=[bc])
                                s1, bs1 = stg.next()
                                P.op("vector", lambda e, ci=ci, s1=s1, bk=bks[2]: e.tensor_tensor(s1[:, 0:N], ps[:, bk, 0:N], ci[:, 0:N], ALU.mult), reads=[b_ps[bks[2]], bc], writes=[bs1])
                                P.dma("sync", lambda e, i=i, s1=s1: e.dma_start(out=cxT.ap()[i, :, col0:col0 + N], in_=s1[:, 0:N]), reads=[bs1], pwrites=[B["cxT"]])
                                s2, bs2 = stg.next()
                                P.op("scalar", lambda e, s2=s2, bk=bks[0]: e.activation(out=s2[:, 0:N], in_=ps[:, bk, 0:N], func=AF.Identity), reads=[b_ps[bks[0]]], writes=[bs2])
                                P.dma("sync", lambda e, i=i, s2=s2: e.dma_start(out=bT.ap()[i, :, col0:col0 + N], in_=s2[:, 0:N]), reads=[bs2], pwrites=[B["bT"]])
                        for half in range(2):
                            do_half(half)
                        for p in range(2):
                            def evac_q(t, bk, p=p):
                                t_e = col0 // 128 + t
                                rope(bk, t_e, qr, b_qr, rt, True)
                                cc = col0 + t * 128
                                q_finish(qr, b_qr, sq, b_sq, red, b_red, qa, b_qa, 66, qTs, b_qTs,
                                         lambda: qT1.ap()[p * 8:(p + 1) * 8, :, cc:cc + 128].rearrange("h r n -> r h n"), B["qT1"])
                            linear_tm(wsrc(1, "w_in"), 3072 + p * 512, hT, b_hT, ntile, evac_q)
                    def evac_kv(t, bk):
                        cc = col0 + t * 128
                        if kind == "e":
                            t_e = col0 // 128 + t
                            pv = ps[:, bk, 0:256].rearrange("p (h two d) -> p h two d", two=2, d=32)
                            cosb = cs[:, t_e, 0:32].unsqueeze(1).broadcast_to([128, 4, 32])
                            sinb = cs[:, t_e, 32:64].unsqueeze(1).broadcast_to([128, 4, 32])
                            (t1, bt1), (t2, bt2) = rt
                            t1v = t1[:, 0:128].rearrange("p (h d) -> p h d", d=32)
                            t2v = t2[:, 0:128].rearrange("p (h d) -> p h d", d=32)
                            x1 = pv[:, :, 0, :]
                            x2 = pv[:, :, 1, :]
                            P.op("vector", lambda e: e.tensor_tensor(t1v, x1, cosb, ALU.mult), reads=[b_ps[bk], b_cs], writes=[bt1])
                            P.op("vector", lambda e: e.tensor_tensor(t2v, x2, sinb, ALU.mult), reads=[b_ps[bk], b_cs], writes=[bt2])
                            P.op("vector", lambda e: e.tensor_tensor(kr[:, :, 0:32], t1v, t2v, ALU.subtract), reads=[bt1, bt2], writes=[b_kr])
                            P.op("vector", lambda e: e.tensor_tensor(t1v, x2, cosb, ALU.mult), reads=[b_ps[bk], b_cs], writes=[bt1])
                            P.op("vector", lambda e: e.tensor_tensor(t2v, x1, sinb, ALU.mult), reads=[b_ps[bk], b_cs], writes=[bt2])
                            P.op("vector", lambda e: e.tensor_tensor(kr[:, :, 32:64], t1v, t2v, ALU.add), reads=[bt1, bt2], pwrites=[b_kr])
                        else:
                            P.op("vector", lambda e: e.tensor_copy(kr[:].rearrange("p h d -> p (h d)"), ps[:, bk, 0:256]), reads=[b_ps[bk]], writes=[b_kr])
                        P.op("vector", lambda e: e.tensor_tensor(ksq[:], kr[:], kr[:], ALU.mult), reads=[b_kr], writes=[b_ksq])
                        P.op("vector", lambda e: e.tensor_reduce(kred[:, 0:4], ksq[:], AX.X, ALU.add), reads=[b_ksq], writes=[b_kred])
                        kmax_update(kred, b_kred, kfirst)
                        P.op("scalar", lambda e: e.activation(out=kb_[:], in_=kr[:].rearrange("p h d -> p (h d)"), func=AF.Identity), reads=[b_kr], writes=[b_kb])
                        bk2 = prot.next()
                        for hh in range(2):
                            P.op("tensor", lambda e, hh=hh: e.transpose(psb(bk2)[:, hh * 128:(hh + 1) * 128], kb_[:, hh * 128:(hh + 1) * 128], identb[:]),
                                 reads=[b_kb, b_identb], **wr(b_ps[bk2], hh == 0))
                        P.op("vector", lambda e: e.tensor_copy(kTs[:].rearrange("q h n -> q (h n)"), psb(bk2)[:, 0:256]), reads=[b_ps[bk2]], writes=[b_kTs])
                        P.dma("sync", lambda e: e.dma_start(out=kt1.ap().rearrange("(h q) n -> q h n", q=128)[:, :, cc:cc + 128], in_=kTs[:]),
                              reads=[b_kTs], pwrites=[B["kt1"]])
                        vi, bv = vb.next()
                        P.op("scalar", lambda e: e.activation(out=vi[:], in_=ps[:, bk, 256:512], func=AF.Identity), reads=[b_ps[bk]], writes=[bv])
                        P.dma("sync", lambda e: e.dma_start(out=v1.ap()[cc:cc + 128, :], in_=vi[:]), reads=[bv], pwrites=[B["v1"]])
                    linear_tm(wsrc(1, "w_in"), 4096, hT, b_hT, ntile, evac_kv)
                for blk in blocks:
                    do_block(*blk)
                kb_finish(1)

        def l1_conv():
            with contextlib.ExitStack() as st:
                sw = sbt(st, "d_sw", [128, 8, 3], F32); b_sw = Buf()
                cxS = [sbt(st, "d_cx%d" % i, [128, NE], F32) for i in range(2)]; b_cx = [Buf(), Buf()]
                bS = [sbt(st, "d_b%d" % i, [128, NE], F32) for i in range(2)]; b_b = [Buf(), Buf()]
                acc = [sbt(st, "d_acc%d" % i, [128, TO], F32) for i in range(2)]; b_acc = [Buf(), Buf()]
                cl = [sbt(st, "d_cl%d" % i, [128, TO], BF16) for i in range(2)]; b_cl = [Buf(), Buf()]
                P.dma("sync", lambda e: e.dma_start(out=sw[:], in_=sconv_wT), writes=[b_sw])
                for i in range(8):
                    s = i % 2
                    eng = "vector"
                    P.dma("sync", lambda e, i=i, s=s: e.dma_start(out=cxS[s][:], in_=cxT.ap()[i]), reads=[B["cxT"]], writes=[b_cx[s]])
                    P.dma("sync", lambda e, i=i, s=s: e.dma_start(out=bS[s][:], in_=bT.ap()[i]), reads=[B["bT"]], writes=[b_b[s]])
                    P.op(eng, lambda e, s=s: e.tensor_scalar(cxS[s][:, 0:128], cxS[s][:, 0:128], flg[:, 0:1], None, ALU.mult), reads=[b_cx[s], b_flg], writes=[b_cx[s]])
                    P.op(eng, lambda e, s=s: e.tensor_scalar(cxS[s][:, NE - 128:NE], cxS[s][:, NE - 128:NE], flg[:, 1:2], None, ALU.mult), reads=[b_cx[s], b_flg], writes=[b_cx[s]])
                    P.op(eng, lambda e, s=s, i=i: e.tensor_scalar(acc[s][:], cxS[s][:, 127:127 + TO], sw[:, i, 0:1], None, ALU.mult), reads=[b_cx[s], b_sw], writes=[b_acc[s]])
                    for j in (1, 2):
                        P.op(eng, lambda e, s=s, i=i, j=j: e.scalar_tensor_tensor(acc[s][:], cxS[s][:, 127 + j:127 + j + TO], sw[:, i, j:j + 1], acc[s][:], ALU.mult, ALU.add),
                             reads=[b_cx[s], b_sw, b_acc[s]], writes=[b_acc[s]])
                    P.op(eng, lambda e, s=s: e.tensor_tensor(cl[s][:], acc[s][:], bS[s][:, 128:128 + TO], ALU.mult), reads=[b_acc[s], b_b[s]], writes=[b_cl[s]])
                    P.dma("sync", lambda e, i=i, s=s: e.dma_start(out=mixT[1].ap()[i, :, 128:128 + TO], in_=cl[s][:]), reads=[b_cl[s]], pwrites=[B["mixT1"]])

        def l1_attn():
            with contextlib.ExitStack() as st:
                KT = sbt(st, "f_KT", [66, 4, TT], BF16); b_KT = Buf()
                V = sbt(st, "f_V", [128, TT // 128, 256], BF16); b_V = Buf()
                qS = [sbt(st, "f_q%d" % i, [66, NE], BF16) for i in range(2)]; b_qS = [Buf(), Buf()]
                kfk = sbt(st, "f_kfk", [66, 16], BF16); b_kfk = Buf()
                msk = sbt(st, "f_msk", [128, 4, 128], BF16); b_msk = Buf()
                mskf = sbt(st, "f_mskf", [128, 2, 128], F32); b_mskf = Buf()
                pA = Rot([(sbt(st, "f_pA%d" % i, [128, 640], BF16), Buf()) for i in range(3)])
                pk = Rot([(sbt(st, "f_pk%d" % i, [1, 128], BF16), Buf()) for i in range(2)])
                rs = Rot([(sbt(st, "f_rs%d" % i, [64, 128], F32), Buf()) for i in range(2)])
                ost = [sbt(st, "f_ost%d" % i, [64, TO], BF16) for i in range(2)]; b_ost = [Buf(), Buf()]
                P.dma("sync", lambda e: e.dma_start(out=mskf[:], in_=masks_in), writes=[b_mskf])
                P.op("vector", lambda e: e.tensor_copy(msk[:, 0:2, :], mskf[:]), reads=[b_mskf], writes=[b_msk])
                P.op("vector", lambda e: e.tensor_scalar(msk[:, 2, :], mskf[:, 0, :], flg[:, 0:1], None, ALU.mult), reads=[b_mskf, b_flg], pwrites=[b_msk])
                P.op("vector", lambda e: e.tensor_scalar(msk[:, 3, :], mskf[:, 1, :], flg[:, 1:2], None, ALU.mult), reads=[b_mskf, b_flg], pwrites=[b_msk])
                P.op("vector", lambda e: e.memset(KT[64:66, :, :], 0.0), writes=[b_KT])
                P.op("vector", lambda e: e.memset(KT[64:65, :, :], 1.0), reads=[], writes=[b_KT])
                P.op("vector", lambda e: e.tensor_scalar(KT[64:65, :, :], KT[64:65, :, :], negkb[1][64:65, 0:1], None, ALU.mult), reads=[b_negkb[1], b_KT], writes=[b_KT])
                for kvh in range(4):
                    P.dma("sync", lambda e, kvh=kvh: e.dma_start(out=KT[0:64, kvh, :], in_=kt1.ap()[kvh * 64:(kvh + 1) * 64, :]), reads=[B["kt1"]], pwrites=[b_KT])
                P.dma("sync", lambda e: e.dma_start(out=V[:], in_=v1.ap().rearrange("(c p) e -> p c e", p=128)), reads=[B["v1"]], writes=[b_V])
                P.op("vector", lambda e: e.memset(kfk[:], 0.0), writes=[b_kfk])
                P.op("vector", lambda e: e.memset(kfk[64:65, :], 1.0), writes=[b_kfk])
                P.op("vector", lambda e: e.tensor_scalar(kfk[64:65, :], kfk[64:65, :], negkb[1][64:65, 0:1], None, ALU.mult), reads=[b_negkb[1], b_kfk], writes=[b_kfk])
                P.dma("gpsimd", lambda e: e.dma_start(out=kfk[65:66, :], in_=sink_in), reads=[], writes=[], pwrites=[b_kfk])
                NCH_CTX = NE // 128
                for head in range(16):
                    kvh = head // 4
                    qi = head % 2
                    P.dma("sync", lambda e, head=head, qi=qi: e.dma_start(out=qS[qi][:], in_=qT1.ap()[head]), reads=[B["qT1"]], writes=[b_qS[qi]])
                    def do_tile(head, kvh, qi, n):
                        qcol = qS[qi][:, n * 128:(n + 1) * 128]
                        bA = prot.next()
                        bB = prot.next()
                        chunks = [NCH_CTX, NCH_CTX + 1, n - 1, n, n + 1]
                        for ci, ch in enumerate(chunks):
                            dstp = ps[:, bA, ci * 128:(ci + 1) * 128] if ci < 4 else ps[:, bB, 0:128]
                            bkk = bA if ci < 4 else bB
                            P.op("tensor", lambda e, ch=ch, dstp=dstp, kvh=kvh: e.matmul(dstp, KT[:, kvh, ch * 128:(ch + 1) * 128], qcol, start=True, stop=True),
                                 reads=[b_KT, b_qS[qi]], **wr(b_ps[bkk], ci == 0 or ci == 4))
                        P.op("tensor", lambda e, head=head: e.matmul(ps[0:1, bB, 128:256], kfk[:, head:head + 1], qcol, start=True, stop=True),
                             reads=[b_kfk, b_qS[qi]], pwrites=[b_ps[bB]])
                        pa, bpa = pA.next()
                        P.op("scalar", lambda e, pa=pa: e.activation(out=pa[:, 0:512], in_=ps[:, bA, :], func=AF.Exp), reads=[b_ps[bA]], writes=[bpa])
                        P.op("scalar", lambda e, pa=pa: e.activation(out=pa[:, 512:640], in_=ps[:, bB, 0:128], func=AF.Exp), reads=[b_ps[bB]], pwrites=[bpa])
                        pki, bpk = pk.next()
                        P.op("scalar", lambda e, pki=pki: e.activation(out=pki[:], in_=ps[0:1, bB, 128:256], func=AF.Exp), reads=[b_ps[bB]], writes=[bpk])
                        mp = 2 if n == 1 else 0
                        mn = 3 if n == NT else 1
                        P.op("vector", lambda e, pa=pa, mp=mp: e.tensor_tensor(pa[:, 256:384], pa[:, 256:384], msk[:, mp, :], ALU.mult), reads=[bpa, b_msk], writes=[bpa])
                        P.op("vector", lambda e, pa=pa, mn=mn: e.tensor_tensor(pa[:, 512:640], pa[:, 512:640], msk[:, mn, :], ALU.mult), reads=[bpa, b_msk], writes=[bpa])
                        bo = prot.next()
                        bs = prot.next()
                        for ci, ch in enumerate(chunks):
                            P.op("tensor", lambda e, ci=ci, ch=ch, pa=pa, kvh=kvh: e.matmul(ps[0:64, bo, 0:128], V[:, ch, kvh * 64:(kvh + 1) * 64], pa[:, ci * 128:(ci + 1) * 128],
                                                                                              start=(ci == 0), stop=(ci == 4)),
                                 reads=[b_V, bpa], **wr(b_ps[bo], ci == 0))
                            P.op("tensor", lambda e, ci=ci, pa=pa: e.matmul(ps[0:64, bs, 0:128], onesb[:, 0:64], pa[:, ci * 128:(ci + 1) * 128], start=(ci == 0), stop=False),
                                 reads=[b_onesb, bpa], **wr(b_ps[bs], ci == 0))
                        P.op("tensor", lambda e, pki=pki: e.matmul(ps[0:64, bs, 0:128], onesb[0:1, 0:64], pki[:], start=False, stop=True),
                             reads=[b_onesb, bpk], pwrites=[b_ps[bs]])
                        ri, br = rs.next()
                        P.op("vector", lambda e, ri=ri: e.reciprocal(ri[:], ps[0:64, bs, 0:128]), reads=[b_ps[bs]], writes=[br])
                        P.op("vector", lambda e, ri=ri, n=n: e.tensor_tensor(ost[qi][:, (n - 1) * 128:n * 128], ps[0:64, bo, 0:128], ri[:], ALU.mult),
                             reads=[b_ps[bo], br], **wr(b_ost[qi], n == 1))
                    for n in range(1, NT + 1):
                        do_tile(head, kvh, qi, n)
                    P.dma("sync", lambda e, head=head, qi=qi: e.dma_start(out=mixT[1].ap()[8 + head // 2, (head % 2) * 64:(head % 2) * 64 + 64, 128:128 + TO], in_=ost[qi][:]),
                          reads=[b_ost[qi]], pwrites=[B["mixT1"]])

        if 'adaln' in phases:
            adaln_phase()
            P.barrier()
        if 'l0in' in phases:
            l0_inproj()
            P.barrier()
        if 'l0conv' in phases:
            l0_conv()
            P.barrier()
        if 'l0attn' in phases:
            l0_attn()
            P.barrier()
        if 'tail0' in phases:
            tail_phase(0, [(0, c0, N, None) for (c0, N) in EBLOCKS] + [(1, NE, NCTX, None)], xT0, B["xT0"], xT1, B["xT1"], False)
            P.barrier()
        if 'l1in' in phases:
            l1_inproj()
            P.barrier()
        if 'l1conv' in phases:
            l1_conv()
            P.barrier()
        if 'l1attn' in phases:
            l1_attn()
            P.barrier()
        if 'tail1' in phases:
            tail_phase(1, [(0, 128 + 512 * b, 512, 512 * b) for b in range(4)], xT1, B["xT1"], None, None, True)
        scr = dict(xT0=xT0, xT1=xT1, uT_e=uT_e, uT_c=uT_c, mixT0=mixT[0], mixT1=mixT[1], qT0=qT0, kt_all=kt_all,
                   v_all=v_all, kt_ctx=kt_ctx, cxT=cxT, bT=bT, qT1=qT1, kt1=kt1, v1=v1)
        for nm in dbg:
            if nm == 'tab':
                o = nc.dram_tensor("dbg_tab", [128, 2 * 2 * 6 * KC], F32, kind="ExternalOutput").ap()
                out_toks.append(P.dma("sync", lambda e, o=o: e.dma_start(out=o, in_=tab[:].rearrange("p a b c d -> p (a b c d)")), reads=[b_tab]))
            elif nm == 'negkb':
                for l in range(2):
                    o = nc.dram_tensor("dbg_negkb%d" % l, [128, 1], F32, kind="ExternalOutput").ap()
                    out_toks.append(P.dma("sync", lambda e, o=o, l=l: e.dma_start(out=o, in_=negkb[l][:]), reads=[b_negkb[l]]))
            else:
                t = scr[nm]
                o = nc.dram_tensor("dbg_" + nm, list(t.shape), t.dtype, kind="ExternalOutput").ap()
                out_toks.append(P.dma("sync", lambda e, o=o, t=t: e.dma_start(out=o, in_=t.ap()), reads=[B[nm]]))
        P.finish_wait("sync", out_toks)
        P.emit(top)
    return nc, declared


def _bind_tab(tab, modT, gn, l, r, i):
    def T():
        return tab[:, l, r, i, :]

    def M(v):
        return modT[:, v * KC:(v + 1) * KC, r]
    if i == 0:
        return lambda e: e.scalar_tensor_tensor(T(), M(1), 1.0, gn[:, 0, :], ALU.add, ALU.mult)
    if i == 1:
        return lambda e: e.tensor_copy(T(), M(0))
    if i == 2:
        return lambda e: e.tensor_tensor(T(), M(2), gn[:, 1, :], ALU.mult)
    if i == 3:
        return lambda e: e.scalar_tensor_tensor(T(), M(4), 1.0, gn[:, 2, :], ALU.add, ALU.mult)
    if i == 4:
        return lambda e: e.tensor_copy(T(), M(3))
    return lambda e: e.tensor_tensor(T(), M(5), gn[:, 3, :], ALU.mult)


_NC_CACHE = {}
_ALL_INPUTS = ("x", "c", "ctx", "c_ctx",
               "l0_mod_w", "l0_mod_b", "l0_norm_mix_pre", "l0_norm_mix_post", "l0_norm_mlp_pre", "l0_norm_mlp_post",
               "l0_w_in", "l0_conv_w", "l0_conv_b", "l0_ln_g", "l0_ln_b", "l0_lambda_q1", "l0_lambda_k1", "l0_lambda_q2",
               "l0_lambda_k2", "l0_subln_g", "l0_w_out", "l0_mlp_w1", "l0_mlp_w2",
               "l1_mod_w", "l1_mod_b", "l1_norm_mix_pre", "l1_norm_mix_post", "l1_norm_mlp_pre", "l1_norm_mlp_post",
               "l1_w_in", "l1_sconv_w", "l1_sink", "l1_w_out", "l1_mlp_w1", "l1_mlp_w2")


def _fm(v, nchunk):
    return np.ascontiguousarray(np.asarray(v, np.float32).reshape(nchunk, 128).T)


def prep_inputs(inp):
    f = lambda k: np.asarray(inp[k], np.float32)
    x = f("x")[0]
    ctx = f("ctx")[0]
    inv = np.power(10000.0, -np.arange(16, dtype=np.float32) / 16).astype(np.float32)
    ident = np.eye(128, dtype=np.float32)
    jj = np.arange(128)[:, None]
    ii = np.arange(128)[None, :]
    masks = np.stack([(jj >= ii), (jj <= ii)], axis=1).astype(np.float32)
    shared = dict(ctx=np.ascontiguousarray(ctx), ident=ident, masks=np.ascontiguousarray(masks), x_all=np.ascontiguousarray(x))
    pa = np.arange(SEQ)
    anga = np.concatenate([(pa // 64).astype(np.float32)[:, None] * inv, (pa % 64).astype(np.float32)[:, None] * inv], axis=-1).astype(np.float32)
    shared["cs_all"] = np.concatenate([np.cos(anga), np.sin(anga)], axis=-1).astype(np.float32)
    cv = np.stack([f("c")[0], f("c_ctx")], axis=-1)
    shared["cvT"] = np.ascontiguousarray(cv.reshape(KC, 128, 2).transpose(1, 0, 2))
    for l in range(2):
        pre = "l%d_" % l
        shared[pre + "mod_w"] = f(pre + "mod_w")
        shared[pre + "mod_bT"] = _fm(f(pre + "mod_b"), 96)
        shared[pre + "gains"] = np.ascontiguousarray(np.stack(
            [_fm(f(pre + k), KC) for k in ("norm_mix_pre", "norm_mix_post", "norm_mlp_pre", "norm_mlp_post")], axis=1))
        for k in ("w_in", "w_out", "mlp_w1", "mlp_w2"):
            shared[pre + k] = f(pre + k)
    shared["l0_conv_wT"] = np.ascontiguousarray(f("l0_conv_w").T.reshape(8, 128, 31).transpose(1, 0, 2))
    shared["l0_conv_misc"] = np.ascontiguousarray(np.stack([_fm(f("l0_" + k), 8) for k in ("conv_b", "ln_g", "ln_b")], axis=1))
    shared["l0_lam"] = np.stack([f("l0_lambda_q1"), f("l0_lambda_k1"), f("l0_lambda_q2"), f("l0_lambda_k2")], axis=0)
    shared["l0_subln"] = f("l0_subln_g").reshape(128, 1)
    shared["l1_sconv_wT"] = np.ascontiguousarray(f("l1_sconv_w").T.reshape(8, 128, 3).transpose(1, 0, 2))
    shared["l1_sink"] = f("l1_sink").reshape(1, 16)
    in_maps = []
    for r in range(NCORES):
        s = r * TO
        xe = np.zeros((NE + 256, D), np.float32)
        lo = s - 256
        hi = s + TO + 256
        a = max(lo, 0)
        b = min(hi, SEQ)
        xe[a - lo:b - lo] = x[a:b]
        pos = np.arange(s - 128, s + TO + 128)
        pos = np.clip(pos, 0, SEQ - 1)
        row = (pos // 64).astype(np.float32)
        col = (pos % 64).astype(np.float32)
        ang = np.concatenate([row[:, None] * inv, col[:, None] * inv], axis=-1).astype(np.float32)
        cs = np.concatenate([np.cos(ang), np.sin(ang)], axis=-1).astype(np.float32)
        fl = np.zeros((128, 2), np.float32)
        fl[:, 0] = 1.0 if r > 0 else 0.0
        fl[:, 1] = 1.0 if r < NCORES - 1 else 0.0
        m = dict(shared)
        m["x_ext"] = xe
        m["cs_ext"] = cs
        m["flags"] = fl
        in_maps.append(m)
    return in_maps


def kernel(**inp):
    if "nc" not in _NC_CACHE:
        _NC_CACHE["nc"] = build()
    nc, declared = _NC_CACHE["nc"]
    in_maps = prep_inputs(inp)
    in_maps = [{k: v for k, v in m.items() if k in declared} for m in in_maps]
    res = run_bass_kernel_spmd(nc, in_maps, core_ids=list(range(NCORES)))
    out = np.concatenate([np.asarray(res.results[r]["out"], np.float32) for r in range(NCORES)], axis=0)
    return out[None]
```
